# Optimizing a Trainium2 kernel written in Bass

```python
import math
import jax, jax.numpy as jnp
from jax import lax
import numpy as np

D_MODEL = 1024
BATCH = 8
SEQ = 2048
DEPTH = 1

M_HEADS = 4
M_HEAD_DIM = D_MODEL // M_HEADS
D_M = M_HEADS * M_HEAD_DIM
QK_CONV = 4
CHUNK = 64
D_C = D_MODEL
CONF_CONV = 31
D_FF = ((8 * D_MODEL // 3 + 255) // 256) * 256
PLE_DIM = 256
EPS = 1e-6
SPLITS = (D_M, 2 * D_M, 3 * D_M, 4 * D_M, 4 * D_M + M_HEADS, 4 * D_M + 2 * M_HEADS, 4 * D_M + 2 * M_HEADS + 2 * D_C)
N_IN = 4 * D_M + 2 * M_HEADS + 2 * D_C + 2 * D_MODEL

kernel_name = 'hybrid_mlstm_conformer_conv_gated_block'


def rmsnorm(x, g):
    xf = x.astype(jnp.float32)
    y = xf * lax.rsqrt(jnp.mean(xf * xf, axis=-1, keepdims=True) + EPS) * g.astype(jnp.float32)
    return y.astype(x.dtype)


def layernorm(x, g, b):
    xf = x.astype(jnp.float32)
    mu = jnp.mean(xf, axis=-1, keepdims=True)
    var = jnp.mean(jnp.square(xf - mu), axis=-1, keepdims=True)
    y = (xf - mu) * lax.rsqrt(var + EPS) * g.astype(jnp.float32) + b.astype(jnp.float32)
    return y.astype(x.dtype)


def causal_dwconv(x, w, b):
    K, C = w.shape
    y = lax.conv_general_dilated(x, w[:, None, :].astype(x.dtype), window_strides=(1,), padding=[(K - 1, 0)], dimension_numbers=('NWC', 'WIO', 'NWC'), feature_group_count=C)
    return y + b.astype(x.dtype)


def mlstm_chunkwise(q, k, v, ig, lf):
    B, S, H, dk = q.shape
    dv = v.shape[-1]
    nc = S // CHUNK
    f32 = jnp.float32

    def to_chunks(t):
        return t.astype(f32).reshape(B, nc, CHUNK, H, -1).transpose(1, 0, 3, 2, 4)

    def gate_chunks(t):
        return t.astype(f32).reshape(B, nc, CHUNK, H).transpose(1, 0, 3, 2)

    causal = jnp.tril(jnp.ones((CHUNK, CHUNK), dtype=bool))

    def step(carry, inp):
        C, n, m = carry
        qc, kc, vc, igc, lfc = inp
        b = jnp.cumsum(lfc, axis=-1)
        D = b[..., :, None] - b[..., None, :] + igc[..., None, :]
        D = jnp.where(causal, D, -jnp.inf)
        inter = b + m[..., None]
        m_t = jnp.maximum(jnp.max(D, axis=-1), inter)
        s = jnp.einsum('bhtd,bhsd->bhts', qc, kc) * jnp.exp(D - m_t[..., None])
        w_inter = jnp.exp(inter - m_t)
        num = jnp.einsum('bhts,bhse->bhte', s, vc) + w_inter[..., None] * jnp.einsum('bhtd,bhde->bhte', qc, C)
        den = jnp.sum(s, axis=-1) + w_inter * jnp.einsum('bhtd,bhd->bht', qc, n)
        h = num / jnp.maximum(jnp.abs(den), jnp.exp(-m_t))[..., None]
        bL = b[..., -1]
        g = bL[..., None] - b + igc
        m_new = jnp.maximum(bL + m, jnp.max(g, axis=-1))
        wk = jnp.exp(g - m_new[..., None])
        decay = jnp.exp(bL + m - m_new)
        C_new = decay[..., None, None] * C + jnp.einsum('bhs,bhsd,bhse->bhde', wk, kc, vc)
        n_new = decay[..., None] * n + jnp.einsum('bhs,bhsd->bhd', wk, kc)
        return (C_new, n_new, m_new), h

    init = (jnp.zeros((B, H, dk, dv), f32), jnp.zeros((B, H, dk), f32), jnp.zeros((B, H), f32))
    _, h = lax.scan(step, init, (to_chunks(q), to_chunks(k), to_chunks(v), gate_chunks(ig), gate_chunks(lf)))
    return h.transpose(1, 0, 3, 2, 4).reshape(B, S, H * dv)


def setup_inputs(seed: int = 0) -> dict:
    key = jax.random.key(seed)
    ks = jax.random.split(key, 32)
    f32 = jnp.float32

    def nrm(k, shape, scale):
        return jax.random.normal(k, shape, f32) * scale

    def gain(k, shape):
        return 1.0 + nrm(k, shape, 0.02)

    b_if = jnp.concatenate([nrm(ks[3], (DEPTH, M_HEADS), 0.1), jnp.linspace(3.0, 6.0, M_HEADS, dtype=f32)[None, :] + nrm(ks[4], (DEPTH, M_HEADS), 0.1)], axis=-1)
    return {
        'x': nrm(ks[0], (BATCH, SEQ, D_MODEL), 1.0),
        'p': nrm(ks[1], (DEPTH, BATCH, SEQ, PLE_DIM), 1.0),
        'norm_mix_g': gain(ks[2], (DEPTH, D_MODEL)),
        'w_in': nrm(ks[5], (DEPTH, D_MODEL, N_IN), D_MODEL ** -0.5),
        'b_if': b_if,
        'conv_qk_w': nrm(ks[6], (DEPTH, QK_CONV, 2 * D_M), QK_CONV ** -0.5),
        'conv_qk_b': nrm(ks[7], (DEPTH, 2 * D_M), 0.02),
        'mh_norm_g': gain(ks[8], (DEPTH, D_M)),
        'conf_conv_w': nrm(ks[9], (DEPTH, CONF_CONV, D_C), CONF_CONV ** -0.5),
        'conf_conv_b': nrm(ks[10], (DEPTH, D_C), 0.02),
        'conf_ln_g': gain(ks[11], (DEPTH, D_C)),
        'conf_ln_b': nrm(ks[12], (DEPTH, D_C), 0.02),
        'w_branch_m': nrm(ks[13], (DEPTH, D_M, D_MODEL), D_M ** -0.5),
        'w_branch_c': nrm(ks[14], (DEPTH, D_C, D_MODEL), D_C ** -0.5),
        'w_out': nrm(ks[15], (DEPTH, D_MODEL, D_MODEL), D_MODEL ** -0.5),
        'norm_ffn_g': gain(ks[16], (DEPTH, D_MODEL)),
        'w_ffn_gate': nrm(ks[17], (DEPTH, D_MODEL, D_FF), D_MODEL ** -0.5),
        'w_ffn_up': nrm(ks[18], (DEPTH, D_MODEL, D_FF), D_MODEL ** -0.5),
        'w_ffn_down': nrm(ks[19], (DEPTH, D_FF, D_MODEL), D_FF ** -0.5),
        'norm_ple_g': gain(ks[20], (DEPTH, D_MODEL)),
        'w_ple_gate': nrm(ks[21], (DEPTH, D_MODEL, D_MODEL), D_MODEL ** -0.5),
        'w_ple_proj': nrm(ks[22], (DEPTH, PLE_DIM, D_MODEL), PLE_DIM ** -0.5),
        'final_g': gain(ks[23], (D_MODEL,)),
    }


def reference(x, p, norm_mix_g, w_in, b_if, conv_qk_w, conv_qk_b, mh_norm_g, conf_conv_w, conf_conv_b, conf_ln_g, conf_ln_b, w_branch_m, w_branch_c, w_out, norm_ffn_g, w_ffn_gate, w_ffn_up, w_ffn_down, norm_ple_g, w_ple_gate, w_ple_proj, final_g):
    B, S, _ = x.shape
    for l in range(DEPTH):
        h = rmsnorm(x, norm_mix_g[l])
        proj = h @ w_in[l]
        q_pre, k_pre, v, o_pre, i_pre, f_pre, conf_in, gate_pre = jnp.split(proj, SPLITS, axis=-1)

        qk = jax.nn.silu(causal_dwconv(jnp.concatenate([q_pre, k_pre], axis=-1), conv_qk_w[l], conv_qk_b[l]))
        q, k = jnp.split(qk, 2, axis=-1)
        q = q.reshape(B, S, M_HEADS, M_HEAD_DIM) * (M_HEAD_DIM ** -0.5)
        k = k.reshape(B, S, M_HEADS, M_HEAD_DIM)
        v = v.reshape(B, S, M_HEADS, M_HEAD_DIM)
        i_b, f_b = jnp.split(b_if[l], 2)
        ig = i_pre.astype(jnp.float32) + i_b.astype(jnp.float32)
        lf = jax.nn.log_sigmoid(f_pre.astype(jnp.float32) + f_b.astype(jnp.float32))
        hm = mlstm_chunkwise(q, k, v, ig, lf).astype(x.dtype)
        hm = rmsnorm(hm.reshape(B, S, M_HEADS, M_HEAD_DIM), mh_norm_g[l].reshape(M_HEADS, M_HEAD_DIM)).reshape(B, S, D_M)
        hm = hm * jax.nn.sigmoid(o_pre)
        branch_m = hm @ w_branch_m[l]

        a, ga = jnp.split(conf_in, 2, axis=-1)
        u = a * jax.nn.sigmoid(ga)
        u = causal_dwconv(u, conf_conv_w[l], conf_conv_b[l])
        u = jax.nn.silu(layernorm(u, conf_ln_g[l], conf_ln_b[l]))
        branch_c = u @ w_branch_c[l]

        g_m, g_c = jnp.split(jax.nn.sigmoid(gate_pre), 2, axis=-1)
        x = x + (g_m * branch_m + g_c * branch_c) @ w_out[l]

        f = rmsnorm(x, norm_ffn_g[l])
        x = x + (jax.nn.silu(f @ w_ffn_gate[l]) * (f @ w_ffn_up[l])) @ w_ffn_down[l]

        gate = jax.nn.sigmoid(rmsnorm(x, norm_ple_g[l]) @ w_ple_gate[l])
        x = x + gate * (p[l].astype(x.dtype) @ w_ple_proj[l])
    return rmsnorm(x, final_g)
```

```python
import contextlib
import numpy as np
import concourse.bass as bass
import concourse.mybir as mybir
from concourse.bass_utils import run_bass_kernel_spmd

F32 = mybir.dt.float32
BF16 = mybir.dt.bfloat16
AF = mybir.ActivationFunctionType
ALU = mybir.AluOpType
AX = mybir.AxisListType

SEQ = 2048
D = 1024
NIN = 8200
DFF = 2816
NFF = DFF // 128
PLE = 256
TT = 512
NJ = TT // 128
NT = SEQ // TT
EPS = 1e-6
HQ = 3
HC = 30
NCORES = 8


class Sched:
    def __init__(self, nc, es):
        self.nc = nc
        self.es = es
        self.eng = {'pe': nc.tensor, 'act': nc.scalar, 'dve': nc.vector, 'pool': nc.gpsimd, 'sp': nc.sync}
        self.sems = {}
        self.cnt = {}
        for e in ['pe', 'act', 'dve', 'pool']:
            self.sems[e] = es.enter_context(nc.semaphore('s_' + e))
            self.cnt[e] = 0
        self.clock = {e: {} for e in self.eng}
        self.snap = {}
        self.lastw = {}
        self.readers = {}
        self.nwaits = 0
        self.nbank = 0

    def bank(self):
        b = self.nbank % 6
        self.nbank += 1
        return b

    def _deps(self, reads, writes):
        deps = {}

        def add(k, v):
            if deps.get(k, 0) < v:
                deps[k] = v
        for r in reads:
            if r in self.lastw:
                add(*self.lastw[r])
        for w in writes:
            if w in self.lastw:
                add(*self.lastw[w])
            for k, v in self.readers.get(w, {}).items():
                add(k, v)
        return deps

    def _wait(self, e, deps):
        ck = self.clock[e]
        for k, v in sorted(deps.items(), key=lambda kv: -kv[1]):
            if e == 'pe' and k == 'pe':
                continue
            if ck.get(k, 0) >= v:
                continue
            mult = 16 if k.startswith('dq') else 1
            self.eng[e].wait_ge(self.sems[k], v * mult)
            self.nwaits += 1
            for kk, vv in self.snap.get((k, v), {}).items():
                if ck.get(kk, 0) < vv:
                    ck[kk] = vv
            ck[k] = max(ck.get(k, 0), v)

    def _record(self, key, val, reads, writes):
        for r in reads:
            d = self.readers.setdefault(r, {})
            if d.get(key, 0) < val:
                d[key] = val
        for w in writes:
            self.lastw[w] = (key, val)
            self.readers[w] = {}

    def op(self, e, fn, reads=(), writes=()):
        self._wait(e, self._deps(reads, writes))
        ins = fn()
        self.cnt[e] += 1
        v = self.cnt[e]
        ins.then_inc(self.sems[e], 1)
        self.snap[(e, v)] = dict(self.clock[e])
        self._record(e, v, reads, writes)
        return ins

    def dma(self, q, semkey, out, in_, reads=(), writes=(), **kw):
        semkey = 'dq' + semkey
        if semkey not in self.sems:
            self.sems[semkey] = self.es.enter_context(self.nc.semaphore('d_' + semkey))
            self.cnt[semkey] = 0
        deps = self._deps(reads, writes)
        if self.cnt[semkey] > 0:
            if deps.get(semkey, 0) < self.cnt[semkey]:
                deps[semkey] = self.cnt[semkey]
        self._wait(q, deps)
        ins = self.eng[q].dma_start(out=out, in_=in_, **kw)
        self.cnt[semkey] += 1
        v = self.cnt[semkey]
        ins.then_inc(self.sems[semkey], 16)
        self.snap[(semkey, v)] = dict(self.clock[q])
        self._record(semkey, v, reads, writes)
        return ins

    def alias(self, new_keys, old_keys):
        deps = {}
        for k in old_keys:
            if k in self.lastw:
                kk, vv = self.lastw[k]
                if deps.get(kk, 0) < vv:
                    deps[kk] = vv
            for kk, vv in self.readers.get(k, {}).items():
                if deps.get(kk, 0) < vv:
                    deps[kk] = vv
        for nk in new_keys:
            d = self.readers.setdefault(nk, {})
            for kk, vv in deps.items():
                if d.get(kk, 0) < vv:
                    d[kk] = vv

    def finish(self, e, keys):
        deps = {}
        for k in keys:
            if k in self.lastw:
                kk, vv = self.lastw[k]
                if deps.get(kk, 0) < vv:
                    deps[kk] = vv
        self._wait(e, deps)


def weight_tiles():
    t = []
    for hf in range(2):
        t.append(('A%d' % hf, [('w_in', 4104 + 512 * hf, 512, 8), ('w_in', 5128 + 512 * hf, 512, 8)]))
    t.append(('Q', [('w_in', 0, 1024, 8)]))
    t.append(('K', [('w_in', 1024, 1024, 8)]))
    t.append(('V', [('w_in', 2048, 1024, 8)]))
    t.append(('O', [('w_in', 3072, 1024, 8)]))
    for n in range(4):
        t.append(('M%d' % n, [('w_in', 6152 + 256 * n, 256, 8), ('w_in', 7176 + 256 * n, 256, 8),
                              ('w_branch_m', 256 * n, 256, 8), ('w_branch_c', 256 * n, 256, 8)]))
    t.append(('WO', [('w_out', 0, 1024, 8)]))
    for f in range(6):
        nc_ = 512 if f < 5 else 256
        t.append(('F%d' % f, [('w_ffn_gate', 512 * f, nc_, 8), ('w_ffn_up', 512 * f, nc_, 8)]))
    for b in range(4):
        t.append(('D%d' % b, [('w_ffn_down', 256 * b, 256, NFF)]))
    t.append(('PG', [('w_ple_gate', 0, 1024, 8)]))
    t.append(('PP', [('w_ple_proj', 0, 1024, 2)]))
    return t


def build_program(debug=False):
    nc = bass.Bass("TRN2", target_bir_lowering=False)
    din = {}

    def dram_in(name, shape):
        din[name] = nc.dram_tensor(name, shape, F32, kind="ExternalInput").ap()
        return din[name]

    x = dram_in("x", [SEQ, D])
    p_in = dram_in("p", [SEQ, PLE])
    dram_in("w_in", [D, NIN])
    dram_in("w_branch_m", [D, D])
    dram_in("w_branch_c", [D, D])
    dram_in("w_out", [D, D])
    dram_in("w_ffn_gate", [D, DFF])
    dram_in("w_ffn_up", [D, DFF])
    dram_in("w_ffn_down", [DFF, D])
    dram_in("w_ple_gate", [D, D])
    dram_in("w_ple_proj", [PLE, D])
    for nm in ["norm_mix_g", "mh_norm_g", "conf_conv_b", "conf_ln_g", "conf_ln_b", "norm_ffn_g", "norm_ple_g", "final_g"]:
        dram_in(nm, [D])
    dram_in("b_if", [8])
    dram_in("conv_qk_w", [4, 2 * D])
    dram_in("conv_qk_b", [2 * D])
    dram_in("conf_conv_w", [31, D])
    y = nc.dram_tensor("y", [SEQ, D], F32, kind="ExternalOutput").ap()

    dbg = {}
    if debug:
        for nm, shp, dt in [('x1', [SEQ, D], F32), ('x2', [SEQ, D], F32), ('x3', [SEQ, D], F32), ('hm', [SEQ, D], BF16),
                            ('us', [D, SEQ], BF16), ('q', [D, SEQ], BF16), ('k', [D, SEQ], BF16), ('v', [SEQ, D], BF16),
                            ('hT', [D, SEQ], BF16), ('mg', [D, SEQ], BF16), ('gates', [SEQ, 8], F32),
                            ('gsm', [NT * 128, 6 * 4 * NJ], F32), ('misc', [128, 2560], F32), ('ub', [D, SEQ], BF16), ('uc', [D, SEQ], F32)]:
            dbg[nm] = nc.dram_tensor('dbg_' + nm, shp, dt, kind="ExternalOutput").ap()
    tiles = weight_tiles()
    scr = {}
    for name, pieces in tiles:
        X = sum(ncols * kc for (_, _, ncols, kc) in pieces)
        scr[name] = nc.dram_tensor("scr_" + name, [128, X], BF16).ap()

    with contextlib.ExitStack() as es:
        S = Sched(nc, es)

        def sb(name, shape, dt):
            return es.enter_context(nc.sbuf_tensor(name, shape, dt))

        banks = [es.enter_context(nc.psum_tensor("bank%d" % i, [128, 512], F32)) for i in range(8)]

        def bk(i):
            return banks[i], 'ps%d' % i

        xt = sb("xt", [128, NJ, D], F32)
        hbm = sb("hbm", [128, NJ * D], BF16)
        hb = hbm[:].rearrange("p (j d) -> p j d", j=NJ)
        hmT = hbm[:].rearrange("p (c t) -> p c t", c=8)
        hT = sb("hT", [128, 8, TT], BF16)
        wsl = [sb("wsl%d" % i, [128, 8192], BF16) for i in range(3)]
        ub = sb("ub", [128, 8, HC + TT], BF16)
        XR = sb("XR", [128, 12288], BF16)
        uc = XR[:, 0:8192].bitcast(F32).rearrange("p (i t) -> p i t", i=8)
        ucb = XR[:, 8192:8192 + 2 * TT].rearrange("p (s t) -> p s t", s=2)
        usq = XR[:, 8192 + 2 * TT:8192 + 4 * TT].rearrange("p (s t) -> p s t", s=2)
        qT = XR[:, 0:4096].rearrange("p (c t) -> p c t", c=8)
        kT = XR[:, 4096:8192].rearrange("p (c t) -> p c t", c=8)
        vb = XR[:, 8192:12288].rearrange("p (j d) -> p j d", j=NJ)
        hid = XR[:, 0:NFF * TT].rearrange("p (n t) -> p n t", n=NFF)
        usp = sb("usp", [128, 4096], BF16)
        usT = usp[:].rearrange("p (c t) -> p c t", c=8)
        pt = usp[:, 0:2048].bitcast(F32).rearrange("p (j d) -> p j d", j=NJ)
        pb = usp[:, 2048:3072].rearrange("p (j d) -> p j d", j=NJ)
        pT = usp[:, 3072:4096].rearrange("p (c t) -> p c t", c=2)
        YR = sb("YR", [128, 8 * (HQ + TT)], BF16)
        qp = YR[:].rearrange("p (c t) -> p c t", c=8)
        mgT = YR[:, 0:4096].rearrange("p (c t) -> p c t", c=8)
        qhalo = sb("qhalo", [128, 16, HQ], BF16)
        kw = sb("kw", [128, 2, 4, 256], BF16)
        PTt = sb("PTt", [128, 2, 4, 128], BF16)
        Cd = sb("Cd", [128, 4, 2, 256], BF16)
        ndb = sb("ndb", [128, 8], BF16)
        Cst = sb("Cst", [128, 4, 2, 256], F32)
        nst = sb("nst", [128, 8], F32)
        mprev = sb("mprev", [128, 4], F32)
        hm = sb("hm", [128, 2, D], BF16)
        tmp = sb("tmp", [128, 4, 512], F32)
        tho = sb("tho", [128, 2, D], F32)
        gh = sb("gh", [128, D], F32)
        fg = sb("fg", [128, D], F32)
        identf = sb("identf", [128, 128], F32)
        identb = sb("identb", [128, 128], BF16)
        tri = sb("tri", [128, 128], F32)
        onesf = sb("onesf", [128, 128], F32)
        mask16 = sb("mask16", [128, 128], F32)
        onesm = sb("onesm", [128, 128], BF16)
        onescol = sb("onescol", [128, 2], BF16)
        mh4 = sb("mh4", [128, 16], F32)
        pcol = sb("pcol", [128, 384], F32)
        rows = [sb("rows%d" % i, [128, 128], F32) for i in range(4)]
        dg = sb("dg", [128, 8, 128], BF16)
        wif = sb("wif", [128, 8, 8], BF16)
        ifb = sb("ifb", [128, NJ, 8], F32)
        sqj = sb("sqj", [128, D], BF16)
        ss = sb("ss", [128, 4 * NJ], F32)
        gsb = sb("gsb", [128, NJ, 8], F32)
        gt = sb("gt", [128, 12, 4 * NJ], F32)
        bsb = sb("bsb", [128, 8 * NJ], F32)
        ab = sb("ab", [128, 4 * NJ], F32)
        amx = sb("amx", [128, 1], F32)
        dgm = sb("dgm", [128, 4 * NJ], F32)
        amaxb = sb("amaxb", [128, 4 * NJ], F32)
        Mb = sb("Mb", [128, 4 * NJ], F32)
        dmb = sb("dmb", [128, 4 * NJ], F32)
        wkb = sb("wkb", [128, 4 * NJ], F32)
        decb = sb("decb", [128, 4 * NJ], F32)
        dec16b = sb("dec16b", [128, 4 * NJ], F32)
        thrb = sb("thrb", [128, 4 * NJ], F32)
        lnst = sb("lnst", [128, 4, TT], F32)
        osm = sb("osm", [128, 2, 24], F32)

        PC_MIX, PC_FFN, PC_PLE, PC_QKB, PC_CCB, PC_LNG, PC_LNB = 0, 8, 16, 24, 40, 48, 56
        PC_QKW = 64
        PC_CW0 = 128
        PC_CW1 = 256

        def cw_col(k, i):
            return (PC_CW0 + k * 8 + i) if k < 16 else (PC_CW1 + (k - 16) * 8 + i)

        scr_keys = {}
        for name, pieces in tiles:
            off = 0
            keys = []
            for pi, (src, c0, ncols, kc) in enumerate(pieces):
                key = 'scr_%s_%d' % (name, pi)
                dst = scr[name][:, off:off + kc * ncols].rearrange("p (c n) -> p c n", c=kc)
                srcap = din[src][:, c0:c0 + ncols].rearrange("(c p) n -> p c n", p=128)
                S.dma('pool', 'cast_%s_%d' % (name, pi), dst, srcap, writes=[key])
                keys.append(key)
                off += kc * ncols
            scr_keys[name] = keys
        S.dma('pool', 'wif', wif[:], din['w_in'][:, 4096:4104].rearrange("(c p) n -> p c n", p=128), writes=['wif'])

        order = [name for name, _ in tiles] * NT
        wstate = {'next': 0, 'cur': {}}
        tile_X = {name: sum(ncols * kc for (_, _, ncols, kc) in pieces) for name, pieces in tiles}

        def wissue():
            i = wstate['next']
            if i >= len(order):
                return
            name = order[i]
            slot = i % 3
            S.dma('sp', 'w%d' % slot, wsl[slot][:, 0:tile_X[name]], scr[name][:, :],
                  reads=scr_keys[name], writes=['w%d' % slot])
            wstate['next'] += 1

        def wget(name):
            i = wstate['cur'].get('i', -1) + 1
            assert order[i] == name, (order[i], name)
            wstate['cur']['i'] = i
            slot = i % 3
            return wsl[slot], 'w%d' % slot

        def wrelease():
            wissue()

        for _ in range(3):
            wissue()

        S.op('pool', lambda: nc.gpsimd.memset(identf[:], 1.0), writes=['identf'])
        S.op('pool', lambda: nc.gpsimd.affine_select(out=identf[:], in_=identf[:], pattern=[[-1, 128]], base=0,
                                                      channel_multiplier=1, compare_op=ALU.is_equal, fill=0.0),
             reads=['identf'], writes=['identf'])
        S.op('pool', lambda: nc.gpsimd.memset(tri[:], 1.0), writes=['tri'])
        S.op('pool', lambda: nc.gpsimd.affine_select(out=tri[:], in_=tri[:], pattern=[[1, 128]], base=0,
                                                      channel_multiplier=-1, compare_op=ALU.is_ge, fill=0.0),
             reads=['tri'], writes=['tri'])
        S.op('pool', lambda: nc.gpsimd.memset(onesf[:], 1.0), writes=['onesf'])
        S.op('pool', lambda: nc.gpsimd.memset(onesm[:], 1.0 / D), writes=['onesm'])
        S.op('pool', lambda: nc.gpsimd.memset(onescol[:], 1.0), writes=['onescol'])
        S.op('pool', lambda: nc.gpsimd.memset(mh4[:], -0.5), writes=['mh4'])
        S.op('dve', lambda: nc.vector.tensor_copy(out=identb[:], in_=identf[:]), reads=['identf'], writes=['identb'])
        S.op('dve', lambda: nc.vector.tensor_scalar(out=mask16[:], in0=tri[:], scalar1=1.0 / 16, scalar2=None,
                                                    op0=ALU.mult), reads=['tri'], writes=['mask16'])
        S.op('pool', lambda: nc.gpsimd.memset(Cst[:], 0.0), writes=['C'])
        S.op('pool', lambda: nc.gpsimd.memset(nst[:], 0.0), writes=['nst'])
        S.op('pool', lambda: nc.gpsimd.memset(mprev[:], 0.0), writes=['mprev'])
        S.op('pool', lambda: nc.gpsimd.memset(ub[:], 0.0), writes=['ub%d' % i for i in range(8)])
        S.op('pool', lambda: nc.gpsimd.memset(qhalo[:], 0.0), writes=['qhalo'])
        S.dma('sp', 'c_gh', gh[:], din['mh_norm_g'].partition_broadcast(128), writes=['gh'])
        S.dma('sp', 'c_fg', fg[:], din['final_g'].partition_broadcast(128), writes=['fg'])
        for j in range(NJ):
            S.dma('sp', 'c_ifb%d' % j, ifb[:, j, :], din['b_if'].partition_broadcast(128), writes=['ifb'])
        S.op('dve', lambda: nc.vector.tensor_scalar(out=gh[:], in0=gh[:], scalar1=0.5, scalar2=None, op0=ALU.mult),
             reads=['gh'], writes=['gh'])
        S.op('pool', lambda: nc.gpsimd.memset(rows[0][:], 0.0), writes=['rows0'])
        rowspec = [("norm_mix_g", PC_MIX), ("norm_ffn_g", PC_FFN), ("norm_ple_g", PC_PLE), ("conf_conv_b", PC_CCB),
                   ("conf_ln_g", PC_LNG), ("conf_ln_b", PC_LNB)]
        for nm, base in rowspec:
            S.dma('sp', 'c_' + nm, rows[0][base:base + 8, :], din[nm].rearrange("(i p) -> i p", p=128),
                  reads=['rows0'], writes=['rows0_' + nm])
        S.dma('sp', 'c_qkb', rows[0][PC_QKB:PC_QKB + 16, :], din['conv_qk_b'].rearrange("(i p) -> i p", p=128),
              reads=['rows0'], writes=['rows0_qkb'])
        S.dma('sp', 'c_qkw', rows[1][0:64, :], din['conv_qk_w'].rearrange("k (i p) -> (k i) p", p=128), writes=['rows1'])
        S.dma('sp', 'c_cw0', rows[2][0:128, :], din['conf_conv_w'][0:16, :].rearrange("k (i p) -> (k i) p", p=128),
              writes=['rows2'])
        S.dma('sp', 'c_cw1', rows[3][0:120, :], din['conf_conv_w'][16:31, :].rearrange("k (i p) -> (k i) p", p=128),
              writes=['rows3'])
        r0keys = ['rows0_' + nm for nm, _ in rowspec] + ['rows0_qkb']
        for gi, (R, base, keys) in enumerate([(64, 0, r0keys), (64, PC_QKW, ['rows1']), (128, PC_CW0, ['rows2']),
                                              (120, PC_CW1, ['rows3'])]):
            bt, bkey = bk(S.bank())
            S.op('pe', lambda gi=gi, R=R, bt=bt: nc.tensor.matmul(bt[:, 0:R], lhsT=rows[gi][0:R, :],
                                                               rhs=identf[0:R, 0:R], start=True, stop=True),
                 reads=keys + ['identf'], writes=[bkey])
            S.op('dve', lambda R=R, base=base, bt=bt: nc.vector.tensor_copy(out=pcol[:, base:base + R], in_=bt[:, 0:R]),
                 reads=[bkey], writes=['pcol'])

        tmpi = {'i': 0}

        def tmpbuf():
            i = tmpi['i'] % 4
            tmpi['i'] += 1
            return tmp[:, i, :], 'tmp%d' % i

        dgi = {'i': 0}

        def diag(col, scale):
            i = dgi['i'] % 8
            dgi['i'] += 1
            S.op('pool', lambda: nc.gpsimd.tensor_scalar(out=dg[:, i, :], in0=identf[:], scalar1=pcol[:, col:col + 1],
                                                         scalar2=scale, op0=ALU.mult, op1=ALU.mult),
                 reads=['identf', 'pcol'], writes=['dg%d' % i])
            return dg[:, i, :], 'dg%d' % i

        evi = {'i': 0}

        def evac_copy(out, in_, reads, writes):
            evi['i'] += 1
            if evi['i'] % 2:
                S.op('act', lambda: nc.scalar.copy(out=out, in_=in_), reads=reads, writes=writes)
            else:
                S.op('dve', lambda: nc.vector.tensor_copy(out=out, in_=in_), reads=reads, writes=writes)

        def js(j):
            return slice(j * 128, (j + 1) * 128)

        def norm_T(gbase):
            for j in range(NJ):
                S.op('act', lambda j=j: nc.scalar.activation(out=sqj[:], in_=xt[:, j, :], func=AF.Square,
                                                           accum_out=ss[:, j:j + 1]),
                     reads=['xt%d' % j], writes=['ssa%d' % j])
            S.op('dve', lambda: nc.vector.tensor_scalar(out=ss[:, NJ:2 * NJ], in0=ss[:, 0:NJ], scalar1=1.0 / D, scalar2=EPS,
                                                        op0=ALU.mult, op1=ALU.add), reads=['ssa%d' % j for j in range(NJ)], writes=['ss'])
            S.op('pool', lambda: nc.gpsimd.tensor_tensor(out=ss[:, 2 * NJ:3 * NJ], in0=ss[:, NJ:2 * NJ], in1=mh4[:, 0:NJ],
                                                         op=ALU.pow), reads=['ss', 'mh4'], writes=['ss'])
            for j in range(NJ):
                S.op('dve', lambda j=j: nc.vector.tensor_scalar(out=hb[:, j, :], in0=xt[:, j, :],
                                                              scalar1=ss[:, 2 * NJ + j:2 * NJ + j + 1], scalar2=None,
                                                              op0=ALU.mult),
                     reads=['xt%d' % j, 'ss'], writes=['hb%d' % j])
            for c in range(8):
                if c % 2 == 0:
                    bt, bkey = bk(S.bank())
                    btb = bt[:].bitcast(BF16)
                for j in range(NJ):
                    o = (c % 2) * 512 + j * 128
                    S.op('pe', lambda c=c, j=j, o=o, btb=btb: nc.tensor.transpose(out=btb[:, o:o + 128],
                                                                               in_=hb[:, j, c * 128:(c + 1) * 128],
                                                                               identity=identb[:]),
                         reads=['hb%d' % j, 'identb'], writes=[bkey])
                o = (c % 2) * 512
                if c % 2 == 0:
                    S.op('dve', lambda c=c, o=o, btb=btb: nc.vector.tensor_scalar(out=hT[:, c, :], in0=btb[:, o:o + TT],
                                                                               scalar1=pcol[:, gbase + c:gbase + c + 1],
                                                                               scalar2=None, op0=ALU.mult),
                         reads=[bkey, 'pcol'], writes=['hT%d' % c])
                else:
                    S.op('act', lambda c=c, o=o, btb=btb: nc.scalar.activation(out=hT[:, c, :], in_=btb[:, o:o + TT],
                                                                            func=AF.Identity,
                                                                            scale=pcol[:, gbase + c:gbase + c + 1]),
                         reads=[bkey, 'pcol'], writes=['hT%d' % c])

        def mm8(bt, bkey, lhs_fn, rhs_fn, lkeys, n=8):
            for kc in range(n):
                S.op('pe', lambda kc=kc: nc.tensor.matmul(bt, lhsT=lhs_fn(kc), rhs=rhs_fn(kc), start=(kc == 0),
                                                         stop=(kc == n - 1)),
                     reads=lkeys, writes=[bkey])

        hbk = ['hb%d' % j for j in range(NJ)]
        hTk = ['hT%d' % c for c in range(8)]
        hmTk = ['hmT%d' % j for j in range(NJ)]
        uck = ['uc%d' % i for i in range(8)] + ['ucb0', 'ucb1', 'usq0', 'usq1']
        qkvk = ['qT%d' % c for c in range(8)] + ['kT%d' % c for c in range(8)] + ['vb%d' % j for j in range(NJ)]
        hidk = ['hid%d' % n for n in range(NFF)]
        usk = ['us%d' % i for i in range(8)]
        ppk = ['pt%d' % j for j in range(NJ)] + ['pb', 'pT']
        qpk = ['qp%d' % c for c in range(8)]
        mgk = ['mg%d' % c for c in range(8)]
        ubk = ['ub%d' % i for i in range(8)]

        dcnt = {'i': 0}

        def dump(name, dst, src, reads):
            if not debug:
                return
            dcnt['i'] += 1
            S.dma('sp', 'dbg%d' % (dcnt['i'] % 6), dst, src, reads=reads, writes=['dbgout%d' % dcnt['i']])

        for T in range(NT):
            t0 = T * TT
            for j in range(NJ):
                S.dma('sp', 'x%d' % j, xt[:, j, :], x[t0 + j * 128:t0 + (j + 1) * 128, :], writes=['xt%d' % j])

            S.alias(hbk, hmTk)
            norm_T(PC_MIX)

            if debug:
                dump('hT', dbg['hT'][:, t0:t0 + TT].rearrange("(c p) t -> p c t", p=128), hT[:, :, :], hTk)
            gbt, gbkey = bk(S.bank())
            for j in range(NJ):
                mm8(gbt[:, j * 8:(j + 1) * 8], gbkey, lambda kc, j=j: hT[:, kc, js(j)], lambda kc: wif[:, kc, :],
                    hTk + ['wif'])
            S.op('dve', lambda: nc.vector.tensor_tensor(out=gsb[:].rearrange("p j e -> p (j e)"), in0=gbt[:, 0:8 * NJ],
                                                        in1=ifb[:].rearrange("p j e -> p (j e)"), op=ALU.add),
                 reads=[gbkey, 'ifb'], writes=['gsb'])

            if debug:
                dump('gates', dbg['gates'][t0:t0 + TT, :].rearrange("(j p) e -> p j e", p=128), gsb[:, :, :], ['gsb'])
            S.alias(uck, hidk)
            S.op('pool', lambda: nc.gpsimd.tensor_copy(out=ub[:, :, 0:HC], in_=ub[:, :, TT:TT + HC]),
                 reads=ubk, writes=ubk)
            for hf in range(2):
                wt, wkey = wget('A%d' % hf)
                wv = wt[:, 0:8192].rearrange("p (m c n) -> p m c n", m=2, c=8)
                for il in range(4):
                    i = hf * 4 + il
                    ba, bakey = bk(S.bank())
                    bg, bgkey = bk(S.bank())
                    mm8(ba[:, 0:TT], bakey, lambda kc, il=il: wv[:, 0, kc, il * 128:(il + 1) * 128],
                        lambda kc: hT[:, kc, :], hTk + [wkey])
                    mm8(bg[:, 0:TT], bgkey, lambda kc, il=il: wv[:, 1, kc, il * 128:(il + 1) * 128],
                        lambda kc: hT[:, kc, :], hTk + [wkey])
                    tb, tkey = tmpbuf()
                    S.op('act', lambda: nc.scalar.activation(out=tb, in_=bg[:, 0:TT], func=AF.Tanh, scale=0.5),
                         reads=[bgkey], writes=[tkey])
                    S.op('dve', lambda i=i: nc.vector.scalar_tensor_tensor(out=ub[:, i, HC:HC + TT], in0=tb, scalar=1.0,
                                                                         in1=ba[:, 0:TT], op0=ALU.add, op1=ALU.mult),
                         reads=[tkey, bakey], writes=['ub%d' % i])
                wrelease()
            Mbank, Mkey = bk(6)
            Qbank, Qkey = bk(7)
            for i in range(8):
                bc, bckey = bk(S.bank())
                for k in range(31):
                    dgt, dgkey = diag(cw_col(k, i), 0.5)
                    S.op('pe', lambda k=k, i=i, dgt=dgt: nc.tensor.matmul(bc[:, 0:TT], lhsT=dgt, rhs=ub[:, i, k:k + TT],
                                                                       start=(k == 0), stop=(k == 30)),
                         reads=[dgkey, 'ub%d' % i], writes=[bckey])
                S.op('act', lambda i=i: nc.scalar.activation(out=uc[:, i, :], in_=bc[:, 0:TT], func=AF.Identity,
                                                           bias=pcol[:, PC_CCB + i:PC_CCB + i + 1]),
                     reads=[bckey, 'pcol'], writes=['uc%d' % i])
                s2 = i % 2
                S.op('dve', lambda i=i, s2=s2: nc.vector.tensor_copy(out=ucb[:, s2, :], in_=uc[:, i, :]),
                     reads=['uc%d' % i], writes=['ucb%d' % s2])
                S.op('act', lambda i=i, s2=s2: nc.scalar.activation(out=usq[:, s2, :], in_=uc[:, i, :], func=AF.Square),
                     reads=['uc%d' % i], writes=['usq%d' % s2])
                S.op('pe', lambda i=i, s2=s2: nc.tensor.matmul(Mbank[:, 0:TT], lhsT=onesm[:], rhs=ucb[:, s2, :],
                                                             start=(i == 0), stop=(i == 7)),
                     reads=['onesm', 'ucb%d' % s2], writes=[Mkey])
                S.op('pe', lambda i=i, s2=s2: nc.tensor.matmul(Qbank[:, 0:TT], lhsT=onesm[:], rhs=usq[:, s2, :],
                                                             start=(i == 0), stop=(i == 7)),
                     reads=['onesm', 'usq%d' % s2], writes=[Qkey])

            if debug:
                dump('ub', dbg['ub'][:, t0:t0 + TT].rearrange("(c p) t -> p c t", p=128), ub[:, :, HC:HC + TT], ubk)
                dump('uc', dbg['uc'][:, t0:t0 + TT].rearrange("(c p) t -> p c t", p=128), uc[:, :, :], uck)
            msq, lnv, rstd, mr = lnst[:, 0, :], lnst[:, 1, :], lnst[:, 2, :], lnst[:, 3, :]
            S.op('act', lambda: nc.scalar.activation(out=msq, in_=Mbank[:, 0:TT], func=AF.Square),
                 reads=[Mkey], writes=['ln0'])
            S.op('dve', lambda: nc.vector.tensor_tensor(out=msq, in0=Qbank[:, 0:TT], in1=msq, op=ALU.subtract),
                 reads=[Qkey, 'ln0'], writes=['ln0'])
            S.op('act', lambda: nc.scalar.activation(out=lnv, in_=msq, func=AF.Ln, bias=EPS), reads=['ln0'], writes=['ln1'])
            S.op('act', lambda: nc.scalar.activation(out=rstd, in_=lnv, func=AF.Exp, scale=-0.5),
                 reads=['ln1'], writes=['ln2'])
            S.op('dve', lambda: nc.vector.tensor_tensor(out=mr, in0=Mbank[:, 0:TT], in1=rstd, op=ALU.mult),
                 reads=[Mkey, 'ln2'], writes=['ln3'])
            G = 4 * NJ

            def g3(t):
                return t.rearrange("p (j h) -> p j h", j=NJ)
            zf = gsb[:, :, 4:8]
            igf = gsb[:, :, 0:4]
            t_az, t_e, t_l, t_mn, t_lf, t_wa, t_ta = [gt[:, i, :] for i in range(7)]
            S.op('dve', lambda: nc.vector.scalar_tensor_tensor(out=g3(t_az), in0=zf, scalar=-1.0, in1=zf, op0=ALU.mult,
                                                               op1=ALU.max), reads=['gsb'], writes=['gt0'])
            S.op('act', lambda: nc.scalar.activation(out=t_e, in_=t_az, func=AF.Exp, scale=-1.0), reads=['gt0'], writes=['gt1'])
            S.op('act', lambda: nc.scalar.activation(out=t_l, in_=t_e, func=AF.Ln, bias=1.0), reads=['gt1'], writes=['gt2'])
            S.op('dve', lambda: nc.vector.tensor_scalar(out=g3(t_mn), in0=zf, scalar1=0.0, scalar2=None, op0=ALU.min),
                 reads=['gsb'], writes=['gt3'])
            S.op('dve', lambda: nc.vector.tensor_tensor(out=t_lf, in0=t_mn, in1=t_l, op=ALU.subtract),
                 reads=['gt3', 'gt2'], writes=['gt4'])
            g1, g1key = bk(S.bank())
            S.op('pe', lambda: nc.tensor.matmul(g1[:, 0:G], lhsT=tri[:], rhs=t_lf, start=True, stop=True),
                 reads=['tri', 'gt4'], writes=[g1key])
            S.op('pe', lambda: nc.tensor.matmul(g1[:, G:2 * G], lhsT=onesf[:], rhs=t_lf, start=True, stop=True),
                 reads=['onesf', 'gt4'], writes=[g1key])
            S.op('dve', lambda: nc.vector.tensor_copy(out=bsb[:, 0:2 * G], in_=g1[:, 0:2 * G]), reads=[g1key], writes=['bsb'])
            S.op('dve', lambda: nc.vector.tensor_tensor(out=g3(ab[:, :]), in0=igf, in1=g3(bsb[:, 0:G]), op=ALU.subtract),
                 reads=['gsb', 'bsb'], writes=['ab'])
            g2, g2key = bk(S.bank())
            S.op('pe', lambda: nc.tensor.matmul(g2[0:G, 0:128], lhsT=ab[:, :], rhs=identf[:], start=True, stop=True),
                 reads=['ab', 'identf'], writes=[g2key])
            S.op('dve', lambda: nc.vector.tensor_reduce(out=amx[0:G, 0:1], in_=g2[0:G, 0:128], axis=AX.X, op=ALU.max),
                 reads=[g2key], writes=['amx'])
            S.op('dve', lambda: nc.vector.tensor_scalar(out=dgm[0:G, 0:G], in0=identf[0:G, 0:G], scalar1=amx[0:G, 0:1],
                                                        scalar2=None, op0=ALU.mult), reads=['amx', 'identf'], writes=['dgm'])
            S.op('pe', lambda: nc.tensor.matmul(g2[:, 128:128 + G], lhsT=onesf[0:G, :], rhs=dgm[0:G, 0:G], start=True,
                                                stop=True), reads=['onesf', 'dgm'], writes=[g2key])
            S.op('dve', lambda: nc.vector.tensor_copy(out=amaxb[:, :], in_=g2[:, 128:128 + G]), reads=[g2key],
                 writes=['amaxb'])
            for j in range(NJ):
                hs = slice(4 * j, 4 * j + 4)
                S.op('dve', lambda hs=hs: nc.vector.tensor_tensor(out=Mb[:, hs], in0=mprev[:, :], in1=amaxb[:, hs], op=ALU.max),
                     reads=['mprev', 'amaxb'], writes=['Mb'])
                S.op('dve', lambda hs=hs: nc.vector.tensor_tensor(out=dmb[:, hs], in0=mprev[:, :], in1=Mb[:, hs],
                                                                op=ALU.subtract), reads=['mprev', 'Mb'], writes=['dmb'])
                S.op('dve', lambda hs=hs, j=j: nc.vector.tensor_tensor(out=mprev[:, :], in0=bsb[:, G + 4 * j:G + 4 * j + 4],
                                                                     in1=Mb[:, hs], op=ALU.add),
                     reads=['bsb', 'Mb'], writes=['mprev'])
            S.op('dve', lambda: nc.vector.tensor_tensor(out=t_wa, in0=ab[:, :], in1=Mb[:, :], op=ALU.subtract),
                 reads=['ab', 'Mb'], writes=['gt5'])
            S.op('act', lambda: nc.scalar.activation(out=wkb[:, :], in_=t_wa, func=AF.Exp), reads=['gt5'], writes=['wkb'])
            S.op('act', lambda: nc.scalar.activation(out=decb[:, :], in_=dmb[:, :], func=AF.Exp), reads=['dmb'], writes=['decb'])
            S.op('dve', lambda: nc.vector.tensor_scalar(out=dec16b[:, :], in0=decb[:, :], scalar1=1.0 / 16, scalar2=None,
                                                        op0=ALU.mult), reads=['decb'], writes=['dec16b'])
            S.op('dve', lambda: nc.vector.tensor_tensor(out=t_ta, in0=bsb[:, 0:G], in1=Mb[:, :], op=ALU.add),
                 reads=['bsb', 'Mb'], writes=['gt6'])
            S.op('act', lambda: nc.scalar.activation(out=thrb[:, :], in_=t_ta, func=AF.Exp, scale=-1.0),
                 reads=['gt6'], writes=['thrb'])
            S.alias(usk, ppk)
            for i in range(8):
                ta, takey = tmpbuf()
                S.op('dve', lambda i=i, ta=ta: nc.vector.tensor_tensor(out=ta, in0=uc[:, i, :], in1=rstd, op=ALU.mult),
                     reads=['uc%d' % i, 'ln2'], writes=[takey])
                S.op('pool', lambda ta=ta: nc.gpsimd.tensor_tensor(out=ta, in0=ta, in1=mr, op=ALU.subtract),
                     reads=[takey, 'ln3'], writes=[takey])
                S.op('act', lambda i=i, ta=ta: nc.scalar.activation(out=usT[:, i, :], in_=ta, func=AF.Silu,
                                                                  scale=pcol[:, PC_LNG + i:PC_LNG + i + 1],
                                                                  bias=pcol[:, PC_LNB + i:PC_LNB + i + 1]),
                     reads=[takey, 'pcol'], writes=['us%d' % i])

            if debug:
                dump('us', dbg['us'][:, t0:t0 + TT].rearrange("(c p) t -> p c t", p=128), usT[:, :, :], usk)
                G_ = 4 * NJ
                for ii, (tt_, kk_) in enumerate([(bsb[:, 0:G_], 'bsb'), (ab[:, :], 'ab'), (Mb[:, :], 'Mb'), (wkb[:, :], 'wkb'),
                                                 (decb[:, :], 'decb'), (thrb[:, :], 'thrb')]):
                    dump('gsm', dbg['gsm'][T * 128:(T + 1) * 128, ii * G_:(ii + 1) * G_], tt_, [kk_])
            S.alias(qkvk, uck)
            for piece, (wname, dstT, dkey) in enumerate([('Q', qT, 'qT'), ('K', kT, 'kT')]):
                wt, wkey = wget(wname)
                wv = wt[:, 0:8192].rearrange("p (c n) -> p c n", c=8)
                S.alias(qpk, mgk)
                for c in range(8):
                    bq, bqkey = bk(S.bank())
                    mm8(bq[:, 0:TT], bqkey, lambda kc, c=c: wv[:, kc, c * 128:(c + 1) * 128], lambda kc: hT[:, kc, :],
                        hTk + [wkey])
                    evac_copy(qp[:, c, HQ:HQ + TT], bq[:, 0:TT], [bqkey], ['qp%d' % c])
                wrelease()
                S.op('pool', lambda piece=piece: nc.gpsimd.tensor_copy(out=qp[:, :, 0:HQ],
                                                                       in_=qhalo[:, piece * 8:(piece + 1) * 8, :]),
                     reads=['qhalo'] + qpk, writes=qpk)
                S.op('pool', lambda piece=piece: nc.gpsimd.tensor_copy(out=qhalo[:, piece * 8:(piece + 1) * 8, :],
                                                                       in_=qp[:, :, TT:TT + HQ]),
                     reads=qpk, writes=['qhalo'])
                for c in range(8):
                    bq, bqkey = bk(S.bank())
                    cc = piece * 8 + c
                    for k in range(4):
                        dgt, dgkey = diag(PC_QKW + k * 16 + cc, 1.0)
                        S.op('pe', lambda k=k, c=c, dgt=dgt, bq=bq: nc.tensor.matmul(bq[:, 0:TT], lhsT=dgt,
                                                                                  rhs=qp[:, c, k:k + TT], start=(k == 0),
                                                                                  stop=(k == 3)),
                             reads=[dgkey, 'qp%d' % c], writes=[bqkey])
                    S.op('act', lambda c=c, cc=cc, bq=bq, dstT=dstT: nc.scalar.activation(
                        out=dstT[:, c, :], in_=bq[:, 0:TT], func=AF.Silu, bias=pcol[:, PC_QKB + cc:PC_QKB + cc + 1]),
                        reads=[bqkey, 'pcol'], writes=['%s%d' % (dkey, c)])

            wt, wkey = wget('V')
            wv = wt[:, 0:8192].rearrange("p (c n) -> p c n", c=8)
            for j in range(NJ):
                for hf in range(2):
                    bv, bvkey = bk(S.bank())
                    mm8(bv[:, :], bvkey, lambda kc, j=j: hT[:, kc, js(j)], lambda kc, hf=hf: wv[:, kc, hf * 512:(hf + 1) * 512],
                        hTk + [wkey])
                    evac_copy(vb[:, j, hf * 512:(hf + 1) * 512], bv[:, :], [bvkey], ['vb%d' % j])
            wrelease()

            if debug:
                dump('q', dbg['q'][:, t0:t0 + TT].rearrange("(c p) t -> p c t", p=128), qT[:, :, :], qkvk)
                dump('k', dbg['k'][:, t0:t0 + TT].rearrange("(c p) t -> p c t", p=128), kT[:, :, :], qkvk)
                dump('v', dbg['v'][t0:t0 + TT, :].rearrange("(j p) d -> p j d", p=128), vb[:, :, :], qkvk)
            wtO, wOkey = wget('O')
            wO = wtO[:, 0:8192].rearrange("p (c n) -> p c n", c=8)
            S.alias(hmTk, hbk)
            for j in range(NJ):
                s2 = j % 2
                gi = 4 * j
                for h in range(4):
                    S.op('pool', lambda h=h: nc.gpsimd.tensor_scalar(
                        out=Cd[:, h, :, :].rearrange("p a e -> p (a e)"), in0=Cst[:, h, :, :].rearrange("p a e -> p (a e)"),
                        scalar1=dec16b[:, gi + h:gi + h + 1], scalar2=1.0, op0=ALU.mult, op1=ALU.mult),
                        reads=['C', 'dec16b'], writes=['Cd'])
                    S.op('pool', lambda h=h: nc.gpsimd.tensor_scalar(
                        out=ndb[:, 2 * h:2 * h + 2], in0=nst[:, 2 * h:2 * h + 2], scalar1=dec16b[:, gi + h:gi + h + 1],
                        scalar2=1.0, op0=ALU.mult, op1=ALU.mult), reads=['nst', 'dec16b'], writes=['ndb'])
                bT, bTkey = bk(S.bank())
                bTb = bT[:].bitcast(BF16)
                for c in range(8):
                    S.op('pe', lambda c=c: nc.tensor.transpose(out=bTb[:, c * 128:(c + 1) * 128], in_=kT[:, c, js(j)],
                                                              identity=identb[:]),
                         reads=['kT%d' % c, 'identb'], writes=[bTkey])
                for h in range(4):
                    S.op('dve', lambda h=h: nc.vector.tensor_scalar(out=kw[:, s2, h, :], in0=bTb[:, h * 256:(h + 1) * 256],
                                                                  scalar1=wkb[:, gi + h:gi + h + 1], scalar2=None,
                                                                  op0=ALU.mult),
                         reads=[bTkey, 'wkb'], writes=['kw%d' % s2])
                bS, bSkey = bk(S.bank())
                for h in range(4):
                    for a in range(2):
                        S.op('pe', lambda h=h, a=a: nc.tensor.matmul(bS[:, h * 128:(h + 1) * 128], lhsT=kT[:, 2 * h + a, js(j)],
                                                                   rhs=qT[:, 2 * h + a, js(j)], start=(a == 0), stop=(a == 1)),
                             reads=['kT%d' % (2 * h + a), 'qT%d' % (2 * h + a)], writes=[bSkey])
                for h in range(4):
                    S.op('dve', lambda h=h: nc.vector.scalar_tensor_tensor(out=PTt[:, s2, h, :], in0=bS[:, h * 128:(h + 1) * 128],
                                                                         scalar=wkb[:, gi + h:gi + h + 1], in1=mask16[:],
                                                                         op0=ALU.mult, op1=ALU.mult),
                         reads=[bSkey, 'wkb', 'mask16'], writes=['PT%d' % s2])
                bN = [bk(S.bank()), bk(S.bank())]
                bD, bDkey = bk(S.bank())
                for h in range(4):
                    bn, bnkey = bN[h // 2]
                    o = (h % 2) * 256
                    S.op('pe', lambda h=h, bn=bn, o=o: nc.tensor.matmul(bn[:, o:o + 256], lhsT=PTt[:, s2, h, :],
                                                                     rhs=vb[:, j, h * 256:(h + 1) * 256], start=True, stop=False),
                         reads=['PT%d' % s2, 'vb%d' % j], writes=[bnkey])
                    for a in range(2):
                        S.op('pe', lambda h=h, a=a, bn=bn, o=o: nc.tensor.matmul(bn[:, o:o + 256], lhsT=qT[:, 2 * h + a, js(j)],
                                                                              rhs=Cd[:, h, a, :], start=False, stop=(a == 1)),
                             reads=['qT%d' % (2 * h + a), 'Cd'], writes=[bnkey])
                    S.op('pe', lambda h=h: nc.tensor.matmul(bD[:, h:h + 1], lhsT=PTt[:, s2, h, :], rhs=onescol[:, 0:1],
                                                           start=True, stop=False),
                         reads=['PT%d' % s2, 'onescol'], writes=[bDkey])
                    for a in range(2):
                        S.op('pe', lambda h=h, a=a: nc.tensor.matmul(bD[:, h:h + 1], lhsT=qT[:, 2 * h + a, js(j)],
                                                                   rhs=ndb[:, 2 * h + a:2 * h + a + 1], start=False,
                                                                   stop=(a == 1)),
                             reads=['qT%d' % (2 * h + a), 'ndb'], writes=[bDkey])
                thj = tho[:, s2, :]
                for hf in range(2):
                    bo, bokey = bk(S.bank())
                    mm8(bo[:, :], bokey, lambda kc: hT[:, kc, js(j)], lambda kc, hf=hf: wO[:, kc, hf * 512:(hf + 1) * 512],
                        hTk + [wOkey])
                    S.op('act', lambda hf=hf, bo=bo: nc.scalar.activation(out=thj[:, hf * 512:(hf + 1) * 512], in_=bo[:, :],
                                                                       func=AF.Tanh, scale=0.5),
                         reads=[bokey], writes=['tho%d' % s2])
                S.op('dve', lambda: nc.vector.scalar_tensor_tensor(out=thj, in0=thj, scalar=1.0, in1=gh[:], op0=ALU.add,
                                                                   op1=ALU.mult), reads=['tho%d' % s2, 'gh'], writes=['tho%d' % s2])
                sm = osm[:, s2, :]
                for h in range(4):
                    bn, bnkey = bN[h // 2]
                    o = (h % 2) * 256
                    S.op('act', lambda h=h, bn=bn, o=o: nc.scalar.activation(out=sqj[:, 0:256], in_=bn[:, o:o + 256],
                                                                          func=AF.Square, accum_out=sm[:, h:h + 1]),
                         reads=[bnkey], writes=['osm%d' % s2])
                S.op('dve', lambda: nc.vector.tensor_copy(out=sm[:, 4:8], in_=bD[:, 0:4]), reads=[bDkey], writes=['osm%d' % s2])
                S.op('dve', lambda: nc.vector.scalar_tensor_tensor(out=sm[:, 8:12], in0=sm[:, 4:8], scalar=-1.0, in1=sm[:, 4:8],
                                                                   op0=ALU.mult, op1=ALU.max),
                     reads=['osm%d' % s2], writes=['osm%d' % s2])
                S.op('dve', lambda: nc.vector.tensor_tensor(out=sm[:, 8:12], in0=sm[:, 8:12], in1=thrb[:, gi:gi + 4], op=ALU.max),
                     reads=['osm%d' % s2, 'thrb'], writes=['osm%d' % s2])
                S.op('dve', lambda: nc.vector.scalar_tensor_tensor(out=sm[:, 12:16], in0=sm[:, 8:12], scalar=EPS, in1=sm[:, 8:12],
                                                                   op0=ALU.mult, op1=ALU.mult),
                     reads=['osm%d' % s2], writes=['osm%d' % s2])
                S.op('dve', lambda: nc.vector.scalar_tensor_tensor(out=sm[:, 16:20], in0=sm[:, 0:4], scalar=1.0 / 256,
                                                                   in1=sm[:, 12:16], op0=ALU.mult, op1=ALU.add),
                     reads=['osm%d' % s2], writes=['osm%d' % s2])
                S.op('pool', lambda: nc.gpsimd.tensor_tensor(out=sm[:, 20:24], in0=sm[:, 16:20], in1=mh4[:, 0:4], op=ALU.pow),
                     reads=['osm%d' % s2, 'mh4'], writes=['osm%d' % s2])
                for h in range(4):
                    bn, bnkey = bN[h // 2]
                    o = (h % 2) * 256
                    S.op('dve', lambda h=h, bn=bn, o=o: nc.vector.scalar_tensor_tensor(
                        out=hm[:, s2, h * 256:(h + 1) * 256], in0=bn[:, o:o + 256], scalar=sm[:, 20 + h:21 + h],
                        in1=thj[:, h * 256:(h + 1) * 256], op0=ALU.mult, op1=ALU.mult),
                        reads=[bnkey, 'osm%d' % s2, 'tho%d' % s2], writes=['hm%d' % s2])
                if debug:
                    if T == 0 and j == 0:
                        misc = es.enter_context(nc.sbuf_tensor("miscb", [128, 2560], F32))
                        S.op('dve', lambda: nc.vector.tensor_copy(out=misc[:, 0:512], in_=bN[0][0][:, :]), reads=[bN[0][1]], writes=['misc'])
                        S.op('dve', lambda: nc.vector.tensor_copy(out=misc[:, 512:1024], in_=bN[1][0][:, :]), reads=[bN[1][1]], writes=['misc'])
                        S.op('dve', lambda: nc.vector.tensor_copy(out=misc[:, 1024:1536], in_=PTt[:, s2, :, :].rearrange("p h t -> p (h t)")), reads=['PT%d' % s2], writes=['misc'])
                        S.op('dve', lambda: nc.vector.tensor_copy(out=misc[:, 1536:1560], in_=sm), reads=['osm%d' % s2], writes=['misc'])
                        S.op('dve', lambda: nc.vector.tensor_copy(out=misc[:, 1600:1616], in_=wkb[:, :]), reads=['wkb'], writes=['misc'])
                        S.op('dve', lambda: nc.vector.tensor_copy(out=misc[:, 1616:1632], in_=thrb[:, :]), reads=['thrb'], writes=['misc'])
                        S.op('dve', lambda: nc.vector.tensor_copy(out=misc[:, 2048:2560], in_=bS[:, :]), reads=[bSkey], writes=['misc'])
                        dump('misc', dbg['misc'][:, :], misc[:, :], ['misc'])
                bU = [bk(S.bank()) for _ in range(4)]
                bD2, bD2key = bk(S.bank())
                for h in range(4):
                    for a in range(2):
                        bu, bukey = bU[h]
                        S.op('pe', lambda h=h, a=a, bu=bu: nc.tensor.matmul(bu[:, a * 256:(a + 1) * 256],
                                                                         lhsT=kw[:, s2, h, a * 128:(a + 1) * 128],
                                                                         rhs=vb[:, j, h * 256:(h + 1) * 256], start=True, stop=True),
                             reads=['kw%d' % s2, 'vb%d' % j], writes=[bukey])
                        S.op('pe', lambda h=h, a=a: nc.tensor.matmul(bD2[:, 2 * h + a:1 + 2 * h + a],
                                                                   lhsT=kw[:, s2, h, a * 128:(a + 1) * 128], rhs=onescol[:, 0:1],
                                                                   start=True, stop=True),
                             reads=['kw%d' % s2, 'onescol'], writes=[bD2key])
                for h in range(4):
                    bu, bukey = bU[h]
                    S.op('dve', lambda h=h, bu=bu: nc.vector.scalar_tensor_tensor(
                        out=Cst[:, h, :, :].rearrange("p a e -> p (a e)"), in0=Cst[:, h, :, :].rearrange("p a e -> p (a e)"),
                        scalar=decb[:, gi + h:gi + h + 1], in1=bu[:, 0:512], op0=ALU.mult, op1=ALU.add),
                        reads=['C', 'decb', bukey], writes=['C'])
                    S.op('dve', lambda h=h: nc.vector.scalar_tensor_tensor(
                        out=nst[:, 2 * h:2 * h + 2], in0=nst[:, 2 * h:2 * h + 2], scalar=decb[:, gi + h:gi + h + 1],
                        in1=bD2[:, 2 * h:2 + 2 * h], op0=ALU.mult, op1=ALU.add),
                        reads=['nst', 'decb', bD2key], writes=['nst'])
                if debug:
                    dump('hm', dbg['hm'][t0 + j * 128:t0 + (j + 1) * 128, :], hm[:, s2, :], ['hm%d' % s2])
                bH, bHkey = bk(S.bank())
                bHb = bH[:].bitcast(BF16)
                for c in range(8):
                    S.op('pe', lambda c=c: nc.tensor.transpose(out=bHb[:, c * 128:(c + 1) * 128],
                                                              in_=hm[:, s2, c * 128:(c + 1) * 128], identity=identb[:]),
                         reads=['hm%d' % s2, 'identb'], writes=[bHkey])
                evac_copy(hmT[:, :, js(j)], bHb[:, 0:1024].rearrange("p (c t) -> p c t", c=8), [bHkey], ['hmT%d' % j])
            wrelease()

            S.alias(mgk, qpk)
            for n in range(4):
                wt, wkey = wget('M%d' % n)
                wv = wt[:, 0:8192].rearrange("p (m c n) -> p m c n", m=4, c=8)
                for sub in range(2):
                    c = 2 * n + sub
                    cs = slice(sub * 128, (sub + 1) * 128)
                    b_gm, k_gm = bk(S.bank())
                    b_bm, k_bm = bk(S.bank())
                    mm8(b_gm[:, 0:TT], k_gm, lambda kc: wv[:, 0, kc, cs], lambda kc: hT[:, kc, :], hTk + [wkey])
                    mm8(b_bm[:, 0:TT], k_bm, lambda kc: wv[:, 2, kc, cs], lambda kc: hmT[:, kc, :], hmTk + [wkey])
                    t1, t1k = tmpbuf()
                    S.op('act', lambda: nc.scalar.activation(out=t1, in_=b_gm[:, 0:TT], func=AF.Tanh, scale=0.5),
                         reads=[k_gm], writes=[t1k])
                    S.op('dve', lambda: nc.vector.scalar_tensor_tensor(out=t1, in0=t1, scalar=1.0, in1=b_bm[:, 0:TT],
                                                                       op0=ALU.add, op1=ALU.mult),
                         reads=[t1k, k_bm], writes=[t1k])
                    b_gc, k_gc = bk(S.bank())
                    b_bc, k_bc = bk(S.bank())
                    mm8(b_gc[:, 0:TT], k_gc, lambda kc: wv[:, 1, kc, cs], lambda kc: hT[:, kc, :], hTk + [wkey])
                    mm8(b_bc[:, 0:TT], k_bc, lambda kc: wv[:, 3, kc, cs], lambda kc: usT[:, kc, :], usk + [wkey])
                    t2, t2k = tmpbuf()
                    S.op('act', lambda: nc.scalar.activation(out=t2, in_=b_gc[:, 0:TT], func=AF.Tanh, scale=0.5),
                         reads=[k_gc], writes=[t2k])
                    S.op('dve', lambda: nc.vector.scalar_tensor_tensor(out=t2, in0=t2, scalar=1.0, in1=b_bc[:, 0:TT],
                                                                       op0=ALU.add, op1=ALU.mult),
                         reads=[t2k, k_bc], writes=[t2k])
                    S.op('pool', lambda c=c: nc.gpsimd.tensor_tensor(out=mgT[:, c, :], in0=t1, in1=t2, op=ALU.add),
                         reads=[t1k, t2k], writes=['mg%d' % c])
                wrelease()

            if debug:
                dump('mg', dbg['mg'][:, t0:t0 + TT].rearrange("(c p) t -> p c t", p=128), mgT[:, :, :], mgk)
            wt, wkey = wget('WO')
            wv = wt[:, 0:8192].rearrange("p (c n) -> p c n", c=8)
            for j in range(NJ):
                for hf in range(2):
                    bo, bokey = bk(S.bank())
                    mm8(bo[:, :], bokey, lambda kc: mgT[:, kc, js(j)], lambda kc, hf=hf: wv[:, kc, hf * 512:(hf + 1) * 512],
                        mgk + [wkey])
                    S.op('dve', lambda hf=hf, bo=bo: nc.vector.scalar_tensor_tensor(
                        out=xt[:, j, hf * 512:(hf + 1) * 512], in0=bo[:, :], scalar=0.5, in1=xt[:, j, hf * 512:(hf + 1) * 512],
                        op0=ALU.mult, op1=ALU.add), reads=[bokey, 'xt%d' % j], writes=['xt%d' % j])
            wrelease()

            if debug:
                dump('x1', dbg['x1'][t0:t0 + TT, :].rearrange("(j p) d -> p j d", p=128), xt[:, :, :], ['xt%d' % j for j in range(NJ)])
            S.alias(hbk, hmTk)
            norm_T(PC_FFN)
            S.alias(hidk, qkvk)
            for f in range(6):
                wt, wkey = wget('F%d' % f)
                ncl = 4 if f < 5 else 2
                wv = wt[:, 0:2 * 8 * ncl * 128].rearrange("p (m c n) -> p m c n", m=2, c=8)
                for il in range(ncl):
                    n = 4 * f + il
                    cs = slice(il * 128, (il + 1) * 128)
                    b_g, k_g = bk(S.bank())
                    b_u, k_u = bk(S.bank())
                    mm8(b_g[:, 0:TT], k_g, lambda kc: wv[:, 0, kc, cs], lambda kc: hT[:, kc, :], hTk + [wkey])
                    mm8(b_u[:, 0:TT], k_u, lambda kc: wv[:, 1, kc, cs], lambda kc: hT[:, kc, :], hTk + [wkey])
                    t1, t1k = tmpbuf()
                    S.op('act', lambda: nc.scalar.activation(out=t1, in_=b_g[:, 0:TT], func=AF.Silu), reads=[k_g], writes=[t1k])
                    S.op('dve', lambda n=n: nc.vector.tensor_tensor(out=hid[:, n, :], in0=t1, in1=b_u[:, 0:TT], op=ALU.mult),
                         reads=[t1k, k_u], writes=['hid%d' % n])
                wrelease()
            for b in range(4):
                wt, wkey = wget('D%d' % b)
                wv = wt[:, 0:NFF * 256].rearrange("p (c n) -> p c n", c=NFF)
                for j in range(NJ):
                    bd, bdkey = bk(S.bank())
                    mm8(bd[:, 0:256], bdkey, lambda kc: hid[:, kc, js(j)], lambda kc: wv[:, kc, :], hidk + [wkey], n=NFF)
                    S.op('dve', lambda b=b, bd=bd: nc.vector.tensor_tensor(out=xt[:, j, b * 256:(b + 1) * 256], in0=bd[:, 0:256],
                                                                         in1=xt[:, j, b * 256:(b + 1) * 256], op=ALU.add),
                         reads=[bdkey, 'xt%d' % j], writes=['xt%d' % j])
                wrelease()

            if debug:
                dump('x2', dbg['x2'][t0:t0 + TT, :].rearrange("(j p) d -> p j d", p=128), xt[:, :, :], ['xt%d' % j for j in range(NJ)])
            S.alias(ppk, usk)
            for j in range(NJ):
                S.dma('sp', 'pld%d' % j, pt[:, j, :], p_in[t0 + j * 128:t0 + (j + 1) * 128, :], writes=['pt%d' % j])
            S.op('pool', lambda: nc.gpsimd.tensor_copy(out=pb[:, :, :], in_=pt[:, :, :]), reads=['pt%d' % j for j in range(NJ)], writes=['pb'])
            bp_, bpkey = bk(S.bank())
            bpb = bp_[:].bitcast(BF16)
            for c in range(2):
                for j in range(NJ):
                    o = c * 512 + j * 128
                    S.op('pe', lambda c=c, j=j, o=o: nc.tensor.transpose(out=bpb[:, o:o + 128], in_=pb[:, j, c * 128:(c + 1) * 128],
                                                                      identity=identb[:]),
                         reads=['pb', 'identb'], writes=[bpkey])
            for c in range(2):
                evac_copy(pT[:, c, :], bpb[:, c * 512:c * 512 + TT], [bpkey], ['pT'])
            norm_T(PC_PLE)
            wtg, wgkey = wget('PG')
            wtp, wpkey = wget('PP')
            wg = wtg[:, 0:8192].rearrange("p (c n) -> p c n", c=8)
            wp = wtp[:, 0:2048].rearrange("p (c n) -> p c n", c=2)
            for j in range(NJ):
                for hf in range(2):
                    b_g, k_g = bk(S.bank())
                    b_p, k_p = bk(S.bank())
                    mm8(b_g[:, :], k_g, lambda kc: hT[:, kc, js(j)], lambda kc, hf=hf: wg[:, kc, hf * 512:(hf + 1) * 512],
                        hTk + [wgkey])
                    mm8(b_p[:, :], k_p, lambda kc: pT[:, kc, js(j)], lambda kc, hf=hf: wp[:, kc, hf * 512:(hf + 1) * 512],
                        ['pT', wpkey], n=2)
                    t1, t1k = tmpbuf()
                    S.op('act', lambda: nc.scalar.activation(out=t1, in_=b_g[:, :], func=AF.Tanh, scale=0.5),
                         reads=[k_g], writes=[t1k])
                    S.op('dve', lambda: nc.vector.scalar_tensor_tensor(out=t1, in0=t1, scalar=1.0, in1=b_p[:, :], op0=ALU.add,
                                                                       op1=ALU.mult), reads=[t1k, k_p], writes=[t1k])
                    S.op('dve', lambda hf=hf: nc.vector.scalar_tensor_tensor(
                        out=xt[:, j, hf * 512:(hf + 1) * 512], in0=t1, scalar=0.5, in1=xt[:, j, hf * 512:(hf + 1) * 512],
                        op0=ALU.mult, op1=ALU.add), reads=[t1k, 'xt%d' % j], writes=['xt%d' % j])
            wrelease()
            wrelease()

            if debug:
                dump('x3', dbg['x3'][t0:t0 + TT, :].rearrange("(j p) d -> p j d", p=128), xt[:, :, :], ['xt%d' % j for j in range(NJ)])
            for j in range(NJ):
                S.op('act', lambda j=j: nc.scalar.activation(out=sqj[:], in_=xt[:, j, :], func=AF.Square,
                                                           accum_out=ss[:, j:j + 1]), reads=['xt%d' % j], writes=['ssa%d' % j])
            S.op('dve', lambda: nc.vector.tensor_scalar(out=ss[:, NJ:2 * NJ], in0=ss[:, 0:NJ], scalar1=1.0 / D, scalar2=EPS,
                                                        op0=ALU.mult, op1=ALU.add), reads=['ssa%d' % j for j in range(NJ)], writes=['ss'])
            S.op('pool', lambda: nc.gpsimd.tensor_tensor(out=ss[:, 2 * NJ:3 * NJ], in0=ss[:, NJ:2 * NJ], in1=mh4[:, 0:NJ],
                                                         op=ALU.pow), reads=['ss', 'mh4'], writes=['ss'])
            for j in range(NJ):
                s2 = j % 2
                S.op('dve', lambda j=j, s2=s2: nc.vector.scalar_tensor_tensor(
                    out=tho[:, s2, :], in0=xt[:, j, :], scalar=ss[:, 2 * NJ + j:2 * NJ + j + 1], in1=fg[:], op0=ALU.mult,
                    op1=ALU.mult), reads=['xt%d' % j, 'ss', 'fg'], writes=['tho%d' % s2])
                S.dma('sp', 'o%d' % s2, y[t0 + j * 128:t0 + (j + 1) * 128, :], tho[:, s2, :], reads=['tho%d' % s2],
                      writes=['y%d_%d' % (T, j)])
        S.finish('sp', ['y%d_%d' % (T, j) for T in range(NT) for j in range(NJ)] + ['dbgout%d' % (i + 1) for i in range(dcnt['i'])])
        print("sched: waits=%d pe=%d act=%d dve=%d pool=%d" % (S.nwaits, S.cnt['pe'], S.cnt['act'], S.cnt['dve'], S.cnt['pool']))
    return nc


_PROGRAM = {}


def kernel(**inputs):
    f32 = lambda a: np.ascontiguousarray(np.asarray(a, dtype=np.float32))
    x = f32(inputs['x'])
    p = f32(inputs['p'])[0]
    shared = {
        'w_in': f32(inputs['w_in'][0]), 'w_branch_m': f32(inputs['w_branch_m'][0]),
        'w_branch_c': f32(inputs['w_branch_c'][0]), 'w_out': f32(inputs['w_out'][0]),
        'w_ffn_gate': f32(inputs['w_ffn_gate'][0]), 'w_ffn_up': f32(inputs['w_ffn_up'][0]),
        'w_ffn_down': f32(inputs['w_ffn_down'][0]), 'w_ple_gate': f32(inputs['w_ple_gate'][0]),
        'w_ple_proj': f32(inputs['w_ple_proj'][0]),
        'norm_mix_g': f32(inputs['norm_mix_g'][0]), 'mh_norm_g': f32(inputs['mh_norm_g'][0]),
        'conf_conv_b': f32(inputs['conf_conv_b'][0]), 'conf_ln_g': f32(inputs['conf_ln_g'][0]),
        'conf_ln_b': f32(inputs['conf_ln_b'][0]), 'norm_ffn_g': f32(inputs['norm_ffn_g'][0]),
        'norm_ple_g': f32(inputs['norm_ple_g'][0]), 'final_g': f32(inputs['final_g']),
        'b_if': f32(inputs['b_if'][0]), 'conv_qk_w': f32(inputs['conv_qk_w'][0]),
        'conv_qk_b': f32(inputs['conv_qk_b'][0]), 'conf_conv_w': f32(inputs['conf_conv_w'][0]),
    }
    if 'nc' not in _PROGRAM:
        _PROGRAM['nc'] = build_program()
    nc = _PROGRAM['nc']
    in_maps = []
    for b in range(NCORES):
        m = dict(shared)
        m['x'] = np.ascontiguousarray(x[b])
        m['p'] = np.ascontiguousarray(p[b])
        in_maps.append(m)
    res = run_bass_kernel_spmd(nc, in_maps, core_ids=list(range(NCORES)))
    out = np.stack([np.asarray(res.results[b]['y'], dtype=np.float32) for b in range(NCORES)], axis=0)
    return out
```

```python
import contextlib
import numpy as np
import concourse.bass as bass
import concourse.mybir as mybir
from concourse.bass_utils import run_bass_kernel_spmd

F32 = mybir.dt.float32
BF16 = mybir.dt.bfloat16
AF = mybir.ActivationFunctionType
ALU = mybir.AluOpType
AX = mybir.AxisListType

SEQ = 2048
D = 1024
NIN = 8200
DFF = 2816
NFF = DFF // 128
PLE = 256
TT = 512
NJ = TT // 128
NT = SEQ // TT
EPS = 1e-6
HQ = 3
HC = 30
NCORES = 8


class Sched:
    def __init__(self, nc, es):
        self.nc = nc
        self.es = es
        self.eng = {'pe': nc.tensor, 'act': nc.scalar, 'dve': nc.vector, 'pool': nc.gpsimd, 'sp': nc.sync}
        self.sems = {}
        self.cnt = {}
        for e in ['pe', 'act', 'dve', 'pool']:
            self.sems[e] = es.enter_context(nc.semaphore('s_' + e))
            self.cnt[e] = 0
        self.clock = {e: {} for e in self.eng}
        self.snap = {}
        self.lastw = {}
        self.readers = {}
        self.nwaits = 0
        self.nbank = 0

    def bank(self):
        b = self.nbank % 6
        self.nbank += 1
        return b

    def _deps(self, reads, writes):
        deps = {}

        def add(k, v):
            if deps.get(k, 0) < v:
                deps[k] = v
        for r in reads:
            if r in self.lastw:
                add(*self.lastw[r])
        for w in writes:
            if w in self.lastw:
                add(*self.lastw[w])
            for k, v in self.readers.get(w, {}).items():
                add(k, v)
        return deps

    def _wait(self, e, deps):
        ck = self.clock[e]
        for k, v in sorted(deps.items(), key=lambda kv: -kv[1]):
            if e == 'pe' and k == 'pe':
                continue
            if ck.get(k, 0) >= v:
                continue
            mult = 16 if k.startswith('dq') else 1
            self.eng[e].wait_ge(self.sems[k], v * mult)
            self.nwaits += 1
            for kk, vv in self.snap.get((k, v), {}).items():
                if ck.get(kk, 0) < vv:
                    ck[kk] = vv
            ck[k] = max(ck.get(k, 0), v)

    def _record(self, key, val, reads, writes):
        for r in reads:
            d = self.readers.setdefault(r, {})
            if d.get(key, 0) < val:
                d[key] = val
        for w in writes:
            self.lastw[w] = (key, val)
            self.readers[w] = {}

    def op(self, e, fn, reads=(), writes=()):
        self._wait(e, self._deps(reads, writes))
        ins = fn()
        self.cnt[e] += 1
        v = self.cnt[e]
        ins.then_inc(self.sems[e], 1)
        self.snap[(e, v)] = dict(self.clock[e])
        self._record(e, v, reads, writes)
        return ins

    def dma(self, q, semkey, out, in_, reads=(), writes=(), **kw):
        semkey = 'dq' + semkey
        if semkey not in self.sems:
            self.sems[semkey] = self.es.enter_context(self.nc.semaphore('d_' + semkey))
            self.cnt[semkey] = 0
        deps = self._deps(reads, writes)
        if self.cnt[semkey] > 0:
            if deps.get(semkey, 0) < self.cnt[semkey]:
                deps[semkey] = self.cnt[semkey]
        self._wait(q, deps)
        ins = self.eng[q].dma_start(out=out, in_=in_, **kw)
        self.cnt[semkey] += 1
        v = self.cnt[semkey]
        ins.then_inc(self.sems[semkey], 16)
        self.snap[(semkey, v)] = dict(self.clock[q])
        self._record(semkey, v, reads, writes)
        return ins

    def alias(self, new_keys, old_keys):
        deps = {}
        for k in old_keys:
            if k in self.lastw:
                kk, vv = self.lastw[k]
                if deps.get(kk, 0) < vv:
                    deps[kk] = vv
            for kk, vv in self.readers.get(k, {}).items():
                if deps.get(kk, 0) < vv:
                    deps[kk] = vv
        for nk in new_keys:
            d = self.readers.setdefault(nk, {})
            for kk, vv in deps.items():
                if d.get(kk, 0) < vv:
                    d[kk] = vv

    def finish(self, e, keys):
        deps = {}
        for k in keys:
            if k in self.lastw:
                kk, vv = self.lastw[k]
                if deps.get(kk, 0) < vv:
                    deps[kk] = vv
        self._wait(e, deps)


def weight_tiles():
    t = []
    for hf in range(2):
        t.append(('A%d' % hf, [('w_in', 4104 + 512 * hf, 512, 8), ('w_in', 5128 + 512 * hf, 512, 8)]))
    t.append(('Q', [('w_in', 0, 1024, 8)]))
    t.append(('K', [('w_in', 1024, 1024, 8)]))
    t.append(('V', [('w_in', 2048, 1024, 8)]))
    t.append(('O', [('w_in', 3072, 1024, 8)]))
    for n in range(4):
        t.append(('M%d' % n, [('w_in', 6152 + 256 * n, 256, 8), ('w_in', 7176 + 256 * n, 256, 8),
                              ('w_branch_m', 256 * n, 256, 8), ('w_branch_c', 256 * n, 256, 8)]))
    t.append(('WO', [('w_out', 0, 1024, 8)]))
    for f in range(6):
        nc_ = 512 if f < 5 else 256
        t.append(('F%d' % f, [('w_ffn_gate', 512 * f, nc_, 8), ('w_ffn_up', 512 * f, nc_, 8)]))
    for b in range(4):
        t.append(('D%d' % b, [('w_ffn_down', 256 * b, 256, NFF)]))
    t.append(('PG', [('w_ple_gate', 0, 1024, 8)]))
    t.append(('PP', [('w_ple_proj', 0, 1024, 2)]))
    return t


def build_program(debug=False):
    nc = bass.Bass("TRN2", target_bir_lowering=False)
    din = {}

    def dram_in(name, shape):
        din[name] = nc.dram_tensor(name, shape, F32, kind="ExternalInput").ap()
        return din[name]

    x = dram_in("x", [SEQ, D])
    p_in = dram_in("p", [SEQ, PLE])
    dram_in("w_in", [D, NIN])
    dram_in("w_branch_m", [D, D])
    dram_in("w_branch_c", [D, D])
    dram_in("w_out", [D, D])
    dram_in("w_ffn_gate", [D, DFF])
    dram_in("w_ffn_up", [D, DFF])
    dram_in("w_ffn_down", [DFF, D])
    dram_in("w_ple_gate", [D, D])
    dram_in("w_ple_proj", [PLE, D])
    for nm in ["norm_mix_g", "mh_norm_g", "conf_conv_b", "conf_ln_g", "conf_ln_b", "norm_ffn_g", "norm_ple_g", "final_g"]:
        dram_in(nm, [D])
    dram_in("b_if", [8])
    dram_in("conv_qk_w", [4, 2 * D])
    dram_in("conv_qk_b", [2 * D])
    dram_in("conf_conv_w", [31, D])
    y = nc.dram_tensor("y", [SEQ, D], F32, kind="ExternalOutput").ap()

    dbg = {}
    if debug:
        for nm, shp, dt in [('x1', [SEQ, D], F32), ('x2', [SEQ, D], F32), ('x3', [SEQ, D], F32), ('hm', [SEQ, D], BF16),
                            ('us', [D, SEQ], BF16), ('q', [D, SEQ], BF16), ('k', [D, SEQ], BF16), ('v', [SEQ, D], BF16),
                            ('hT', [D, SEQ], BF16), ('mg', [D, SEQ], BF16), ('gates', [SEQ, 8], F32),
                            ('gsm', [NT * 128, 6 * 4 * NJ], F32), ('misc', [128, 2560], F32), ('ub', [D, SEQ], BF16), ('uc', [D, SEQ], F32)]:
            dbg[nm] = nc.dram_tensor('dbg_' + nm, shp, dt, kind="ExternalOutput").ap()
    tiles = weight_tiles()
    scr = {}
    for name, pieces in tiles:
        X = sum(ncols * kc for (_, _, ncols, kc) in pieces)
        scr[name] = nc.dram_tensor("scr_" + name, [128, X], BF16).ap()

    with contextlib.ExitStack() as es:
        S = Sched(nc, es)

        def sb(name, shape, dt):
            return es.enter_context(nc.sbuf_tensor(name, shape, dt))

        banks = [es.enter_context(nc.psum_tensor("bank%d" % i, [128, 512], F32)) for i in range(8)]

        def bk(i):
            return banks[i], 'ps%d' % i

        xt = sb("xt", [128, NJ, D], F32)
        hbm = sb("hbm", [128, NJ * D], BF16)
        hb = hbm[:].rearrange("p (j d) -> p j d", j=NJ)
        hmT = hbm[:].rearrange("p (c t) -> p c t", c=8)
        hT = sb("hT", [128, 8, TT], BF16)
        wsl = [sb("wsl%d" % i, [128, 8192], BF16) for i in range(3)]
        ub = sb("ub", [128, 8, HC + TT], BF16)
        XR = sb("XR", [128, 12288], BF16)
        uc = XR[:, 0:8192].bitcast(F32).rearrange("p (i t) -> p i t", i=8)
        ucb = XR[:, 8192:8192 + 2 * TT].rearrange("p (s t) -> p s t", s=2)
        usq = XR[:, 8192 + 2 * TT:8192 + 4 * TT].rearrange("p (s t) -> p s t", s=2)
        qT = XR[:, 0:4096].rearrange("p (c t) -> p c t", c=8)
        kT = XR[:, 4096:8192].rearrange("p (c t) -> p c t", c=8)
        vb = XR[:, 8192:12288].rearrange("p (j d) -> p j d", j=NJ)
        hid = XR[:, 0:NFF * TT].rearrange("p (n t) -> p n t", n=NFF)
        usp = sb("usp", [128, 4096], BF16)
        usT = usp[:].rearrange("p (c t) -> p c t", c=8)
        pt = usp[:, 0:2048].bitcast(F32).rearrange("p (j d) -> p j d", j=NJ)
        pb = usp[:, 2048:3072].rearrange("p (j d) -> p j d", j=NJ)
        pT = usp[:, 3072:4096].rearrange("p (c t) -> p c t", c=2)
        YR = sb("YR", [128, 8 * (HQ + TT)], BF16)
        qp = YR[:].rearrange("p (c t) -> p c t", c=8)
        mgT = YR[:, 0:4096].rearrange("p (c t) -> p c t", c=8)
        qhalo = sb("qhalo", [128, 16, HQ], BF16)
        kw = sb("kw", [128, 2, 4, 256], BF16)
        PTt = sb("PTt", [128, 2, 4, 128], BF16)
        Cd = sb("Cd", [128, 4, 2, 256], BF16)
        ndb = sb("ndb", [128, 8], BF16)
        Cst = sb("Cst", [128, 4, 2, 256], F32)
        nst = sb("nst", [128, 8], F32)
        mprev = sb("mprev", [128, 4], F32)
        hm = sb("hm", [128, 2, D], BF16)
        tmp = sb("tmp", [128, 4, 512], F32)
        tho = sb("tho", [128, 2, D], F32)
        gh = sb("gh", [128, D], F32)
        fg = sb("fg", [128, D], F32)
        identf = sb("identf", [128, 128], F32)
        identb = sb("identb", [128, 128], BF16)
        tri = sb("tri", [128, 128], F32)
        onesf = sb("onesf", [128, 128], F32)
        mask16 = sb("mask16", [128, 128], F32)
        onesm = sb("onesm", [128, 128], BF16)
        onescol = sb("onescol", [128, 2], BF16)
        mh4 = sb("mh4", [128, 16], F32)
        pcol = sb("pcol", [128, 384], F32)
        rows = [sb("rows%d" % i, [128, 128], F32) for i in range(4)]
        dg = sb("dg", [128, 16, 128], BF16)
        wif = sb("wif", [128, 8, 8], BF16)
        ifb = sb("ifb", [128, NJ, 8], F32)
        sqj = sb("sqj", [128, D], BF16)
        ss = sb("ss", [128, 4 * NJ], F32)
        gsb = sb("gsb", [128, NJ, 8], F32)
        gt = sb("gt", [128, 12, 4 * NJ], F32)
        bsb = sb("bsb", [128, 8 * NJ], F32)
        ab = sb("ab", [128, 4 * NJ], F32)
        amx = sb("amx", [128, 1], F32)
        dgm = sb("dgm", [128, 4 * NJ], F32)
        amaxb = sb("amaxb", [128, 4 * NJ], F32)
        Mb = sb("Mb", [128, 4 * NJ], F32)
        dmb = sb("dmb", [128, 4 * NJ], F32)
        wkb = sb("wkb", [128, 4 * NJ], F32)
        decb = sb("decb", [128, 4 * NJ], F32)
        dec16b = sb("dec16b", [128, 4 * NJ], F32)
        thrb = sb("thrb", [128, 4 * NJ], F32)
        lnst = sb("lnst", [128, 4, TT], F32)
        osm = sb("osm", [128, 2, 24], F32)

        PC_MIX, PC_FFN, PC_PLE, PC_QKB, PC_CCB, PC_LNG, PC_LNB = 0, 8, 16, 24, 40, 48, 56
        PC_QKW = 64
        PC_CW0 = 128
        PC_CW1 = 256

        def cw_col(k, i):
            return (PC_CW0 + k * 8 + i) if k < 16 else (PC_CW1 + (k - 16) * 8 + i)

        scr_keys = {}
        cast_state = {'next': 0}
        LOOKAHEAD = 6

        def cast_next():
            i = cast_state['next']
            if i >= len(tiles):
                return
            cast_state['next'] += 1
            name, pieces = tiles[i]
            off = 0
            for pi, (src, c0, ncols, kc) in enumerate(pieces):
                key = 'scr_%s_%d' % (name, pi)
                dst = scr[name][:, off:off + kc * ncols].rearrange("p (c n) -> p c n", c=kc)
                srcap = din[src][:, c0:c0 + ncols].rearrange("(c p) n -> p c n", p=128)
                S.dma('pool', 'cast_%s_%d' % (name, pi), dst, srcap, writes=[key])
                off += kc * ncols

        for name, pieces in tiles:
            scr_keys[name] = ['scr_%s_%d' % (name, pi) for pi in range(len(pieces))]
        S.dma('pool', 'wif', wif[:], din['w_in'][:, 4096:4104].rearrange("(c p) n -> p c n", p=128), writes=['wif'])
        for _ in range(LOOKAHEAD):
            cast_next()

        order = [name for name, _ in tiles] * NT
        wstate = {'next': 0, 'cur': {}}
        tile_X = {name: sum(ncols * kc for (_, _, ncols, kc) in pieces) for name, pieces in tiles}

        def wissue():
            i = wstate['next']
            if i >= len(order):
                return
            name = order[i]
            slot = i % 3
            S.dma('sp', 'w%d' % slot, wsl[slot][:, 0:tile_X[name]], scr[name][:, :],
                  reads=scr_keys[name], writes=['w%d' % slot])
            wstate['next'] += 1

        def wget(name):
            i = wstate['cur'].get('i', -1) + 1
            assert order[i] == name, (order[i], name)
            wstate['cur']['i'] = i
            slot = i % 3
            if i < len(tiles):
                cast_next()
            return wsl[slot], 'w%d' % slot

        def wrelease():
            wissue()

        S.op('pool', lambda: nc.gpsimd.memset(identf[:], 1.0), writes=['identf'])
        S.op('pool', lambda: nc.gpsimd.affine_select(out=identf[:], in_=identf[:], pattern=[[-1, 128]], base=0,
                                                      channel_multiplier=1, compare_op=ALU.is_equal, fill=0.0),
             reads=['identf'], writes=['identf'])
        S.op('pool', lambda: nc.gpsimd.memset(tri[:], 1.0), writes=['tri'])
        S.op('pool', lambda: nc.gpsimd.affine_select(out=tri[:], in_=tri[:], pattern=[[1, 128]], base=0,
                                                      channel_multiplier=-1, compare_op=ALU.is_ge, fill=0.0),
             reads=['tri'], writes=['tri'])
        S.op('pool', lambda: nc.gpsimd.memset(onesf[:], 1.0), writes=['onesf'])
        S.op('pool', lambda: nc.gpsimd.memset(onesm[:], 1.0 / D), writes=['onesm'])
        S.op('pool', lambda: nc.gpsimd.memset(onescol[:], 1.0), writes=['onescol'])
        S.op('pool', lambda: nc.gpsimd.memset(mh4[:], -0.5), writes=['mh4'])
        S.op('dve', lambda: nc.vector.tensor_copy(out=identb[:], in_=identf[:]), reads=['identf'], writes=['identb'])
        S.op('dve', lambda: nc.vector.tensor_scalar(out=mask16[:], in0=tri[:], scalar1=1.0 / 16, scalar2=None,
                                                    op0=ALU.mult), reads=['tri'], writes=['mask16'])
        S.op('pool', lambda: nc.gpsimd.memset(Cst[:], 0.0), writes=['C'])
        S.op('pool', lambda: nc.gpsimd.memset(nst[:], 0.0), writes=['nst'])
        S.op('pool', lambda: nc.gpsimd.memset(mprev[:], 0.0), writes=['mprev'])
        S.op('pool', lambda: nc.gpsimd.memset(ub[:], 0.0), writes=['ub%d' % i for i in range(8)])
        S.op('pool', lambda: nc.gpsimd.memset(qhalo[:], 0.0), writes=['qhalo'])
        S.dma('sp', 'c_gh', gh[:], din['mh_norm_g'].partition_broadcast(128), writes=['gh'])
        S.dma('sp', 'c_fg', fg[:], din['final_g'].partition_broadcast(128), writes=['fg'])
        for j in range(NJ):
            S.dma('sp', 'c_ifb%d' % j, ifb[:, j, :], din['b_if'].partition_broadcast(128), writes=['ifb'])
        S.op('dve', lambda: nc.vector.tensor_scalar(out=gh[:], in0=gh[:], scalar1=0.5, scalar2=None, op0=ALU.mult),
             reads=['gh'], writes=['gh'])
        S.op('pool', lambda: nc.gpsimd.memset(rows[0][:], 0.0), writes=['rows0'])
        rowspec = [("norm_mix_g", PC_MIX), ("norm_ffn_g", PC_FFN), ("norm_ple_g", PC_PLE), ("conf_conv_b", PC_CCB),
                   ("conf_ln_g", PC_LNG), ("conf_ln_b", PC_LNB)]
        for nm, base in rowspec:
            S.dma('sp', 'c_' + nm, rows[0][base:base + 8, :], din[nm].rearrange("(i p) -> i p", p=128),
                  reads=['rows0'], writes=['rows0_' + nm])
        S.dma('sp', 'c_qkb', rows[0][PC_QKB:PC_QKB + 16, :], din['conv_qk_b'].rearrange("(i p) -> i p", p=128),
              reads=['rows0'], writes=['rows0_qkb'])
        S.dma('sp', 'c_qkw', rows[1][0:64, :], din['conv_qk_w'].rearrange("k (i p) -> (k i) p", p=128), writes=['rows1'])
        S.dma('sp', 'c_cw0', rows[2][0:128, :], din['conf_conv_w'][0:16, :].rearrange("k (i p) -> (k i) p", p=128),
              writes=['rows2'])
        S.dma('sp', 'c_cw1', rows[3][0:120, :], din['conf_conv_w'][16:31, :].rearrange("k (i p) -> (k i) p", p=128),
              writes=['rows3'])
        r0keys = ['rows0_' + nm for nm, _ in rowspec] + ['rows0_qkb']
        for gi, (R, base, keys) in enumerate([(64, 0, r0keys), (64, PC_QKW, ['rows1']), (128, PC_CW0, ['rows2']),
                                              (120, PC_CW1, ['rows3'])]):
            bt, bkey = bk(S.bank())
            S.op('pe', lambda gi=gi, R=R, bt=bt: nc.tensor.matmul(bt[:, 0:R], lhsT=rows[gi][0:R, :],
                                                               rhs=identf[0:R, 0:R], start=True, stop=True),
                 reads=keys + ['identf'], writes=[bkey])
            S.op('dve', lambda R=R, base=base, bt=bt: nc.vector.tensor_copy(out=pcol[:, base:base + R], in_=bt[:, 0:R]),
                 reads=[bkey], writes=['pcol'])

        for j in range(NJ):
            S.dma('sp', 'x%d' % j, xt[:, j, :], x[j * 128:(j + 1) * 128, :], writes=['xt%d' % j])
        for _ in range(3):
            wissue()

        tmpi = {'i': 0}

        def tmpbuf():
            i = tmpi['i'] % 4
            tmpi['i'] += 1
            return tmp[:, i, :], 'tmp%d' % i

        dgi = {'i': 0}

        def diag(col, scale):
            i = dgi['i'] % 16
            n = dgi['i']
            dgi['i'] += 1
            if n % 3 == 2:
                S.op('pool', lambda: nc.gpsimd.tensor_scalar(out=dg[:, i, :], in0=identf[:], scalar1=pcol[:, col:col + 1],
                                                             scalar2=scale, op0=ALU.mult, op1=ALU.mult),
                     reads=['identf', 'pcol'], writes=['dg%d' % i])
            else:
                S.op('dve', lambda: nc.vector.tensor_scalar(out=dg[:, i, :], in0=identf[:], scalar1=pcol[:, col:col + 1],
                                                            scalar2=scale, op0=ALU.mult, op1=ALU.mult),
                     reads=['identf', 'pcol'], writes=['dg%d' % i])
            return dg[:, i, :], 'dg%d' % i

        evi = {'i': 0}

        def evac_copy(out, in_, reads, writes):
            evi['i'] += 1
            if evi['i'] % 2:
                S.op('act', lambda: nc.scalar.copy(out=out, in_=in_), reads=reads, writes=writes)
            else:
                S.op('dve', lambda: nc.vector.tensor_copy(out=out, in_=in_), reads=reads, writes=writes)

        def js(j):
            return slice(j * 128, (j + 1) * 128)

        def norm_T(gbase):
            for j in range(NJ):
                S.op('act', lambda j=j: nc.scalar.activation(out=sqj[:], in_=xt[:, j, :], func=AF.Square,
                                                           accum_out=ss[:, j:j + 1]),
                     reads=['xt%d' % j], writes=['ssa%d' % j])
            S.op('dve', lambda: nc.vector.tensor_scalar(out=ss[:, NJ:2 * NJ], in0=ss[:, 0:NJ], scalar1=1.0 / D, scalar2=EPS,
                                                        op0=ALU.mult, op1=ALU.add), reads=['ssa%d' % j for j in range(NJ)], writes=['ss'])
            S.op('pool', lambda: nc.gpsimd.tensor_tensor(out=ss[:, 2 * NJ:3 * NJ], in0=ss[:, NJ:2 * NJ], in1=mh4[:, 0:NJ],
                                                         op=ALU.pow), reads=['ss', 'mh4'], writes=['ss'])
            for j in range(NJ):
                S.op('dve', lambda j=j: nc.vector.tensor_scalar(out=hb[:, j, :], in0=xt[:, j, :],
                                                              scalar1=ss[:, 2 * NJ + j:2 * NJ + j + 1], scalar2=None,
                                                              op0=ALU.mult),
                     reads=['xt%d' % j, 'ss'], writes=['hb%d' % j])
            for c in range(8):
                if c % 2 == 0:
                    bt, bkey = bk(S.bank())
                    btb = bt[:].bitcast(BF16)
                for j in range(NJ):
                    o = (c % 2) * 512 + j * 128
                    S.op('pe', lambda c=c, j=j, o=o, btb=btb: nc.tensor.transpose(out=btb[:, o:o + 128],
                                                                               in_=hb[:, j, c * 128:(c + 1) * 128],
                                                                               identity=identb[:]),
                         reads=['hb%d' % j, 'identb'], writes=[bkey])
                o = (c % 2) * 512
                if c % 2 == 0:
                    S.op('dve', lambda c=c, o=o, btb=btb: nc.vector.tensor_scalar(out=hT[:, c, :], in0=btb[:, o:o + TT],
                                                                               scalar1=pcol[:, gbase + c:gbase + c + 1],
                                                                               scalar2=None, op0=ALU.mult),
                         reads=[bkey, 'pcol'], writes=['hT%d' % c])
                else:
                    S.op('act', lambda c=c, o=o, btb=btb: nc.scalar.activation(out=hT[:, c, :], in_=btb[:, o:o + TT],
                                                                            func=AF.Identity,
                                                                            scale=pcol[:, gbase + c:gbase + c + 1]),
                         reads=[bkey, 'pcol'], writes=['hT%d' % c])

        def mm8(bt, bkey, lhs_fn, rhs_fn, lkeys, n=8):
            for kc in range(n):
                S.op('pe', lambda kc=kc: nc.tensor.matmul(bt, lhsT=lhs_fn(kc), rhs=rhs_fn(kc), start=(kc == 0),
                                                         stop=(kc == n - 1)),
                     reads=lkeys, writes=[bkey])

        hbk = ['hb%d' % j for j in range(NJ)]
        hTk = ['hT%d' % c for c in range(8)]
        hmTk = ['hmT%d' % j for j in range(NJ)]
        uck = ['uc%d' % i for i in range(8)] + ['ucb0', 'ucb1', 'usq0', 'usq1']
        qkvk = ['qT%d' % c for c in range(8)] + ['kT%d' % c for c in range(8)] + ['vb%d' % j for j in range(NJ)]
        hidk = ['hid%d' % n for n in range(NFF)]
        usk = ['us%d' % i for i in range(8)]
        ppk = ['pt%d' % j for j in range(NJ)] + ['pb', 'pT']
        qpk = ['qp%d' % c for c in range(8)]
        mgk = ['mg%d' % c for c in range(8)]
        ubk = ['ub%d' % i for i in range(8)]

        dcnt = {'i': 0}

        def dump(name, dst, src, reads):
            if not debug:
                return
            dcnt['i'] += 1
            S.dma('sp', 'dbg%d' % (dcnt['i'] % 6), dst, src, reads=reads, writes=['dbgout%d' % dcnt['i']])

        for T in range(NT):
            t0 = T * TT
            if T > 0:
                for j in range(NJ):
                    S.dma('sp', 'x%d' % j, xt[:, j, :], x[t0 + j * 128:t0 + (j + 1) * 128, :], writes=['xt%d' % j])

            S.alias(hbk, hmTk)
            norm_T(PC_MIX)

            if debug:
                dump('hT', dbg['hT'][:, t0:t0 + TT].rearrange("(c p) t -> p c t", p=128), hT[:, :, :], hTk)
            gbt, gbkey = bk(S.bank())
            for j in range(NJ):
                mm8(gbt[:, j * 8:(j + 1) * 8], gbkey, lambda kc, j=j: hT[:, kc, js(j)], lambda kc: wif[:, kc, :],
                    hTk + ['wif'])
            S.op('dve', lambda: nc.vector.tensor_tensor(out=gsb[:].rearrange("p j e -> p (j e)"), in0=gbt[:, 0:8 * NJ],
                                                        in1=ifb[:].rearrange("p j e -> p (j e)"), op=ALU.add),
                 reads=[gbkey, 'ifb'], writes=['gsb'])

            if debug:
                dump('gates', dbg['gates'][t0:t0 + TT, :].rearrange("(j p) e -> p j e", p=128), gsb[:, :, :], ['gsb'])
            S.alias(uck, hidk)
            S.op('pool', lambda: nc.gpsimd.tensor_copy(out=ub[:, :, 0:HC], in_=ub[:, :, TT:TT + HC]),
                 reads=ubk, writes=ubk)
            for hf in range(2):
                wt, wkey = wget('A%d' % hf)
                wv = wt[:, 0:8192].rearrange("p (m c n) -> p m c n", m=2, c=8)
                for il in range(4):
                    i = hf * 4 + il
                    ba, bakey = bk(S.bank())
                    bg, bgkey = bk(S.bank())
                    mm8(ba[:, 0:TT], bakey, lambda kc, il=il: wv[:, 0, kc, il * 128:(il + 1) * 128],
                        lambda kc: hT[:, kc, :], hTk + [wkey])
                    mm8(bg[:, 0:TT], bgkey, lambda kc, il=il: wv[:, 1, kc, il * 128:(il + 1) * 128],
                        lambda kc: hT[:, kc, :], hTk + [wkey])
                    tb, tkey = tmpbuf()
                    S.op('act', lambda: nc.scalar.activation(out=tb, in_=bg[:, 0:TT], func=AF.Tanh, scale=0.5),
                         reads=[bgkey], writes=[tkey])
                    S.op('dve', lambda i=i: nc.vector.scalar_tensor_tensor(out=ub[:, i, HC:HC + TT], in0=tb, scalar=1.0,
                                                                         in1=ba[:, 0:TT], op0=ALU.add, op1=ALU.mult),
                         reads=[tkey, bakey], writes=['ub%d' % i])
                wrelease()
            Mbank, Mkey = bk(6)
            Qbank, Qkey = bk(7)
            for i in range(8):
                bc, bckey = bk(S.bank())
                for k in range(31):
                    dgt, dgkey = diag(cw_col(k, i), 0.5)
                    S.op('pe', lambda k=k, i=i, dgt=dgt: nc.tensor.matmul(bc[:, 0:TT], lhsT=dgt, rhs=ub[:, i, k:k + TT],
                                                                       start=(k == 0), stop=(k == 30)),
                         reads=[dgkey, 'ub%d' % i], writes=[bckey])
                S.op('act', lambda i=i: nc.scalar.activation(out=uc[:, i, :], in_=bc[:, 0:TT], func=AF.Identity,
                                                           bias=pcol[:, PC_CCB + i:PC_CCB + i + 1]),
                     reads=[bckey, 'pcol'], writes=['uc%d' % i])
                s2 = i % 2
                S.op('dve', lambda i=i, s2=s2: nc.vector.tensor_copy(out=ucb[:, s2, :], in_=uc[:, i, :]),
                     reads=['uc%d' % i], writes=['ucb%d' % s2])
                S.op('act', lambda i=i, s2=s2: nc.scalar.activation(out=usq[:, s2, :], in_=uc[:, i, :], func=AF.Square),
                     reads=['uc%d' % i], writes=['usq%d' % s2])
                S.op('pe', lambda i=i, s2=s2: nc.tensor.matmul(Mbank[:, 0:TT], lhsT=onesm[:], rhs=ucb[:, s2, :],
                                                             start=(i == 0), stop=(i == 7)),
                     reads=['onesm', 'ucb%d' % s2], writes=[Mkey])
                S.op('pe', lambda i=i, s2=s2: nc.tensor.matmul(Qbank[:, 0:TT], lhsT=onesm[:], rhs=usq[:, s2, :],
                                                             start=(i == 0), stop=(i == 7)),
                     reads=['onesm', 'usq%d' % s2], writes=[Qkey])

            if debug:
                dump('ub', dbg['ub'][:, t0:t0 + TT].rearrange("(c p) t -> p c t", p=128), ub[:, :, HC:HC + TT], ubk)
                dump('uc', dbg['uc'][:, t0:t0 + TT].rearrange("(c p) t -> p c t", p=128), uc[:, :, :], uck)
            msq, lnv, rstd, mr = lnst[:, 0, :], lnst[:, 1, :], lnst[:, 2, :], lnst[:, 3, :]
            S.op('act', lambda: nc.scalar.activation(out=msq, in_=Mbank[:, 0:TT], func=AF.Square),
                 reads=[Mkey], writes=['ln0'])
            S.op('dve', lambda: nc.vector.tensor_tensor(out=msq, in0=Qbank[:, 0:TT], in1=msq, op=ALU.subtract),
                 reads=[Qkey, 'ln0'], writes=['ln0'])
            S.op('act', lambda: nc.scalar.activation(out=lnv, in_=msq, func=AF.Ln, bias=EPS), reads=['ln0'], writes=['ln1'])
            S.op('act', lambda: nc.scalar.activation(out=rstd, in_=lnv, func=AF.Exp, scale=-0.5),
                 reads=['ln1'], writes=['ln2'])
            S.op('dve', lambda: nc.vector.tensor_tensor(out=mr, in0=Mbank[:, 0:TT], in1=rstd, op=ALU.mult),
                 reads=[Mkey, 'ln2'], writes=['ln3'])
            G = 4 * NJ

            def g3(t):
                return t.rearrange("p (j h) -> p j h", j=NJ)
            zf = gsb[:, :, 4:8]
            igf = gsb[:, :, 0:4]
            t_az, t_e, t_l, t_mn, t_lf, t_wa, t_ta = [gt[:, i, :] for i in range(7)]
            S.op('dve', lambda: nc.vector.scalar_tensor_tensor(out=g3(t_az), in0=zf, scalar=-1.0, in1=zf, op0=ALU.mult,
                                                               op1=ALU.max), reads=['gsb'], writes=['gt0'])
            S.op('act', lambda: nc.scalar.activation(out=t_e, in_=t_az, func=AF.Exp, scale=-1.0), reads=['gt0'], writes=['gt1'])
            S.op('act', lambda: nc.scalar.activation(out=t_l, in_=t_e, func=AF.Ln, bias=1.0), reads=['gt1'], writes=['gt2'])
            S.op('dve', lambda: nc.vector.tensor_scalar(out=g3(t_mn), in0=zf, scalar1=0.0, scalar2=None, op0=ALU.min),
                 reads=['gsb'], writes=['gt3'])
            S.op('dve', lambda: nc.vector.tensor_tensor(out=t_lf, in0=t_mn, in1=t_l, op=ALU.subtract),
                 reads=['gt3', 'gt2'], writes=['gt4'])
            g1, g1key = bk(S.bank())
            S.op('pe', lambda: nc.tensor.matmul(g1[:, 0:G], lhsT=tri[:], rhs=t_lf, start=True, stop=True),
                 reads=['tri', 'gt4'], writes=[g1key])
            S.op('pe', lambda: nc.tensor.matmul(g1[:, G:2 * G], lhsT=onesf[:], rhs=t_lf, start=True, stop=True),
                 reads=['onesf', 'gt4'], writes=[g1key])
            S.op('dve', lambda: nc.vector.tensor_copy(out=bsb[:, 0:2 * G], in_=g1[:, 0:2 * G]), reads=[g1key], writes=['bsb'])
            S.op('dve', lambda: nc.vector.tensor_tensor(out=g3(ab[:, :]), in0=igf, in1=g3(bsb[:, 0:G]), op=ALU.subtract),
                 reads=['gsb', 'bsb'], writes=['ab'])
            g2, g2key = bk(S.bank())
            S.op('pe', lambda: nc.tensor.matmul(g2[0:G, 0:128], lhsT=ab[:, :], rhs=identf[:], start=True, stop=True),
                 reads=['ab', 'identf'], writes=[g2key])
            S.op('dve', lambda: nc.vector.tensor_reduce(out=amx[0:G, 0:1], in_=g2[0:G, 0:128], axis=AX.X, op=ALU.max),
                 reads=[g2key], writes=['amx'])
            S.op('dve', lambda: nc.vector.tensor_scalar(out=dgm[0:G, 0:G], in0=identf[0:G, 0:G], scalar1=amx[0:G, 0:1],
                                                        scalar2=None, op0=ALU.mult), reads=['amx', 'identf'], writes=['dgm'])
            S.op('pe', lambda: nc.tensor.matmul(g2[:, 128:128 + G], lhsT=onesf[0:G, :], rhs=dgm[0:G, 0:G], start=True,
                                                stop=True), reads=['onesf', 'dgm'], writes=[g2key])
            S.op('dve', lambda: nc.vector.tensor_copy(out=amaxb[:, :], in_=g2[:, 128:128 + G]), reads=[g2key],
                 writes=['amaxb'])
            for j in range(NJ):
                hs = slice(4 * j, 4 * j + 4)
                S.op('dve', lambda hs=hs: nc.vector.tensor_tensor(out=Mb[:, hs], in0=mprev[:, :], in1=amaxb[:, hs], op=ALU.max),
                     reads=['mprev', 'amaxb'], writes=['Mb'])
                S.op('dve', lambda hs=hs: nc.vector.tensor_tensor(out=dmb[:, hs], in0=mprev[:, :], in1=Mb[:, hs],
                                                                op=ALU.subtract), reads=['mprev', 'Mb'], writes=['dmb'])
                S.op('dve', lambda hs=hs, j=j: nc.vector.tensor_tensor(out=mprev[:, :], in0=bsb[:, G + 4 * j:G + 4 * j + 4],
                                                                     in1=Mb[:, hs], op=ALU.add),
                     reads=['bsb', 'Mb'], writes=['mprev'])
            S.op('dve', lambda: nc.vector.tensor_tensor(out=t_wa, in0=ab[:, :], in1=Mb[:, :], op=ALU.subtract),
                 reads=['ab', 'Mb'], writes=['gt5'])
            S.op('act', lambda: nc.scalar.activation(out=wkb[:, :], in_=t_wa, func=AF.Exp), reads=['gt5'], writes=['wkb'])
            S.op('act', lambda: nc.scalar.activation(out=decb[:, :], in_=dmb[:, :], func=AF.Exp), reads=['dmb'], writes=['decb'])
            S.op('dve', lambda: nc.vector.tensor_scalar(out=dec16b[:, :], in0=decb[:, :], scalar1=1.0 / 16, scalar2=None,
                                                        op0=ALU.mult), reads=['decb'], writes=['dec16b'])
            S.op('dve', lambda: nc.vector.tensor_tensor(out=t_ta, in0=bsb[:, 0:G], in1=Mb[:, :], op=ALU.add),
                 reads=['bsb', 'Mb'], writes=['gt6'])
            S.op('act', lambda: nc.scalar.activation(out=thrb[:, :], in_=t_ta, func=AF.Exp, scale=-1.0),
                 reads=['gt6'], writes=['thrb'])
            S.alias(usk, ppk)
            for i in range(8):
                ta, takey = tmpbuf()
                S.op('dve', lambda i=i, ta=ta: nc.vector.tensor_tensor(out=ta, in0=uc[:, i, :], in1=rstd, op=ALU.mult),
                     reads=['uc%d' % i, 'ln2'], writes=[takey])
                S.op('pool', lambda ta=ta: nc.gpsimd.tensor_tensor(out=ta, in0=ta, in1=mr, op=ALU.subtract),
                     reads=[takey, 'ln3'], writes=[takey])
                S.op('act', lambda i=i, ta=ta: nc.scalar.activation(out=usT[:, i, :], in_=ta, func=AF.Silu,
                                                                  scale=pcol[:, PC_LNG + i:PC_LNG + i + 1],
                                                                  bias=pcol[:, PC_LNB + i:PC_LNB + i + 1]),
                     reads=[takey, 'pcol'], writes=['us%d' % i])

            if debug:
                dump('us', dbg['us'][:, t0:t0 + TT].rearrange("(c p) t -> p c t", p=128), usT[:, :, :], usk)
                G_ = 4 * NJ
                for ii, (tt_, kk_) in enumerate([(bsb[:, 0:G_], 'bsb'), (ab[:, :], 'ab'), (Mb[:, :], 'Mb'), (wkb[:, :], 'wkb'),
                                                 (decb[:, :], 'decb'), (thrb[:, :], 'thrb')]):
                    dump('gsm', dbg['gsm'][T * 128:(T + 1) * 128, ii * G_:(ii + 1) * G_], tt_, [kk_])
            S.alias(qkvk, uck)
            for piece, (wname, dstT, dkey) in enumerate([('Q', qT, 'qT'), ('K', kT, 'kT')]):
                wt, wkey = wget(wname)
                wv = wt[:, 0:8192].rearrange("p (c n) -> p c n", c=8)
                S.alias(qpk, mgk)
                for c in range(8):
                    bq, bqkey = bk(S.bank())
                    mm8(bq[:, 0:TT], bqkey, lambda kc, c=c: wv[:, kc, c * 128:(c + 1) * 128], lambda kc: hT[:, kc, :],
                        hTk + [wkey])
                    evac_copy(qp[:, c, HQ:HQ + TT], bq[:, 0:TT], [bqkey], ['qp%d' % c])
                wrelease()
                S.op('pool', lambda piece=piece: nc.gpsimd.tensor_copy(out=qp[:, :, 0:HQ],
                                                                       in_=qhalo[:, piece * 8:(piece + 1) * 8, :]),
                     reads=['qhalo'] + qpk, writes=qpk)
                S.op('pool', lambda piece=piece: nc.gpsimd.tensor_copy(out=qhalo[:, piece * 8:(piece + 1) * 8, :],
                                                                       in_=qp[:, :, TT:TT + HQ]),
                     reads=qpk, writes=['qhalo'])
                for c in range(8):
                    bq, bqkey = bk(S.bank())
                    cc = piece * 8 + c
                    for k in range(4):
                        dgt, dgkey = diag(PC_QKW + k * 16 + cc, 1.0)
                        S.op('pe', lambda k=k, c=c, dgt=dgt, bq=bq: nc.tensor.matmul(bq[:, 0:TT], lhsT=dgt,
                                                                                  rhs=qp[:, c, k:k + TT], start=(k == 0),
                                                                                  stop=(k == 3)),
                             reads=[dgkey, 'qp%d' % c], writes=[bqkey])
                    S.op('act', lambda c=c, cc=cc, bq=bq, dstT=dstT: nc.scalar.activation(
                        out=dstT[:, c, :], in_=bq[:, 0:TT], func=AF.Silu, bias=pcol[:, PC_QKB + cc:PC_QKB + cc + 1]),
                        reads=[bqkey, 'pcol'], writes=['%s%d' % (dkey, c)])

            wt, wkey = wget('V')
            wv = wt[:, 0:8192].rearrange("p (c n) -> p c n", c=8)
            for j in range(NJ):
                for hf in range(2):
                    bv, bvkey = bk(S.bank())
                    mm8(bv[:, :], bvkey, lambda kc, j=j: hT[:, kc, js(j)], lambda kc, hf=hf: wv[:, kc, hf * 512:(hf + 1) * 512],
                        hTk + [wkey])
                    evac_copy(vb[:, j, hf * 512:(hf + 1) * 512], bv[:, :], [bvkey], ['vb%d' % j])
            wrelease()

            if debug:
                dump('q', dbg['q'][:, t0:t0 + TT].rearrange("(c p) t -> p c t", p=128), qT[:, :, :], qkvk)
                dump('k', dbg['k'][:, t0:t0 + TT].rearrange("(c p) t -> p c t", p=128), kT[:, :, :], qkvk)
                dump('v', dbg['v'][t0:t0 + TT, :].rearrange("(j p) d -> p j d", p=128), vb[:, :, :], qkvk)
            wtO, wOkey = wget('O')
            wO = wtO[:, 0:8192].rearrange("p (c n) -> p c n", c=8)
            S.alias(hmTk, hbk)
            for j in range(NJ):
                s2 = j % 2
                gi = 4 * j
                for h in range(4):
                    S.op('pool', lambda h=h: nc.gpsimd.tensor_scalar(
                        out=Cd[:, h, :, :].rearrange("p a e -> p (a e)"), in0=Cst[:, h, :, :].rearrange("p a e -> p (a e)"),
                        scalar1=dec16b[:, gi + h:gi + h + 1], scalar2=1.0, op0=ALU.mult, op1=ALU.mult),
                        reads=['C', 'dec16b'], writes=['Cd'])
                    S.op('pool', lambda h=h: nc.gpsimd.tensor_scalar(
                        out=ndb[:, 2 * h:2 * h + 2], in0=nst[:, 2 * h:2 * h + 2], scalar1=dec16b[:, gi + h:gi + h + 1],
                        scalar2=1.0, op0=ALU.mult, op1=ALU.mult), reads=['nst', 'dec16b'], writes=['ndb'])
                bT, bTkey = bk(S.bank())
                bTb = bT[:].bitcast(BF16)
                for c in range(8):
                    S.op('pe', lambda c=c: nc.tensor.transpose(out=bTb[:, c * 128:(c + 1) * 128], in_=kT[:, c, js(j)],
                                                              identity=identb[:]),
                         reads=['kT%d' % c, 'identb'], writes=[bTkey])
                for h in range(4):
                    S.op('dve', lambda h=h: nc.vector.tensor_scalar(out=kw[:, s2, h, :], in0=bTb[:, h * 256:(h + 1) * 256],
                                                                  scalar1=wkb[:, gi + h:gi + h + 1], scalar2=None,
                                                                  op0=ALU.mult),
                         reads=[bTkey, 'wkb'], writes=['kw%d' % s2])
                bS, bSkey = bk(S.bank())
                for h in range(4):
                    for a in range(2):
                        S.op('pe', lambda h=h, a=a: nc.tensor.matmul(bS[:, h * 128:(h + 1) * 128], lhsT=kT[:, 2 * h + a, js(j)],
                                                                   rhs=qT[:, 2 * h + a, js(j)], start=(a == 0), stop=(a == 1)),
                             reads=['kT%d' % (2 * h + a), 'qT%d' % (2 * h + a)], writes=[bSkey])
                for h in range(4):
                    S.op('dve', lambda h=h: nc.vector.scalar_tensor_tensor(out=PTt[:, s2, h, :], in0=bS[:, h * 128:(h + 1) * 128],
                                                                         scalar=wkb[:, gi + h:gi + h + 1], in1=mask16[:],
                                                                         op0=ALU.mult, op1=ALU.mult),
                         reads=[bSkey, 'wkb', 'mask16'], writes=['PT%d' % s2])
                bN = [bk(S.bank()), bk(S.bank())]
                bD, bDkey = bk(S.bank())
                for h in range(4):
                    bn, bnkey = bN[h // 2]
                    o = (h % 2) * 256
                    S.op('pe', lambda h=h, bn=bn, o=o: nc.tensor.matmul(bn[:, o:o + 256], lhsT=PTt[:, s2, h, :],
                                                                     rhs=vb[:, j, h * 256:(h + 1) * 256], start=True, stop=False),
                         reads=['PT%d' % s2, 'vb%d' % j], writes=[bnkey])
                    for a in range(2):
                        S.op('pe', lambda h=h, a=a, bn=bn, o=o: nc.tensor.matmul(bn[:, o:o + 256], lhsT=qT[:, 2 * h + a, js(j)],
                                                                              rhs=Cd[:, h, a, :], start=False, stop=(a == 1)),
                             reads=['qT%d' % (2 * h + a), 'Cd'], writes=[bnkey])
                    S.op('pe', lambda h=h: nc.tensor.matmul(bD[:, h:h + 1], lhsT=PTt[:, s2, h, :], rhs=onescol[:, 0:1],
                                                           start=True, stop=False),
                         reads=['PT%d' % s2, 'onescol'], writes=[bDkey])
                    for a in range(2):
                        S.op('pe', lambda h=h, a=a: nc.tensor.matmul(bD[:, h:h + 1], lhsT=qT[:, 2 * h + a, js(j)],
                                                                   rhs=ndb[:, 2 * h + a:2 * h + a + 1], start=False,
                                                                   stop=(a == 1)),
                             reads=['qT%d' % (2 * h + a), 'ndb'], writes=[bDkey])
                thj = tho[:, s2, :]
                for hf in range(2):
                    bo, bokey = bk(S.bank())
                    mm8(bo[:, :], bokey, lambda kc: hT[:, kc, js(j)], lambda kc, hf=hf: wO[:, kc, hf * 512:(hf + 1) * 512],
                        hTk + [wOkey])
                    S.op('act', lambda hf=hf, bo=bo: nc.scalar.activation(out=thj[:, hf * 512:(hf + 1) * 512], in_=bo[:, :],
                                                                       func=AF.Tanh, scale=0.5),
                         reads=[bokey], writes=['tho%d' % s2])
                S.op('dve', lambda: nc.vector.scalar_tensor_tensor(out=thj, in0=thj, scalar=1.0, in1=gh[:], op0=ALU.add,
                                                                   op1=ALU.mult), reads=['tho%d' % s2, 'gh'], writes=['tho%d' % s2])
                sm = osm[:, s2, :]
                for h in range(4):
                    bn, bnkey = bN[h // 2]
                    o = (h % 2) * 256
                    S.op('act', lambda h=h, bn=bn, o=o: nc.scalar.activation(out=sqj[:, 0:256], in_=bn[:, o:o + 256],
                                                                          func=AF.Square, accum_out=sm[:, h:h + 1]),
                         reads=[bnkey], writes=['osm%d' % s2])
                S.op('dve', lambda: nc.vector.tensor_copy(out=sm[:, 4:8], in_=bD[:, 0:4]), reads=[bDkey], writes=['osm%d' % s2])
                S.op('dve', lambda: nc.vector.scalar_tensor_tensor(out=sm[:, 8:12], in0=sm[:, 4:8], scalar=-1.0, in1=sm[:, 4:8],
                                                                   op0=ALU.mult, op1=ALU.max),
                     reads=['osm%d' % s2], writes=['osm%d' % s2])
                S.op('dve', lambda: nc.vector.tensor_tensor(out=sm[:, 8:12], in0=sm[:, 8:12], in1=thrb[:, gi:gi + 4], op=ALU.max),
                     reads=['osm%d' % s2, 'thrb'], writes=['osm%d' % s2])
                S.op('dve', lambda: nc.vector.scalar_tensor_tensor(out=sm[:, 12:16], in0=sm[:, 8:12], scalar=EPS, in1=sm[:, 8:12],
                                                                   op0=ALU.mult, op1=ALU.mult),
                     reads=['osm%d' % s2], writes=['osm%d' % s2])
                S.op('dve', lambda: nc.vector.scalar_tensor_tensor(out=sm[:, 16:20], in0=sm[:, 0:4], scalar=1.0 / 256,
                                                                   in1=sm[:, 12:16], op0=ALU.mult, op1=ALU.add),
                     reads=['osm%d' % s2], writes=['osm%d' % s2])
                S.op('pool', lambda: nc.gpsimd.tensor_tensor(out=sm[:, 20:24], in0=sm[:, 16:20], in1=mh4[:, 0:4], op=ALU.pow),
                     reads=['osm%d' % s2, 'mh4'], writes=['osm%d' % s2])
                for h in range(4):
                    bn, bnkey = bN[h // 2]
                    o = (h % 2) * 256
                    S.op('dve', lambda h=h, bn=bn, o=o: nc.vector.scalar_tensor_tensor(
                        out=hm[:, s2, h * 256:(h + 1) * 256], in0=bn[:, o:o + 256], scalar=sm[:, 20 + h:21 + h],
                        in1=thj[:, h * 256:(h + 1) * 256], op0=ALU.mult, op1=ALU.mult),
                        reads=[bnkey, 'osm%d' % s2, 'tho%d' % s2], writes=['hm%d' % s2])
                if debug:
                    if T == 0 and j == 0:
                        misc = es.enter_context(nc.sbuf_tensor("miscb", [128, 2560], F32))
                        S.op('dve', lambda: nc.vector.tensor_copy(out=misc[:, 0:512], in_=bN[0][0][:, :]), reads=[bN[0][1]], writes=['misc'])
                        S.op('dve', lambda: nc.vector.tensor_copy(out=misc[:, 512:1024], in_=bN[1][0][:, :]), reads=[bN[1][1]], writes=['misc'])
                        S.op('dve', lambda: nc.vector.tensor_copy(out=misc[:, 1024:1536], in_=PTt[:, s2, :, :].rearrange("p h t -> p (h t)")), reads=['PT%d' % s2], writes=['misc'])
                        S.op('dve', lambda: nc.vector.tensor_copy(out=misc[:, 1536:1560], in_=sm), reads=['osm%d' % s2], writes=['misc'])
                        S.op('dve', lambda: nc.vector.tensor_copy(out=misc[:, 1600:1616], in_=wkb[:, :]), reads=['wkb'], writes=['misc'])
                        S.op('dve', lambda: nc.vector.tensor_copy(out=misc[:, 1616:1632], in_=thrb[:, :]), reads=['thrb'], writes=['misc'])
                        S.op('dve', lambda: nc.vector.tensor_copy(out=misc[:, 2048:2560], in_=bS[:, :]), reads=[bSkey], writes=['misc'])
                        dump('misc', dbg['misc'][:, :], misc[:, :], ['misc'])
                bU = [bk(S.bank()) for _ in range(4)]
                bD2, bD2key = bk(S.bank())
                for h in range(4):
                    for a in range(2):
                        bu, bukey = bU[h]
                        S.op('pe', lambda h=h, a=a, bu=bu: nc.tensor.matmul(bu[:, a * 256:(a + 1) * 256],
                                                                         lhsT=kw[:, s2, h, a * 128:(a + 1) * 128],
                                                                         rhs=vb[:, j, h * 256:(h + 1) * 256], start=True, stop=True),
                             reads=['kw%d' % s2, 'vb%d' % j], writes=[bukey])
                        S.op('pe', lambda h=h, a=a: nc.tensor.matmul(bD2[:, 2 * h + a:1 + 2 * h + a],
                                                                   lhsT=kw[:, s2, h, a * 128:(a + 1) * 128], rhs=onescol[:, 0:1],
                                                                   start=True, stop=True),
                             reads=['kw%d' % s2, 'onescol'], writes=[bD2key])
                for h in range(4):
                    bu, bukey = bU[h]
                    S.op('dve', lambda h=h, bu=bu: nc.vector.scalar_tensor_tensor(
                        out=Cst[:, h, :, :].rearrange("p a e -> p (a e)"), in0=Cst[:, h, :, :].rearrange("p a e -> p (a e)"),
                        scalar=decb[:, gi + h:gi + h + 1], in1=bu[:, 0:512], op0=ALU.mult, op1=ALU.add),
                        reads=['C', 'decb', bukey], writes=['C'])
                    S.op('dve', lambda h=h: nc.vector.scalar_tensor_tensor(
                        out=nst[:, 2 * h:2 * h + 2], in0=nst[:, 2 * h:2 * h + 2], scalar=decb[:, gi + h:gi + h + 1],
                        in1=bD2[:, 2 * h:2 + 2 * h], op0=ALU.mult, op1=ALU.add),
                        reads=['nst', 'decb', bD2key], writes=['nst'])
                if debug:
                    dump('hm', dbg['hm'][t0 + j * 128:t0 + (j + 1) * 128, :], hm[:, s2, :], ['hm%d' % s2])
                bH, bHkey = bk(S.bank())
                bHb = bH[:].bitcast(BF16)
                for c in range(8):
                    S.op('pe', lambda c=c: nc.tensor.transpose(out=bHb[:, c * 128:(c + 1) * 128],
                                                              in_=hm[:, s2, c * 128:(c + 1) * 128], identity=identb[:]),
                         reads=['hm%d' % s2, 'identb'], writes=[bHkey])
                evac_copy(hmT[:, :, js(j)], bHb[:, 0:1024].rearrange("p (c t) -> p c t", c=8), [bHkey], ['hmT%d' % j])
            wrelease()

            S.alias(mgk, qpk)
            for n in range(4):
                wt, wkey = wget('M%d' % n)
                wv = wt[:, 0:8192].rearrange("p (m c n) -> p m c n", m=4, c=8)
                for sub in range(2):
                    c = 2 * n + sub
                    cs = slice(sub * 128, (sub + 1) * 128)
                    b_gm, k_gm = bk(S.bank())
                    b_bm, k_bm = bk(S.bank())
                    mm8(b_gm[:, 0:TT], k_gm, lambda kc: wv[:, 0, kc, cs], lambda kc: hT[:, kc, :], hTk + [wkey])
                    mm8(b_bm[:, 0:TT], k_bm, lambda kc: wv[:, 2, kc, cs], lambda kc: hmT[:, kc, :], hmTk + [wkey])
                    t1, t1k = tmpbuf()
                    S.op('act', lambda: nc.scalar.activation(out=t1, in_=b_gm[:, 0:TT], func=AF.Tanh, scale=0.5),
                         reads=[k_gm], writes=[t1k])
                    S.op('dve', lambda: nc.vector.scalar_tensor_tensor(out=t1, in0=t1, scalar=1.0, in1=b_bm[:, 0:TT],
                                                                       op0=ALU.add, op1=ALU.mult),
                         reads=[t1k, k_bm], writes=[t1k])
                    b_gc, k_gc = bk(S.bank())
                    b_bc, k_bc = bk(S.bank())
                    mm8(b_gc[:, 0:TT], k_gc, lambda kc: wv[:, 1, kc, cs], lambda kc: hT[:, kc, :], hTk + [wkey])
                    mm8(b_bc[:, 0:TT], k_bc, lambda kc: wv[:, 3, kc, cs], lambda kc: usT[:, kc, :], usk + [wkey])
                    t2, t2k = tmpbuf()
                    S.op('act', lambda: nc.scalar.activation(out=t2, in_=b_gc[:, 0:TT], func=AF.Tanh, scale=0.5),
                         reads=[k_gc], writes=[t2k])
                    S.op('dve', lambda: nc.vector.scalar_tensor_tensor(out=t2, in0=t2, scalar=1.0, in1=b_bc[:, 0:TT],
                                                                       op0=ALU.add, op1=ALU.mult),
                         reads=[t2k, k_bc], writes=[t2k])
                    S.op('pool', lambda c=c: nc.gpsimd.tensor_tensor(out=mgT[:, c, :], in0=t1, in1=t2, op=ALU.add),
                         reads=[t1k, t2k], writes=['mg%d' % c])
                wrelease()

            if debug:
                dump('mg', dbg['mg'][:, t0:t0 + TT].rearrange("(c p) t -> p c t", p=128), mgT[:, :, :], mgk)
            wt, wkey = wget('WO')
            wv = wt[:, 0:8192].rearrange("p (c n) -> p c n", c=8)
            for j in range(NJ):
                for hf in range(2):
                    bo, bokey = bk(S.bank())
                    mm8(bo[:, :], bokey, lambda kc: mgT[:, kc, js(j)], lambda kc, hf=hf: wv[:, kc, hf * 512:(hf + 1) * 512],
                        mgk + [wkey])
                    S.op('dve', lambda hf=hf, bo=bo: nc.vector.scalar_tensor_tensor(
                        out=xt[:, j, hf * 512:(hf + 1) * 512], in0=bo[:, :], scalar=0.5, in1=xt[:, j, hf * 512:(hf + 1) * 512],
                        op0=ALU.mult, op1=ALU.add), reads=[bokey, 'xt%d' % j], writes=['xt%d' % j])
            wrelease()

            if debug:
                dump('x1', dbg['x1'][t0:t0 + TT, :].rearrange("(j p) d -> p j d", p=128), xt[:, :, :], ['xt%d' % j for j in range(NJ)])
            S.alias(hbk, hmTk)
            norm_T(PC_FFN)
            S.alias(hidk, qkvk)
            for f in range(6):
                wt, wkey = wget('F%d' % f)
                ncl = 4 if f < 5 else 2
                wv = wt[:, 0:2 * 8 * ncl * 128].rearrange("p (m c n) -> p m c n", m=2, c=8)
                for il in range(ncl):
                    n = 4 * f + il
                    cs = slice(il * 128, (il + 1) * 128)
                    b_g, k_g = bk(S.bank())
                    b_u, k_u = bk(S.bank())
                    mm8(b_g[:, 0:TT], k_g, lambda kc: wv[:, 0, kc, cs], lambda kc: hT[:, kc, :], hTk + [wkey])
                    mm8(b_u[:, 0:TT], k_u, lambda kc: wv[:, 1, kc, cs], lambda kc: hT[:, kc, :], hTk + [wkey])
                    t1, t1k = tmpbuf()
                    S.op('act', lambda: nc.scalar.activation(out=t1, in_=b_g[:, 0:TT], func=AF.Silu), reads=[k_g], writes=[t1k])
                    S.op('dve', lambda n=n: nc.vector.tensor_tensor(out=hid[:, n, :], in0=t1, in1=b_u[:, 0:TT], op=ALU.mult),
                         reads=[t1k, k_u], writes=['hid%d' % n])
                wrelease()
            for b in range(4):
                wt, wkey = wget('D%d' % b)
                wv = wt[:, 0:NFF * 256].rearrange("p (c n) -> p c n", c=NFF)
                for j in range(NJ):
                    bd, bdkey = bk(S.bank())
                    mm8(bd[:, 0:256], bdkey, lambda kc: hid[:, kc, js(j)], lambda kc: wv[:, kc, :], hidk + [wkey], n=NFF)
                    S.op('dve', lambda b=b, bd=bd: nc.vector.tensor_tensor(out=xt[:, j, b * 256:(b + 1) * 256], in0=bd[:, 0:256],
                                                                         in1=xt[:, j, b * 256:(b + 1) * 256], op=ALU.add),
                         reads=[bdkey, 'xt%d' % j], writes=['xt%d' % j])
                wrelease()

            if debug:
                dump('x2', dbg['x2'][t0:t0 + TT, :].rearrange("(j p) d -> p j d", p=128), xt[:, :, :], ['xt%d' % j for j in range(NJ)])
            S.alias(ppk, usk)
            for j in range(NJ):
                S.dma('sp', 'pld%d' % j, pt[:, j, :], p_in[t0 + j * 128:t0 + (j + 1) * 128, :], writes=['pt%d' % j])
            S.op('pool', lambda: nc.gpsimd.tensor_copy(out=pb[:, :, :], in_=pt[:, :, :]), reads=['pt%d' % j for j in range(NJ)], writes=['pb'])
            bp_, bpkey = bk(S.bank())
            bpb = bp_[:].bitcast(BF16)
            for c in range(2):
                for j in range(NJ):
                    o = c * 512 + j * 128
                    S.op('pe', lambda c=c, j=j, o=o: nc.tensor.transpose(out=bpb[:, o:o + 128], in_=pb[:, j, c * 128:(c + 1) * 128],
                                                                      identity=identb[:]),
                         reads=['pb', 'identb'], writes=[bpkey])
            for c in range(2):
                evac_copy(pT[:, c, :], bpb[:, c * 512:c * 512 + TT], [bpkey], ['pT'])
            norm_T(PC_PLE)
            wtg, wgkey = wget('PG')
            wtp, wpkey = wget('PP')
            wg = wtg[:, 0:8192].rearrange("p (c n) -> p c n", c=8)
            wp = wtp[:, 0:2048].rearrange("p (c n) -> p c n", c=2)
            for j in range(NJ):
                for hf in range(2):
                    b_g, k_g = bk(S.bank())
                    b_p, k_p = bk(S.bank())
                    mm8(b_g[:, :], k_g, lambda kc: hT[:, kc, js(j)], lambda kc, hf=hf: wg[:, kc, hf * 512:(hf + 1) * 512],
                        hTk + [wgkey])
                    mm8(b_p[:, :], k_p, lambda kc: pT[:, kc, js(j)], lambda kc, hf=hf: wp[:, kc, hf * 512:(hf + 1) * 512],
                        ['pT', wpkey], n=2)
                    t1, t1k = tmpbuf()
                    S.op('act', lambda: nc.scalar.activation(out=t1, in_=b_g[:, :], func=AF.Tanh, scale=0.5),
                         reads=[k_g], writes=[t1k])
                    S.op('dve', lambda: nc.vector.scalar_tensor_tensor(out=t1, in0=t1, scalar=1.0, in1=b_p[:, :], op0=ALU.add,
                                                                       op1=ALU.mult), reads=[t1k, k_p], writes=[t1k])
                    S.op('dve', lambda hf=hf: nc.vector.scalar_tensor_tensor(
                        out=xt[:, j, hf * 512:(hf + 1) * 512], in0=t1, scalar=0.5, in1=xt[:, j, hf * 512:(hf + 1) * 512],
                        op0=ALU.mult, op1=ALU.add), reads=[t1k, 'xt%d' % j], writes=['xt%d' % j])
            wrelease()
            wrelease()

            if debug:
                dump('x3', dbg['x3'][t0:t0 + TT, :].rearrange("(j p) d -> p j d", p=128), xt[:, :, :], ['xt%d' % j for j in range(NJ)])
            for j in range(NJ):
                S.op('act', lambda j=j: nc.scalar.activation(out=sqj[:], in_=xt[:, j, :], func=AF.Square,
                                                           accum_out=ss[:, j:j + 1]), reads=['xt%d' % j], writes=['ssa%d' % j])
            S.op('dve', lambda: nc.vector.tensor_scalar(out=ss[:, NJ:2 * NJ], in0=ss[:, 0:NJ], scalar1=1.0 / D, scalar2=EPS,
                                                        op0=ALU.mult, op1=ALU.add), reads=['ssa%d' % j for j in range(NJ)], writes=['ss'])
            S.op('pool', lambda: nc.gpsimd.tensor_tensor(out=ss[:, 2 * NJ:3 * NJ], in0=ss[:, NJ:2 * NJ], in1=mh4[:, 0:NJ],
                                                         op=ALU.pow), reads=['ss', 'mh4'], writes=['ss'])
            for j in range(NJ):
                s2 = j % 2
                S.op('dve', lambda j=j, s2=s2: nc.vector.scalar_tensor_tensor(
                    out=tho[:, s2, :], in0=xt[:, j, :], scalar=ss[:, 2 * NJ + j:2 * NJ + j + 1], in1=fg[:], op0=ALU.mult,
                    op1=ALU.mult), reads=['xt%d' % j, 'ss', 'fg'], writes=['tho%d' % s2])
                S.dma('sp', 'o%d' % s2, y[t0 + j * 128:t0 + (j + 1) * 128, :], tho[:, s2, :], reads=['tho%d' % s2],
                      writes=['y%d_%d' % (T, j)])
        S.finish('sp', ['y%d_%d' % (T, j) for T in range(NT) for j in range(NJ)] + ['dbgout%d' % (i + 1) for i in range(dcnt['i'])])
        print("sched: waits=%d pe=%d act=%d dve=%d pool=%d" % (S.nwaits, S.cnt['pe'], S.cnt['act'], S.cnt['dve'], S.cnt['pool']))
    return nc


_PROGRAM = {}


def kernel(**inputs):
    f32 = lambda a: np.ascontiguousarray(np.asarray(a, dtype=np.float32))
    x = f32(inputs['x'])
    p = f32(inputs['p'])[0]
    shared = {
        'w_in': f32(inputs['w_in'][0]), 'w_branch_m': f32(inputs['w_branch_m'][0]),
        'w_branch_c': f32(inputs['w_branch_c'][0]), 'w_out': f32(inputs['w_out'][0]),
        'w_ffn_gate': f32(inputs['w_ffn_gate'][0]), 'w_ffn_up': f32(inputs['w_ffn_up'][0]),
        'w_ffn_down': f32(inputs['w_ffn_down'][0]), 'w_ple_gate': f32(inputs['w_ple_gate'][0]),
        'w_ple_proj': f32(inputs['w_ple_proj'][0]),
        'norm_mix_g': f32(inputs['norm_mix_g'][0]), 'mh_norm_g': f32(inputs['mh_norm_g'][0]),
        'conf_conv_b': f32(inputs['conf_conv_b'][0]), 'conf_ln_g': f32(inputs['conf_ln_g'][0]),
        'conf_ln_b': f32(inputs['conf_ln_b'][0]), 'norm_ffn_g': f32(inputs['norm_ffn_g'][0]),
        'norm_ple_g': f32(inputs['norm_ple_g'][0]), 'final_g': f32(inputs['final_g']),
        'b_if': f32(inputs['b_if'][0]), 'conv_qk_w': f32(inputs['conv_qk_w'][0]),
        'conv_qk_b': f32(inputs['conv_qk_b'][0]), 'conf_conv_w': f32(inputs['conf_conv_w'][0]),
    }
    if 'nc' not in _PROGRAM:
        _PROGRAM['nc'] = build_program()
    nc = _PROGRAM['nc']
    in_maps = []
    for b in range(NCORES):
        m = dict(shared)
        m['x'] = np.ascontiguousarray(x[b])
        m['p'] = np.ascontiguousarray(p[b])
        in_maps.append(m)
    res = run_bass_kernel_spmd(nc, in_maps, core_ids=list(range(NCORES)))
    out = np.stack([np.asarray(res.results[b]['y'], dtype=np.float32) for b in range(NCORES)], axis=0)
    return out
```

```python
import contextlib
import numpy as np
import concourse.bass as bass
import concourse.mybir as mybir
from concourse.bass_utils import run_bass_kernel_spmd

F32 = mybir.dt.float32
BF16 = mybir.dt.bfloat16
AF = mybir.ActivationFunctionType
ALU = mybir.AluOpType
AX = mybir.AxisListType

SEQ = 2048
D = 1024
NIN = 8200
DFF = 2816
NFF = DFF // 128
PLE = 256
TT = 512
NJ = TT // 128
NT = SEQ // TT
EPS = 1e-6
HQ = 3
HC = 30
NCORES = 8


class Sched:
    def __init__(self, nc, es):
        self.nc = nc
        self.es = es
        self.eng = {'pe': nc.tensor, 'act': nc.scalar, 'dve': nc.vector, 'pool': nc.gpsimd, 'sp': nc.sync}
        self.sems = {}
        self.cnt = {}
        for e in ['pe', 'act', 'dve', 'pool']:
            self.sems[e] = es.enter_context(nc.semaphore('s_' + e))
            self.cnt[e] = 0
        self.clock = {e: {} for e in self.eng}
        self.snap = {}
        self.lastw = {}
        self.readers = {}
        self.nwaits = 0
        self.nbank = 0

    def bank(self):
        b = self.nbank % 6
        self.nbank += 1
        return b

    def _deps(self, reads, writes):
        deps = {}

        def add(k, v):
            if deps.get(k, 0) < v:
                deps[k] = v
        for r in reads:
            if r in self.lastw:
                add(*self.lastw[r])
        for w in writes:
            if w in self.lastw:
                add(*self.lastw[w])
            for k, v in self.readers.get(w, {}).items():
                add(k, v)
        return deps

    def _wait(self, e, deps):
        ck = self.clock[e]
        for k, v in sorted(deps.items(), key=lambda kv: -kv[1]):
            if e == 'pe' and k == 'pe':
                continue
            if ck.get(k, 0) >= v:
                continue
            mult = 16 if k.startswith('dq') else 1
            self.eng[e].wait_ge(self.sems[k], v * mult)
            self.nwaits += 1
            for kk, vv in self.snap.get((k, v), {}).items():
                if ck.get(kk, 0) < vv:
                    ck[kk] = vv
            ck[k] = max(ck.get(k, 0), v)

    def _record(self, key, val, reads, writes):
        for r in reads:
            d = self.readers.setdefault(r, {})
            if d.get(key, 0) < val:
                d[key] = val
        for w in writes:
            self.lastw[w] = (key, val)
            self.readers[w] = {}

    def op(self, e, fn, reads=(), writes=()):
        self._wait(e, self._deps(reads, writes))
        ins = fn()
        self.cnt[e] += 1
        v = self.cnt[e]
        ins.then_inc(self.sems[e], 1)
        self.snap[(e, v)] = dict(self.clock[e])
        self._record(e, v, reads, writes)
        return ins

    def dma(self, q, semkey, out, in_, reads=(), writes=(), **kw):
        semkey = 'dq' + semkey
        if semkey not in self.sems:
            self.sems[semkey] = self.es.enter_context(self.nc.semaphore('d_' + semkey))
            self.cnt[semkey] = 0
        deps = self._deps(reads, writes)
        if self.cnt[semkey] > 0:
            if deps.get(semkey, 0) < self.cnt[semkey]:
                deps[semkey] = self.cnt[semkey]
        self._wait(q, deps)
        ins = self.eng[q].dma_start(out=out, in_=in_, **kw)
        self.cnt[semkey] += 1
        v = self.cnt[semkey]
        ins.then_inc(self.sems[semkey], 16)
        self.snap[(semkey, v)] = dict(self.clock[q])
        self._record(semkey, v, reads, writes)
        return ins

    def alias(self, new_keys, old_keys):
        deps = {}
        for k in old_keys:
            if k in self.lastw:
                kk, vv = self.lastw[k]
                if deps.get(kk, 0) < vv:
                    deps[kk] = vv
            for kk, vv in self.readers.get(k, {}).items():
                if deps.get(kk, 0) < vv:
                    deps[kk] = vv
        for nk in new_keys:
            d = self.readers.setdefault(nk, {})
            for kk, vv in deps.items():
                if d.get(kk, 0) < vv:
                    d[kk] = vv

    def finish(self, e, keys):
        deps = {}
        for k in keys:
            if k in self.lastw:
                kk, vv = self.lastw[k]
                if deps.get(kk, 0) < vv:
                    deps[kk] = vv
        self._wait(e, deps)


def weight_tiles():
    t = []
    for hf in range(2):
        t.append(('A%d' % hf, [('w_in', 4104 + 512 * hf, 512, 8), ('w_in', 5128 + 512 * hf, 512, 8)]))
    t.append(('Q', [('w_in', 0, 1024, 8)]))
    t.append(('V', [('w_in', 2048, 1024, 8)]))
    t.append(('K', [('w_in', 1024, 1024, 8)]))
    t.append(('O', [('w_in', 3072, 1024, 8)]))
    for n in range(4):
        t.append(('M%d' % n, [('w_in', 6152 + 256 * n, 256, 8), ('w_in', 7176 + 256 * n, 256, 8),
                              ('w_branch_m', 256 * n, 256, 8), ('w_branch_c', 256 * n, 256, 8)]))
    t.append(('WO', [('w_out', 0, 1024, 8)]))
    for f in range(6):
        nc_ = 512 if f < 5 else 256
        t.append(('F%d' % f, [('w_ffn_gate', 512 * f, nc_, 8), ('w_ffn_up', 512 * f, nc_, 8)]))
    for b in range(4):
        t.append(('D%d' % b, [('w_ffn_down', 256 * b, 256, NFF)]))
    t.append(('PG', [('w_ple_gate', 0, 1024, 8)]))
    t.append(('PP', [('w_ple_proj', 0, 1024, 2)]))
    return t


def build_program(debug=False):
    nc = bass.Bass("TRN2", target_bir_lowering=False)
    din = {}

    def dram_in(name, shape):
        din[name] = nc.dram_tensor(name, shape, F32, kind="ExternalInput").ap()
        return din[name]

    x = dram_in("x", [SEQ, D])
    p_in = dram_in("p", [SEQ, PLE])
    dram_in("w_in", [D, NIN])
    dram_in("w_branch_m", [D, D])
    dram_in("w_branch_c", [D, D])
    dram_in("w_out", [D, D])
    dram_in("w_ffn_gate", [D, DFF])
    dram_in("w_ffn_up", [D, DFF])
    dram_in("w_ffn_down", [DFF, D])
    dram_in("w_ple_gate", [D, D])
    dram_in("w_ple_proj", [PLE, D])
    for nm in ["norm_mix_g", "mh_norm_g", "conf_conv_b", "conf_ln_g", "conf_ln_b", "norm_ffn_g", "norm_ple_g", "final_g"]:
        dram_in(nm, [D])
    dram_in("b_if", [8])
    dram_in("conv_qk_w", [4, 2 * D])
    dram_in("conv_qk_b", [2 * D])
    dram_in("conf_conv_w", [31, D])
    y = nc.dram_tensor("y", [SEQ, D], F32, kind="ExternalOutput").ap()

    dbg = {}
    if debug:
        for nm, shp, dt in [('x1', [SEQ, D], F32), ('x2', [SEQ, D], F32), ('x3', [SEQ, D], F32), ('hm', [SEQ, D], BF16),
                            ('us', [D, SEQ], BF16), ('q', [D, SEQ], BF16), ('k', [D, SEQ], BF16), ('v', [SEQ, D], BF16),
                            ('hT', [D, SEQ], BF16), ('mg', [D, SEQ], BF16), ('gates', [SEQ, 8], F32),
                            ('gsm', [NT * 128, 6 * 4 * NJ], F32),  ('ub', [D, SEQ], BF16), ('uc', [D, SEQ], F32)]:
            dbg[nm] = nc.dram_tensor('dbg_' + nm, shp, dt, kind="ExternalOutput").ap()
    tiles = weight_tiles()
    scr = {}
    for name, pieces in tiles:
        X = sum(ncols * kc for (_, _, ncols, kc) in pieces)
        scr[name] = nc.dram_tensor("scr_" + name, [128, X], BF16).ap()

    with contextlib.ExitStack() as es:
        S = Sched(nc, es)

        def sb(name, shape, dt):
            return es.enter_context(nc.sbuf_tensor(name, shape, dt))

        banks = [es.enter_context(nc.psum_tensor("bank%d" % i, [128, 512], F32)) for i in range(8)]

        def bk(i):
            return banks[i], 'ps%d' % i

        xt = sb("xt", [128, NJ, D], F32)
        hbm = sb("hbm", [128, NJ * D], BF16)
        hb = hbm[:].rearrange("p (j d) -> p j d", j=NJ)
        hmT = hbm[:].rearrange("p (c t) -> p c t", c=8)
        hT = sb("hT", [128, 8, TT], BF16)
        wsl = [sb("wsl%d" % i, [128, 8192], BF16) for i in range(3)]
        ub = sb("ub", [128, 8, HC + TT], BF16)
        XR = sb("XR", [128, 12288], BF16)
        uc = XR[:, 0:8192].bitcast(F32).rearrange("p (i t) -> p i t", i=8)
        ucb = XR[:, 8192:8192 + 2 * TT].rearrange("p (s t) -> p s t", s=2)
        usq = XR[:, 8192 + 2 * TT:8192 + 4 * TT].rearrange("p (s t) -> p s t", s=2)
        qT = XR[:, 0:4096].rearrange("p (c t) -> p c t", c=8)
        kT = XR[:, 4096:8192].rearrange("p (c t) -> p c t", c=8)
        vb = XR[:, 8192:12288].rearrange("p (j d) -> p j d", j=NJ)
        hid = XR[:, 0:NFF * TT].rearrange("p (n t) -> p n t", n=NFF)
        usp = sb("usp", [128, 4096], BF16)
        usT = usp[:].rearrange("p (c t) -> p c t", c=8)
        pt = usp[:, 0:2048].bitcast(F32).rearrange("p (j d) -> p j d", j=NJ)
        pb = usp[:, 2048:3072].rearrange("p (j d) -> p j d", j=NJ)
        pT = usp[:, 3072:4096].rearrange("p (c t) -> p c t", c=2)
        YR = sb("YR", [128, 8 * (HQ + TT)], BF16)
        qp = YR[:].rearrange("p (c t) -> p c t", c=8)
        mgT = YR[:, 0:4096].rearrange("p (c t) -> p c t", c=8)
        qhalo = sb("qhalo", [128, 16, HQ], BF16)
        kw = sb("kw", [128, 2, 4, 256], BF16)
        PTt = sb("PTt", [128, 2, 4, 128], BF16)
        Cd = sb("Cd", [128, 4, 2, 256], BF16)
        ndb = sb("ndb", [128, 8], BF16)
        Cst = sb("Cst", [128, 4, 2, 256], F32)
        nst = sb("nst", [128, 8], F32)
        mprev = sb("mprev", [128, 4], F32)
        hm = sb("hm", [128, 2, D], BF16)
        tmp = sb("tmp", [128, 4, 512], F32)
        tho = sb("tho", [128, 2, D], F32)
        gh = sb("gh", [128, D], F32)
        fg = sb("fg", [128, D], F32)
        identf = sb("identf", [128, 128], F32)
        identb = sb("identb", [128, 128], BF16)
        tri = sb("tri", [128, 128], F32)
        onesf = sb("onesf", [128, 128], F32)
        mask16 = sb("mask16", [128, 128], F32)
        onesm = sb("onesm", [128, 128], BF16)
        onescol = sb("onescol", [128, 2], BF16)
        mh4 = sb("mh4", [128, 16], F32)
        pcol = sb("pcol", [128, 384], F32)
        rows = [sb("rows%d" % i, [128, 128], F32) for i in range(4)]
        dg = sb("dg", [128, 16, 128], BF16)
        wif = sb("wif", [128, 8, 8], BF16)
        ifb = sb("ifb", [128, NJ, 8], F32)
        sqj = sb("sqj", [128, D], BF16)
        ss = sb("ss", [128, 4 * NJ], F32)
        gsb = sb("gsb", [128, NJ, 8], F32)
        gt = sb("gt", [128, 12, 4 * NJ], F32)
        bsb = sb("bsb", [128, 8 * NJ], F32)
        ab = sb("ab", [128, 4 * NJ], F32)
        amx = sb("amx", [128, 1], F32)
        dgm = sb("dgm", [128, 4 * NJ], F32)
        amaxb = sb("amaxb", [128, 4 * NJ], F32)
        Mb = sb("Mb", [128, 4 * NJ], F32)
        dmb = sb("dmb", [128, 4 * NJ], F32)
        wkb = sb("wkb", [128, 4 * NJ], F32)
        decb = sb("decb", [128, 4 * NJ], F32)
        dec16b = sb("dec16b", [128, 4 * NJ], F32)
        thrb = sb("thrb", [128, 4 * NJ], F32)
        lnst = sb("lnst", [128, 4, TT], F32)
        osm = sb("osm", [128, 2, 24], F32)

        PC_MIX, PC_FFN, PC_PLE, PC_QKB, PC_CCB, PC_LNG, PC_LNB = 0, 8, 16, 24, 40, 48, 56
        PC_QKW = 64
        PC_CW0 = 128
        PC_CW1 = 256

        def cw_col(k, i):
            return (PC_CW0 + k * 8 + i) if k < 16 else (PC_CW1 + (k - 16) * 8 + i)

        scr_keys = {}
        cast_state = {'next': 0}
        LOOKAHEAD = 6

        def cast_next():
            i = cast_state['next']
            if i >= len(tiles):
                return
            cast_state['next'] += 1
            name, pieces = tiles[i]
            off = 0
            for pi, (src, c0, ncols, kc) in enumerate(pieces):
                key = 'scr_%s_%d' % (name, pi)
                dst = scr[name][:, off:off + kc * ncols].rearrange("p (c n) -> p c n", c=kc)
                srcap = din[src][:, c0:c0 + ncols].rearrange("(c p) n -> p c n", p=128)
                S.dma('pool', 'cast_%s_%d' % (name, pi), dst, srcap, writes=[key])
                off += kc * ncols

        for name, pieces in tiles:
            scr_keys[name] = ['scr_%s_%d' % (name, pi) for pi in range(len(pieces))]
        S.dma('pool', 'wif', wif[:], din['w_in'][:, 4096:4104].rearrange("(c p) n -> p c n", p=128), writes=['wif'])
        for _ in range(LOOKAHEAD):
            cast_next()

        order = [name for name, _ in tiles] * NT
        wstate = {'next': 0, 'cur': {}}
        tile_X = {name: sum(ncols * kc for (_, _, ncols, kc) in pieces) for name, pieces in tiles}

        def wissue():
            i = wstate['next']
            if i >= len(order):
                return
            name = order[i]
            slot = i % 3
            S.dma('sp', 'w%d' % slot, wsl[slot][:, 0:tile_X[name]], scr[name][:, :],
                  reads=scr_keys[name], writes=['w%d' % slot])
            wstate['next'] += 1

        def wget(name):
            i = wstate['cur'].get('i', -1) + 1
            assert order[i] == name, (order[i], name)
            wstate['cur']['i'] = i
            slot = i % 3
            if i < len(tiles):
                cast_next()
            return wsl[slot], 'w%d' % slot

        def wrelease():
            wissue()

        S.op('pool', lambda: nc.gpsimd.memset(identf[:], 1.0), writes=['identf'])
        S.op('pool', lambda: nc.gpsimd.affine_select(out=identf[:], in_=identf[:], pattern=[[-1, 128]], base=0,
                                                      channel_multiplier=1, compare_op=ALU.is_equal, fill=0.0),
             reads=['identf'], writes=['identf'])
        S.op('pool', lambda: nc.gpsimd.memset(tri[:], 1.0), writes=['tri'])
        S.op('pool', lambda: nc.gpsimd.affine_select(out=tri[:], in_=tri[:], pattern=[[1, 128]], base=0,
                                                      channel_multiplier=-1, compare_op=ALU.is_ge, fill=0.0),
             reads=['tri'], writes=['tri'])
        S.op('pool', lambda: nc.gpsimd.memset(onesf[:], 1.0), writes=['onesf'])
        S.op('pool', lambda: nc.gpsimd.memset(onesm[:], 1.0 / D), writes=['onesm'])
        S.op('pool', lambda: nc.gpsimd.memset(onescol[:], 1.0), writes=['onescol'])
        S.op('pool', lambda: nc.gpsimd.memset(mh4[:], -0.5), writes=['mh4'])
        S.op('dve', lambda: nc.vector.tensor_copy(out=identb[:], in_=identf[:]), reads=['identf'], writes=['identb'])
        S.op('dve', lambda: nc.vector.tensor_scalar(out=mask16[:], in0=tri[:], scalar1=1.0 / 16, scalar2=None,
                                                    op0=ALU.mult), reads=['tri'], writes=['mask16'])
        S.op('pool', lambda: nc.gpsimd.memset(Cst[:], 0.0), writes=['C%d' % h for h in range(4)])
        S.op('pool', lambda: nc.gpsimd.memset(nst[:], 0.0), writes=['nst'])
        S.op('pool', lambda: nc.gpsimd.memset(mprev[:], 0.0), writes=['mprev'])
        S.op('pool', lambda: nc.gpsimd.memset(ub[:], 0.0), writes=['ub%d' % i for i in range(8)])
        S.op('pool', lambda: nc.gpsimd.memset(qhalo[:], 0.0), writes=['qhalo'])
        S.dma('sp', 'c_gh', gh[:], din['mh_norm_g'].partition_broadcast(128), writes=['gh'])
        S.dma('sp', 'c_fg', fg[:], din['final_g'].partition_broadcast(128), writes=['fg'])
        for j in range(NJ):
            S.dma('sp', 'c_ifb%d' % j, ifb[:, j, :], din['b_if'].partition_broadcast(128), writes=['ifb'])
        S.op('dve', lambda: nc.vector.tensor_scalar(out=gh[:], in0=gh[:], scalar1=0.5, scalar2=None, op0=ALU.mult),
             reads=['gh'], writes=['gh'])
        S.op('pool', lambda: nc.gpsimd.memset(rows[0][:], 0.0), writes=['rows0'])
        rowspec = [("norm_mix_g", PC_MIX), ("norm_ffn_g", PC_FFN), ("norm_ple_g", PC_PLE), ("conf_conv_b", PC_CCB),
                   ("conf_ln_g", PC_LNG), ("conf_ln_b", PC_LNB)]
        for nm, base in rowspec:
            S.dma('sp', 'c_' + nm, rows[0][base:base + 8, :], din[nm].rearrange("(i p) -> i p", p=128),
                  reads=['rows0'], writes=['rows0_' + nm])
        S.dma('sp', 'c_qkb', rows[0][PC_QKB:PC_QKB + 16, :], din['conv_qk_b'].rearrange("(i p) -> i p", p=128),
              reads=['rows0'], writes=['rows0_qkb'])
        S.dma('sp', 'c_qkw', rows[1][0:64, :], din['conv_qk_w'].rearrange("k (i p) -> (k i) p", p=128), writes=['rows1'])
        S.dma('sp', 'c_cw0', rows[2][0:128, :], din['conf_conv_w'][0:16, :].rearrange("k (i p) -> (k i) p", p=128),
              writes=['rows2'])
        S.dma('sp', 'c_cw1', rows[3][0:120, :], din['conf_conv_w'][16:31, :].rearrange("k (i p) -> (k i) p", p=128),
              writes=['rows3'])
        r0keys = ['rows0_' + nm for nm, _ in rowspec] + ['rows0_qkb']
        for gi, (R, base, keys) in enumerate([(64, 0, r0keys), (64, PC_QKW, ['rows1']), (128, PC_CW0, ['rows2']),
                                              (120, PC_CW1, ['rows3'])]):
            bt, bkey = bk(S.bank())
            S.op('pe', lambda gi=gi, R=R, bt=bt: nc.tensor.matmul(bt[:, 0:R], lhsT=rows[gi][0:R, :],
                                                               rhs=identf[0:R, 0:R], start=True, stop=True),
                 reads=keys + ['identf'], writes=[bkey])
            S.op('dve', lambda R=R, base=base, bt=bt: nc.vector.tensor_copy(out=pcol[:, base:base + R], in_=bt[:, 0:R]),
                 reads=[bkey], writes=['pcol'])

        for j in range(NJ):
            S.dma('sp', 'x%d' % j, xt[:, j, :], x[j * 128:(j + 1) * 128, :], writes=['xt%d' % j])
        for _ in range(3):
            wissue()

        S.op('dve', lambda: nc.vector.tensor_scalar(out=pcol[:, PC_CW0:PC_CW1 + 120], in0=pcol[:, PC_CW0:PC_CW1 + 120],
                                                    scalar1=0.5, scalar2=None, op0=ALU.mult), reads=['pcol'], writes=['pcol'])

        tmpi = {'i': 0}

        def tmpbuf():
            i = tmpi['i'] % 4
            tmpi['i'] += 1
            return tmp[:, i, :], 'tmp%d' % i

        dgi = {'i': 0}

        def diag(col, scale):
            i = dgi['i'] % 16
            n = dgi['i']
            dgi['i'] += 1
            if n % 3 == 2:
                S.op('pool', lambda: nc.gpsimd.tensor_scalar(out=dg[:, i, :], in0=identf[:], scalar1=pcol[:, col:col + 1],
                                                             scalar2=1.0, op0=ALU.mult, op1=ALU.mult),
                     reads=['identf', 'pcol'], writes=['dg%d' % i])
            elif n % 3 == 1:
                S.op('act', lambda: nc.scalar.activation(out=dg[:, i, :], in_=identf[:], func=AF.Identity,
                                                         scale=pcol[:, col:col + 1]),
                     reads=['identf', 'pcol'], writes=['dg%d' % i])
            else:
                S.op('dve', lambda: nc.vector.tensor_scalar(out=dg[:, i, :], in0=identf[:], scalar1=pcol[:, col:col + 1],
                                                            scalar2=None, op0=ALU.mult),
                     reads=['identf', 'pcol'], writes=['dg%d' % i])
            return dg[:, i, :], 'dg%d' % i

        evi = {'i': 0}

        def evac_copy(out, in_, reads, writes):
            evi['i'] += 1
            if evi['i'] % 2:
                S.op('act', lambda: nc.scalar.copy(out=out, in_=in_), reads=reads, writes=writes)
            else:
                S.op('dve', lambda: nc.vector.tensor_copy(out=out, in_=in_), reads=reads, writes=writes)

        def js(j):
            return slice(j * 128, (j + 1) * 128)

        class NormT:
            def __init__(self, gbase, bank_ids=None):
                self.gbase = gbase
                ids = bank_ids if bank_ids is not None else [S.bank() for _ in range(4)]
                self.nb = [bk(i) for i in ids]
                self.nbb = [b_[:].bitcast(BF16) for b_, _ in self.nb]

            def pre(self, j):
                S.op('act', lambda: nc.scalar.activation(out=sqj[:], in_=xt[:, j, :], func=AF.Square,
                                                         accum_out=ss[:, j:j + 1]),
                     reads=['xt%d' % j], writes=['ssa%d' % j])
                S.op('dve', lambda: nc.vector.tensor_scalar(out=ss[:, NJ + j:NJ + j + 1], in0=ss[:, j:j + 1],
                                                            scalar1=1.0 / D, scalar2=EPS, op0=ALU.mult, op1=ALU.add),
                     reads=['ssa%d' % j], writes=['ssb%d' % j])
                S.op('pool', lambda: nc.gpsimd.tensor_tensor(out=ss[:, 2 * NJ + j:2 * NJ + j + 1],
                                                             in0=ss[:, NJ + j:NJ + j + 1], in1=mh4[:, 0:1], op=ALU.pow),
                     reads=['ssb%d' % j, 'mh4'], writes=['ssc%d' % j])
                S.op('dve', lambda: nc.vector.tensor_scalar(out=hb[:, j, :], in0=xt[:, j, :],
                                                            scalar1=ss[:, 2 * NJ + j:2 * NJ + j + 1], scalar2=None,
                                                            op0=ALU.mult),
                     reads=['xt%d' % j, 'ssc%d' % j], writes=['hb%d' % j])

            def tr(self, j):
                for c in range(8):
                    o = (c % 2) * 512 + j * 128
                    S.op('pe', lambda: nc.tensor.transpose(out=self.nbb[c // 2][:, o:o + 128],
                                                          in_=hb[:, j, c * 128:(c + 1) * 128], identity=identb[:]),
                         reads=['hb%d' % j, 'identb'], writes=[self.nb[c // 2][1]])

            def evac(self):
                gbase = self.gbase
                for c in range(8):
                    o = (c % 2) * 512
                    bkey = self.nb[c // 2][1]
                    if (c // 2) % 2 == 0:
                        S.op('dve', lambda: nc.vector.tensor_scalar(out=hT[:, c, :], in0=self.nbb[c // 2][:, o:o + TT],
                                                                    scalar1=pcol[:, gbase + c:gbase + c + 1],
                                                                    scalar2=None, op0=ALU.mult),
                             reads=[bkey, 'pcol'], writes=['hT%d' % c])
                    else:
                        S.op('act', lambda: nc.scalar.activation(out=hT[:, c, :], in_=self.nbb[c // 2][:, o:o + TT],
                                                                 func=AF.Identity,
                                                                 scale=pcol[:, gbase + c:gbase + c + 1]),
                             reads=[bkey, 'pcol'], writes=['hT%d' % c])

        def norm_T(gbase):
            nt = NormT(gbase)
            for j in range(NJ):
                nt.pre(j)
                nt.tr(j)
            nt.evac()

        def mm8(bt, bkey, lhs_fn, rhs_fn, lkeys, n=8):
            for kc in range(n):
                S.op('pe', lambda kc=kc: nc.tensor.matmul(bt, lhsT=lhs_fn(kc), rhs=rhs_fn(kc), start=(kc == 0),
                                                         stop=(kc == n - 1)),
                     reads=lkeys, writes=[bkey])

        hbk = ['hb%d' % j for j in range(NJ)]
        hTk = ['hT%d' % c for c in range(8)]
        hmTk = ['hmT%d' % j for j in range(NJ)]
        uck = ['uc%d' % i for i in range(8)] + ['ucb0', 'ucb1', 'usq0', 'usq1']
        qkvk = ['qT%d' % c for c in range(8)] + ['kT%d' % c for c in range(8)] + ['vb%d' % j for j in range(NJ)]
        hidk = ['hid%d' % n for n in range(NFF)]
        usk = ['us%d' % i for i in range(8)]
        ppk = ['pt%d' % j for j in range(NJ)] + ['pb', 'pT']
        qpk = ['qp%d' % c for c in range(8)]
        mgk = ['mg%d' % c for c in range(8)]
        ubk = ['ub%d' % i for i in range(8)]

        dcnt = {'i': 0}

        def dump(name, dst, src, reads):
            if not debug:
                return
            dcnt['i'] += 1
            S.dma('sp', 'dbg%d' % (dcnt['i'] % 6), dst, src, reads=reads, writes=['dbgout%d' % dcnt['i']])

        for T in range(NT):
            t0 = T * TT

            S.alias(hbk, hmTk)
            norm_T(PC_MIX)

            if debug:
                dump('hT', dbg['hT'][:, t0:t0 + TT].rearrange("(c p) t -> p c t", p=128), hT[:, :, :], hTk)
            gbt, gbkey = bk(S.bank())
            for j in range(NJ):
                mm8(gbt[:, j * 8:(j + 1) * 8], gbkey, lambda kc, j=j: hT[:, kc, js(j)], lambda kc: wif[:, kc, :],
                    hTk + ['wif'])
            S.op('dve', lambda: nc.vector.tensor_tensor(out=gsb[:].rearrange("p j e -> p (j e)"), in0=gbt[:, 0:8 * NJ],
                                                        in1=ifb[:].rearrange("p j e -> p (j e)"), op=ALU.add),
                 reads=[gbkey, 'ifb'], writes=['gsb'])

            if debug:
                dump('gates', dbg['gates'][t0:t0 + TT, :].rearrange("(j p) e -> p j e", p=128), gsb[:, :, :], ['gsb'])
            S.alias(uck, hidk)
            S.op('pool', lambda: nc.gpsimd.tensor_copy(out=ub[:, :, 0:HC], in_=ub[:, :, TT:TT + HC]),
                 reads=ubk, writes=ubk)
            for hf in range(2):
                wt, wkey = wget('A%d' % hf)
                wv = wt[:, 0:8192].rearrange("p (m c n) -> p m c n", m=2, c=8)
                for il in range(4):
                    i = hf * 4 + il
                    ba, bakey = bk(S.bank())
                    bg, bgkey = bk(S.bank())
                    mm8(ba[:, 0:TT], bakey, lambda kc, il=il: wv[:, 0, kc, il * 128:(il + 1) * 128],
                        lambda kc: hT[:, kc, :], hTk + [wkey])
                    mm8(bg[:, 0:TT], bgkey, lambda kc, il=il: wv[:, 1, kc, il * 128:(il + 1) * 128],
                        lambda kc: hT[:, kc, :], hTk + [wkey])
                    tb, tkey = tmpbuf()
                    S.op('act', lambda: nc.scalar.activation(out=tb, in_=bg[:, 0:TT], func=AF.Tanh, scale=0.5),
                         reads=[bgkey], writes=[tkey])
                    S.op('dve', lambda i=i: nc.vector.scalar_tensor_tensor(out=ub[:, i, HC:HC + TT], in0=tb, scalar=1.0,
                                                                         in1=ba[:, 0:TT], op0=ALU.add, op1=ALU.mult),
                         reads=[tkey, bakey], writes=['ub%d' % i])
                wrelease()
            Mbank, Mkey = bk(6)
            Qbank, Qkey = bk(7)
            for i in range(8):
                bc, bckey = bk(S.bank())
                for k in range(31):
                    dgt, dgkey = diag(cw_col(k, i), 0.5)
                    S.op('pe', lambda k=k, i=i, dgt=dgt: nc.tensor.matmul(bc[:, 0:TT], lhsT=dgt, rhs=ub[:, i, k:k + TT],
                                                                       start=(k == 0), stop=(k == 30)),
                         reads=[dgkey, 'ub%d' % i], writes=[bckey])
                S.op('act', lambda i=i: nc.scalar.activation(out=uc[:, i, :], in_=bc[:, 0:TT], func=AF.Identity,
                                                           bias=pcol[:, PC_CCB + i:PC_CCB + i + 1]),
                     reads=[bckey, 'pcol'], writes=['uc%d' % i])
                s2 = i % 2
                S.op('dve', lambda i=i, s2=s2: nc.vector.tensor_copy(out=ucb[:, s2, :], in_=uc[:, i, :]),
                     reads=['uc%d' % i], writes=['ucb%d' % s2])
                S.op('act', lambda i=i, s2=s2: nc.scalar.activation(out=usq[:, s2, :], in_=uc[:, i, :], func=AF.Square),
                     reads=['uc%d' % i], writes=['usq%d' % s2])
                S.op('pe', lambda i=i, s2=s2: nc.tensor.matmul(Mbank[:, 0:TT], lhsT=onesm[:], rhs=ucb[:, s2, :],
                                                             start=(i == 0), stop=(i == 7)),
                     reads=['onesm', 'ucb%d' % s2], writes=[Mkey])
                S.op('pe', lambda i=i, s2=s2: nc.tensor.matmul(Qbank[:, 0:TT], lhsT=onesm[:], rhs=usq[:, s2, :],
                                                             start=(i == 0), stop=(i == 7)),
                     reads=['onesm', 'usq%d' % s2], writes=[Qkey])

            if debug:
                dump('ub', dbg['ub'][:, t0:t0 + TT].rearrange("(c p) t -> p c t", p=128), ub[:, :, HC:HC + TT], ubk)
                dump('uc', dbg['uc'][:, t0:t0 + TT].rearrange("(c p) t -> p c t", p=128), uc[:, :, :], uck)
            def qk_proj(piece):
                wt, wkey = wget('Q' if piece == 0 else 'K')
                wv = wt[:, 0:8192].rearrange("p (c n) -> p c n", c=8)
                S.alias(qpk, mgk)
                for c in range(8):
                    bq, bqkey = bk(S.bank())
                    mm8(bq[:, 0:TT], bqkey, lambda kc, c=c: wv[:, kc, c * 128:(c + 1) * 128], lambda kc: hT[:, kc, :],
                        hTk + [wkey])
                    evac_copy(qp[:, c, HQ:HQ + TT], bq[:, 0:TT], [bqkey], ['qp%d' % c])
                wrelease()
                S.op('pool', lambda: nc.gpsimd.tensor_copy(out=qp[:, :, 0:HQ], in_=qhalo[:, piece * 8:(piece + 1) * 8, :]),
                     reads=['qhalo'] + qpk, writes=qpk)
                S.op('pool', lambda: nc.gpsimd.tensor_copy(out=qhalo[:, piece * 8:(piece + 1) * 8, :],
                                                           in_=qp[:, :, TT:TT + HQ]),
                     reads=qpk, writes=['qhalo'])

            def qk_conv(piece):
                dstT, dkey = (qT, 'qT') if piece == 0 else (kT, 'kT')
                for c in range(8):
                    bq, bqkey = bk(S.bank())
                    cc = piece * 8 + c
                    for k in range(4):
                        dgt, dgkey = diag(PC_QKW + k * 16 + cc, 1.0)
                        S.op('pe', lambda: nc.tensor.matmul(bq[:, 0:TT], lhsT=dgt, rhs=qp[:, c, k:k + TT], start=(k == 0),
                                                            stop=(k == 3)),
                             reads=[dgkey, 'qp%d' % c], writes=[bqkey])
                    S.op('act', lambda: nc.scalar.activation(out=dstT[:, c, :], in_=bq[:, 0:TT], func=AF.Silu,
                                                             bias=pcol[:, PC_QKB + cc:PC_QKB + cc + 1]),
                         reads=[bqkey, 'pcol'], writes=['%s%d' % (dkey, c)])

            qk_proj(0)

            S.alias(['vb%d' % j for j in range(NJ)], ['ucb0', 'ucb1', 'usq0', 'usq1'])
            wt, wkey = wget('V')
            wv = wt[:, 0:8192].rearrange("p (c n) -> p c n", c=8)
            for j in range(NJ):
                for hf in range(2):
                    bv, bvkey = bk(S.bank())
                    mm8(bv[:, :], bvkey, lambda kc, j=j: hT[:, kc, js(j)], lambda kc, hf=hf: wv[:, kc, hf * 512:(hf + 1) * 512],
                        hTk + [wkey])
                    evac_copy(vb[:, j, hf * 512:(hf + 1) * 512], bv[:, :], [bvkey], ['vb%d' % j])
            wrelease()

            msq, lnv, rstd, mr = lnst[:, 0, :], lnst[:, 1, :], lnst[:, 2, :], lnst[:, 3, :]
            S.op('act', lambda: nc.scalar.activation(out=msq, in_=Mbank[:, 0:TT], func=AF.Square),
                 reads=[Mkey], writes=['ln0'])
            S.op('dve', lambda: nc.vector.tensor_tensor(out=msq, in0=Qbank[:, 0:TT], in1=msq, op=ALU.subtract),
                 reads=[Qkey, 'ln0'], writes=['ln0'])
            S.op('act', lambda: nc.scalar.activation(out=lnv, in_=msq, func=AF.Ln, bias=EPS), reads=['ln0'], writes=['ln1'])
            S.op('act', lambda: nc.scalar.activation(out=rstd, in_=lnv, func=AF.Exp, scale=-0.5),
                 reads=['ln1'], writes=['ln2'])
            S.op('dve', lambda: nc.vector.tensor_tensor(out=mr, in0=Mbank[:, 0:TT], in1=rstd, op=ALU.mult),
                 reads=[Mkey, 'ln2'], writes=['ln3'])
            G = 4 * NJ

            def g3(t):
                return t.rearrange("p (j h) -> p j h", j=NJ)
            zf = gsb[:, :, 4:8]
            igf = gsb[:, :, 0:4]
            t_az, t_e, t_l, t_mn, t_lf, t_wa, t_ta = [gt[:, i, :] for i in range(7)]
            S.op('dve', lambda: nc.vector.scalar_tensor_tensor(out=g3(t_az), in0=zf, scalar=-1.0, in1=zf, op0=ALU.mult,
                                                               op1=ALU.max), reads=['gsb'], writes=['gt0'])
            S.op('act', lambda: nc.scalar.activation(out=t_e, in_=t_az, func=AF.Exp, scale=-1.0), reads=['gt0'], writes=['gt1'])
            S.op('act', lambda: nc.scalar.activation(out=t_l, in_=t_e, func=AF.Ln, bias=1.0), reads=['gt1'], writes=['gt2'])
            S.op('dve', lambda: nc.vector.tensor_scalar(out=g3(t_mn), in0=zf, scalar1=0.0, scalar2=None, op0=ALU.min),
                 reads=['gsb'], writes=['gt3'])
            S.op('dve', lambda: nc.vector.tensor_tensor(out=t_lf, in0=t_mn, in1=t_l, op=ALU.subtract),
                 reads=['gt3', 'gt2'], writes=['gt4'])
            g1, g1key = bk(S.bank())
            S.op('pe', lambda: nc.tensor.matmul(g1[:, 0:G], lhsT=tri[:], rhs=t_lf, start=True, stop=True),
                 reads=['tri', 'gt4'], writes=[g1key])
            S.op('pe', lambda: nc.tensor.matmul(g1[:, G:2 * G], lhsT=onesf[:], rhs=t_lf, start=True, stop=True),
                 reads=['onesf', 'gt4'], writes=[g1key])
            S.op('dve', lambda: nc.vector.tensor_copy(out=bsb[:, 0:2 * G], in_=g1[:, 0:2 * G]), reads=[g1key], writes=['bsb'])
            S.op('dve', lambda: nc.vector.tensor_tensor(out=g3(ab[:, :]), in0=igf, in1=g3(bsb[:, 0:G]), op=ALU.subtract),
                 reads=['gsb', 'bsb'], writes=['ab'])
            g2, g2key = bk(S.bank())
            S.op('pe', lambda: nc.tensor.matmul(g2[0:G, 0:128], lhsT=ab[:, :], rhs=identf[:], start=True, stop=True),
                 reads=['ab', 'identf'], writes=[g2key])
            S.op('dve', lambda: nc.vector.tensor_reduce(out=amx[0:G, 0:1], in_=g2[0:G, 0:128], axis=AX.X, op=ALU.max),
                 reads=[g2key], writes=['amx'])
            S.op('dve', lambda: nc.vector.tensor_scalar(out=dgm[0:G, 0:G], in0=identf[0:G, 0:G], scalar1=amx[0:G, 0:1],
                                                        scalar2=None, op0=ALU.mult), reads=['amx', 'identf'], writes=['dgm'])
            S.op('pe', lambda: nc.tensor.matmul(g2[:, 128:128 + G], lhsT=onesf[0:G, :], rhs=dgm[0:G, 0:G], start=True,
                                                stop=True), reads=['onesf', 'dgm'], writes=[g2key])
            S.op('dve', lambda: nc.vector.tensor_copy(out=amaxb[:, :], in_=g2[:, 128:128 + G]), reads=[g2key],
                 writes=['amaxb'])
            for j in range(NJ):
                hs = slice(4 * j, 4 * j + 4)
                S.op('dve', lambda hs=hs: nc.vector.tensor_tensor(out=Mb[:, hs], in0=mprev[:, :], in1=amaxb[:, hs], op=ALU.max),
                     reads=['mprev', 'amaxb'], writes=['Mb'])
                S.op('dve', lambda hs=hs: nc.vector.tensor_tensor(out=dmb[:, hs], in0=mprev[:, :], in1=Mb[:, hs],
                                                                op=ALU.subtract), reads=['mprev', 'Mb'], writes=['dmb'])
                S.op('dve', lambda hs=hs, j=j: nc.vector.tensor_tensor(out=mprev[:, :], in0=bsb[:, G + 4 * j:G + 4 * j + 4],
                                                                     in1=Mb[:, hs], op=ALU.add),
                     reads=['bsb', 'Mb'], writes=['mprev'])
            S.op('dve', lambda: nc.vector.tensor_tensor(out=t_wa, in0=ab[:, :], in1=Mb[:, :], op=ALU.subtract),
                 reads=['ab', 'Mb'], writes=['gt5'])
            S.op('act', lambda: nc.scalar.activation(out=wkb[:, :], in_=t_wa, func=AF.Exp), reads=['gt5'], writes=['wkb'])
            S.op('act', lambda: nc.scalar.activation(out=decb[:, :], in_=dmb[:, :], func=AF.Exp), reads=['dmb'], writes=['decb'])
            S.op('dve', lambda: nc.vector.tensor_scalar(out=dec16b[:, :], in0=decb[:, :], scalar1=1.0 / 16, scalar2=None,
                                                        op0=ALU.mult), reads=['decb'], writes=['dec16b'])
            S.op('dve', lambda: nc.vector.tensor_tensor(out=t_ta, in0=bsb[:, 0:G], in1=Mb[:, :], op=ALU.add),
                 reads=['bsb', 'Mb'], writes=['gt6'])
            S.op('act', lambda: nc.scalar.activation(out=thrb[:, :], in_=t_ta, func=AF.Exp, scale=-1.0),
                 reads=['gt6'], writes=['thrb'])
            S.alias(usk, ppk)
            for i in range(8):
                ta, takey = tmpbuf()
                S.op('dve', lambda i=i, ta=ta: nc.vector.tensor_tensor(out=ta, in0=uc[:, i, :], in1=rstd, op=ALU.mult),
                     reads=['uc%d' % i, 'ln2'], writes=[takey])
                S.op('pool', lambda ta=ta: nc.gpsimd.tensor_tensor(out=ta, in0=ta, in1=mr, op=ALU.subtract),
                     reads=[takey, 'ln3'], writes=[takey])
                S.op('act', lambda i=i, ta=ta: nc.scalar.activation(out=usT[:, i, :], in_=ta, func=AF.Silu,
                                                                  scale=pcol[:, PC_LNG + i:PC_LNG + i + 1],
                                                                  bias=pcol[:, PC_LNB + i:PC_LNB + i + 1]),
                     reads=[takey, 'pcol'], writes=['us%d' % i])

            if debug:
                dump('us', dbg['us'][:, t0:t0 + TT].rearrange("(c p) t -> p c t", p=128), usT[:, :, :], usk)
                G_ = 4 * NJ
                for ii, (tt_, kk_) in enumerate([(bsb[:, 0:G_], 'bsb'), (ab[:, :], 'ab'), (Mb[:, :], 'Mb'), (wkb[:, :], 'wkb'),
                                                 (decb[:, :], 'decb'), (thrb[:, :], 'thrb')]):
                    dump('gsm', dbg['gsm'][T * 128:(T + 1) * 128, ii * G_:(ii + 1) * G_], tt_, [kk_])
            S.alias(qkvk, uck)
            qk_conv(0)
            qk_proj(1)
            qk_conv(1)

            if debug:
                dump('q', dbg['q'][:, t0:t0 + TT].rearrange("(c p) t -> p c t", p=128), qT[:, :, :], qkvk)
                dump('k', dbg['k'][:, t0:t0 + TT].rearrange("(c p) t -> p c t", p=128), kT[:, :, :], qkvk)
                dump('v', dbg['v'][t0:t0 + TT, :].rearrange("(j p) d -> p j d", p=128), vb[:, :, :], qkvk)
            wtO, wOkey = wget('O')
            wO = wtO[:, 0:8192].rearrange("p (c n) -> p c n", c=8)
            S.alias(hmTk, hbk)

            def hm_transposes(jj):
                ss2 = jj % 2
                bH, bHkey = bk(7)
                bHb = bH[:].bitcast(BF16)
                for c in range(8):
                    S.op('pe', lambda: nc.tensor.transpose(out=bHb[:, c * 128:(c + 1) * 128],
                                                          in_=hm[:, ss2, c * 128:(c + 1) * 128], identity=identb[:]),
                         reads=['hm%d' % ss2, 'identb'], writes=[bHkey])
                evac_copy(hmT[:, :, js(jj)], bHb[:, 0:1024].rearrange("p (c t) -> p c t", c=8), [bHkey], ['hmT%d' % jj])

            for j in range(NJ):
                s2 = j % 2
                gi = 4 * j
                for h in range(4):
                    S.op('pool', lambda: nc.gpsimd.tensor_scalar(
                        out=Cd[:, h, :, :].rearrange("p a e -> p (a e)"), in0=Cst[:, h, :, :].rearrange("p a e -> p (a e)"),
                        scalar1=dec16b[:, gi + h:gi + h + 1], scalar2=1.0, op0=ALU.mult, op1=ALU.mult),
                        reads=['C%d' % h, 'dec16b'], writes=['Cd%d' % h])
                    S.op('pool', lambda: nc.gpsimd.tensor_scalar(
                        out=ndb[:, 2 * h:2 * h + 2], in0=nst[:, 2 * h:2 * h + 2], scalar1=dec16b[:, gi + h:gi + h + 1],
                        scalar2=1.0, op0=ALU.mult, op1=ALU.mult), reads=['nst', 'dec16b'], writes=['ndb'])
                bT, bTkey = bk(0)
                bTb = bT[:].bitcast(BF16)
                for c in range(8):
                    S.op('pe', lambda: nc.tensor.transpose(out=bTb[:, c * 128:(c + 1) * 128], in_=kT[:, c, js(j)],
                                                          identity=identb[:]),
                         reads=['kT%d' % c, 'identb'], writes=[bTkey])
                for h in range(4):
                    S.op('dve', lambda: nc.vector.tensor_scalar(out=kw[:, s2, h, :], in0=bTb[:, h * 256:(h + 1) * 256],
                                                                scalar1=wkb[:, gi + h:gi + h + 1], scalar2=None, op0=ALU.mult),
                         reads=[bTkey, 'wkb'], writes=['kw%d' % s2])
                bS, bSkey = bk(1)
                for h in range(4):
                    for a in range(2):
                        S.op('pe', lambda: nc.tensor.matmul(bS[:, h * 128:(h + 1) * 128], lhsT=kT[:, 2 * h + a, js(j)],
                                                            rhs=qT[:, 2 * h + a, js(j)], start=(a == 0), stop=(a == 1)),
                             reads=['kT%d' % (2 * h + a), 'qT%d' % (2 * h + a)], writes=[bSkey])
                for h in range(4):
                    S.op('dve', lambda: nc.vector.scalar_tensor_tensor(out=PTt[:, s2, h, :], in0=bS[:, h * 128:(h + 1) * 128],
                                                                       scalar=wkb[:, gi + h:gi + h + 1], in1=mask16[:],
                                                                       op0=ALU.mult, op1=ALU.mult),
                         reads=[bSkey, 'wkb', 'mask16'], writes=['PT%d' % s2])
                thj = tho[:, s2, :]
                for hf in range(2):
                    bo, bokey = bk(5 + hf)
                    mm8(bo[:, :], bokey, lambda kc: hT[:, kc, js(j)], lambda kc: wO[:, kc, hf * 512:(hf + 1) * 512],
                        hTk + [wOkey])
                    S.op('act', lambda: nc.scalar.activation(out=thj[:, hf * 512:(hf + 1) * 512], in_=bo[:, :],
                                                             func=AF.Tanh, scale=0.5),
                         reads=[bokey], writes=['tho%d' % s2])
                S.op('dve', lambda: nc.vector.scalar_tensor_tensor(out=thj, in0=thj, scalar=1.0, in1=gh[:], op0=ALU.add,
                                                                   op1=ALU.mult), reads=['tho%d' % s2, 'gh'], writes=['tho%d' % s2])
                if j > 0:
                    hm_transposes(j - 1)
                bN = [bk(2), bk(3)]
                bD, bDkey = bk(4)
                for h in range(4):
                    bn, bnkey = bN[h // 2]
                    o = (h % 2) * 256
                    S.op('pe', lambda: nc.tensor.matmul(bn[:, o:o + 256], lhsT=PTt[:, s2, h, :],
                                                        rhs=vb[:, j, h * 256:(h + 1) * 256], start=True, stop=False),
                         reads=['PT%d' % s2, 'vb%d' % j], writes=[bnkey])
                    for a in range(2):
                        S.op('pe', lambda: nc.tensor.matmul(bn[:, o:o + 256], lhsT=qT[:, 2 * h + a, js(j)],
                                                            rhs=Cd[:, h, a, :], start=False, stop=(a == 1)),
                             reads=['qT%d' % (2 * h + a), 'Cd%d' % h], writes=[bnkey])
                    S.op('pe', lambda: nc.tensor.matmul(bD[:, h:h + 1], lhsT=PTt[:, s2, h, :], rhs=onescol[:, 0:1],
                                                        start=True, stop=False),
                         reads=['PT%d' % s2, 'onescol'], writes=[bDkey])
                    for a in range(2):
                        S.op('pe', lambda: nc.tensor.matmul(bD[:, h:h + 1], lhsT=qT[:, 2 * h + a, js(j)],
                                                            rhs=ndb[:, 2 * h + a:2 * h + a + 1], start=False, stop=(a == 1)),
                             reads=['qT%d' % (2 * h + a), 'ndb'], writes=[bDkey])
                bU = [bk(0), bk(1), bk(5), bk(6)]
                for h in range(4):
                    bu, bukey = bU[h]
                    for a in range(2):
                        S.op('pe', lambda: nc.tensor.matmul(bu[:, a * 256:(a + 1) * 256],
                                                            lhsT=kw[:, s2, h, a * 128:(a + 1) * 128],
                                                            rhs=vb[:, j, h * 256:(h + 1) * 256], start=True, stop=True),
                             reads=['kw%d' % s2, 'vb%d' % j], writes=[bukey])
                        S.op('pe', lambda: nc.tensor.matmul(bD[:, 8 + 2 * h + a:9 + 2 * h + a],
                                                            lhsT=kw[:, s2, h, a * 128:(a + 1) * 128], rhs=onescol[:, 0:1],
                                                            start=True, stop=True),
                             reads=['kw%d' % s2, 'onescol'], writes=[bDkey])
                for h in range(4):
                    bu, bukey = bU[h]
                    S.op('dve', lambda: nc.vector.scalar_tensor_tensor(
                        out=Cst[:, h, :, :].rearrange("p a e -> p (a e)"), in0=Cst[:, h, :, :].rearrange("p a e -> p (a e)"),
                        scalar=decb[:, gi + h:gi + h + 1], in1=bu[:, 0:512], op0=ALU.mult, op1=ALU.add),
                        reads=['C%d' % h, 'decb', bukey], writes=['C%d' % h])
                    S.op('dve', lambda: nc.vector.scalar_tensor_tensor(
                        out=nst[:, 2 * h:2 * h + 2], in0=nst[:, 2 * h:2 * h + 2], scalar=decb[:, gi + h:gi + h + 1],
                        in1=bD[:, 8 + 2 * h:10 + 2 * h], op0=ALU.mult, op1=ALU.add),
                        reads=['nst', 'decb', bDkey], writes=['nst'])
                sm = osm[:, s2, :]
                for h in range(4):
                    bn, bnkey = bN[h // 2]
                    o = (h % 2) * 256
                    S.op('act', lambda: nc.scalar.activation(out=sqj[:, 0:256], in_=bn[:, o:o + 256], func=AF.Square,
                                                             accum_out=sm[:, h:h + 1]),
                         reads=[bnkey], writes=['osm%d' % s2])
                S.op('dve', lambda: nc.vector.tensor_copy(out=sm[:, 4:8], in_=bD[:, 0:4]), reads=[bDkey], writes=['osm%d' % s2])
                S.op('dve', lambda: nc.vector.scalar_tensor_tensor(out=sm[:, 8:12], in0=sm[:, 4:8], scalar=-1.0, in1=sm[:, 4:8],
                                                                   op0=ALU.mult, op1=ALU.max),
                     reads=['osm%d' % s2], writes=['osm%d' % s2])
                S.op('dve', lambda: nc.vector.tensor_tensor(out=sm[:, 8:12], in0=sm[:, 8:12], in1=thrb[:, gi:gi + 4], op=ALU.max),
                     reads=['osm%d' % s2, 'thrb'], writes=['osm%d' % s2])
                S.op('dve', lambda: nc.vector.scalar_tensor_tensor(out=sm[:, 12:16], in0=sm[:, 8:12], scalar=EPS, in1=sm[:, 8:12],
                                                                   op0=ALU.mult, op1=ALU.mult),
                     reads=['osm%d' % s2], writes=['osm%d' % s2])
                S.op('dve', lambda: nc.vector.scalar_tensor_tensor(out=sm[:, 16:20], in0=sm[:, 0:4], scalar=1.0 / 256,
                                                                   in1=sm[:, 12:16], op0=ALU.mult, op1=ALU.add),
                     reads=['osm%d' % s2], writes=['osm%d' % s2])
                S.op('pool', lambda: nc.gpsimd.tensor_tensor(out=sm[:, 20:24], in0=sm[:, 16:20], in1=mh4[:, 0:4], op=ALU.pow),
                     reads=['osm%d' % s2, 'mh4'], writes=['osm%d' % s2])
                for h in range(4):
                    bn, bnkey = bN[h // 2]
                    o = (h % 2) * 256
                    S.op('dve', lambda: nc.vector.scalar_tensor_tensor(
                        out=hm[:, s2, h * 256:(h + 1) * 256], in0=bn[:, o:o + 256], scalar=sm[:, 20 + h:21 + h],
                        in1=thj[:, h * 256:(h + 1) * 256], op0=ALU.mult, op1=ALU.mult),
                        reads=[bnkey, 'osm%d' % s2, 'tho%d' % s2], writes=['hm%d' % s2])
                if debug:
                    dump('hm', dbg['hm'][t0 + j * 128:t0 + (j + 1) * 128, :], hm[:, s2, :], ['hm%d' % s2])
            wrelease()

            S.alias(mgk, qpk)
            for n in range(4):
                wt, wkey = wget('M%d' % n)
                wv = wt[:, 0:8192].rearrange("p (m c n) -> p m c n", m=4, c=8)
                for sub in range(2):
                    c = 2 * n + sub
                    cs = slice(sub * 128, (sub + 1) * 128)
                    b_gm, k_gm = bk(S.bank())
                    b_gc, k_gc = bk(S.bank())
                    b_bm, k_bm = bk(S.bank())
                    b_bc, k_bc = bk(S.bank())
                    mm8(b_gm[:, 0:TT], k_gm, lambda kc: wv[:, 0, kc, cs], lambda kc: hT[:, kc, :], hTk + [wkey])
                    mm8(b_gc[:, 0:TT], k_gc, lambda kc: wv[:, 1, kc, cs], lambda kc: hT[:, kc, :], hTk + [wkey])
                    if c == 0:
                        hm_transposes(NJ - 1)
                    t1, t1k = tmpbuf()
                    t2, t2k = tmpbuf()
                    S.op('act', lambda: nc.scalar.activation(out=t1, in_=b_gm[:, 0:TT], func=AF.Tanh, scale=0.5),
                         reads=[k_gm], writes=[t1k])
                    S.op('act', lambda: nc.scalar.activation(out=t2, in_=b_gc[:, 0:TT], func=AF.Tanh, scale=0.5),
                         reads=[k_gc], writes=[t2k])
                    mm8(b_bm[:, 0:TT], k_bm, lambda kc: wv[:, 2, kc, cs], lambda kc: hmT[:, kc, :], hmTk + [wkey])
                    mm8(b_bc[:, 0:TT], k_bc, lambda kc: wv[:, 3, kc, cs], lambda kc: usT[:, kc, :], usk + [wkey])
                    S.op('dve', lambda: nc.vector.scalar_tensor_tensor(out=t1, in0=t1, scalar=1.0, in1=b_bm[:, 0:TT],
                                                                       op0=ALU.add, op1=ALU.mult),
                         reads=[t1k, k_bm], writes=[t1k])
                    S.op('dve', lambda: nc.vector.scalar_tensor_tensor(out=t2, in0=t2, scalar=1.0, in1=b_bc[:, 0:TT],
                                                                       op0=ALU.add, op1=ALU.mult),
                         reads=[t2k, k_bc], writes=[t2k])
                    S.op('pool', lambda c=c: nc.gpsimd.tensor_tensor(out=mgT[:, c, :], in0=t1, in1=t2, op=ALU.add),
                         reads=[t1k, t2k], writes=['mg%d' % c])
                wrelease()

            if debug:
                dump('mg', dbg['mg'][:, t0:t0 + TT].rearrange("(c p) t -> p c t", p=128), mgT[:, :, :], mgk)
            wt, wkey = wget('WO')
            wv = wt[:, 0:8192].rearrange("p (c n) -> p c n", c=8)
            S.alias(hbk, hmTk)
            nt = NormT(PC_FFN, bank_ids=[2, 3, 4, 5])
            for j in range(NJ):
                for hf in range(2):
                    bo, bokey = bk(hf)
                    mm8(bo[:, :], bokey, lambda kc: mgT[:, kc, js(j)], lambda kc, hf=hf: wv[:, kc, hf * 512:(hf + 1) * 512],
                        mgk + [wkey])
                    S.op('dve', lambda hf=hf, bo=bo: nc.vector.scalar_tensor_tensor(
                        out=xt[:, j, hf * 512:(hf + 1) * 512], in0=bo[:, :], scalar=0.5, in1=xt[:, j, hf * 512:(hf + 1) * 512],
                        op0=ALU.mult, op1=ALU.add), reads=[bokey, 'xt%d' % j], writes=['xt%d' % j])
                nt.pre(j)
                if j > 0:
                    nt.tr(j - 1)
            nt.tr(NJ - 1)
            nt.evac()
            wrelease()

            if debug:
                dump('x1', dbg['x1'][t0:t0 + TT, :].rearrange("(j p) d -> p j d", p=128), xt[:, :, :], ['xt%d' % j for j in range(NJ)])
            S.alias(ppk, usk)
            for j in range(NJ):
                S.dma('sp', 'pld%d' % j, pt[:, j, :], p_in[t0 + j * 128:t0 + (j + 1) * 128, :], writes=['pt%d' % j])
            S.op('pool', lambda: nc.gpsimd.tensor_copy(out=pb[:, :, :], in_=pt[:, :, :]), reads=['pt%d' % j for j in range(NJ)],
                 writes=['pb'])

            S.alias(hidk, qkvk)
            for f in range(6):
                wt, wkey = wget('F%d' % f)
                ncl = 4 if f < 5 else 2
                wv = wt[:, 0:2 * 8 * ncl * 128].rearrange("p (m c n) -> p m c n", m=2, c=8)
                for il in range(ncl):
                    n = 4 * f + il
                    cs = slice(il * 128, (il + 1) * 128)
                    b_g, k_g = bk(S.bank())
                    b_u, k_u = bk(S.bank())
                    mm8(b_g[:, 0:TT], k_g, lambda kc: wv[:, 0, kc, cs], lambda kc: hT[:, kc, :], hTk + [wkey])
                    mm8(b_u[:, 0:TT], k_u, lambda kc: wv[:, 1, kc, cs], lambda kc: hT[:, kc, :], hTk + [wkey])
                    t1, t1k = tmpbuf()
                    S.op('act', lambda: nc.scalar.activation(out=t1, in_=b_g[:, 0:TT], func=AF.Silu), reads=[k_g], writes=[t1k])
                    S.op('dve', lambda n=n: nc.vector.tensor_tensor(out=hid[:, n, :], in0=t1, in1=b_u[:, 0:TT], op=ALU.mult),
                         reads=[t1k, k_u], writes=['hid%d' % n])
                wrelease()
            for b in range(4):
                wt, wkey = wget('D%d' % b)
                wv = wt[:, 0:NFF * 256].rearrange("p (c n) -> p c n", c=NFF)
                for j in range(NJ):
                    bd, bdkey = bk(S.bank())
                    mm8(bd[:, 0:256], bdkey, lambda kc: hid[:, kc, js(j)], lambda kc: wv[:, kc, :], hidk + [wkey], n=NFF)
                    S.op('dve', lambda b=b, bd=bd: nc.vector.tensor_tensor(out=xt[:, j, b * 256:(b + 1) * 256], in0=bd[:, 0:256],
                                                                         in1=xt[:, j, b * 256:(b + 1) * 256], op=ALU.add),
                         reads=[bdkey, 'xt%d' % j], writes=['xt%d' % j])
                wrelease()

            if debug:
                dump('x2', dbg['x2'][t0:t0 + TT, :].rearrange("(j p) d -> p j d", p=128), xt[:, :, :], ['xt%d' % j for j in range(NJ)])
            bp_, bpkey = bk(S.bank())
            bpb = bp_[:].bitcast(BF16)
            for c in range(2):
                for j in range(NJ):
                    o = c * 512 + j * 128
                    S.op('pe', lambda: nc.tensor.transpose(out=bpb[:, o:o + 128], in_=pb[:, j, c * 128:(c + 1) * 128],
                                                          identity=identb[:]),
                         reads=['pb', 'identb'], writes=[bpkey])
            evac_copy(pT[:, :, :].rearrange("p c t -> p (c t)"), bpb[:, 0:1024], [bpkey], ['pT'])

            norm_T(PC_PLE)
            wtg, wgkey = wget('PG')
            wtp, wpkey = wget('PP')
            wg = wtg[:, 0:8192].rearrange("p (c n) -> p c n", c=8)
            wp = wtp[:, 0:2048].rearrange("p (c n) -> p c n", c=2)
            for j in range(NJ):
                for hf in range(2):
                    b_g, k_g = bk(S.bank())
                    b_p, k_p = bk(S.bank())
                    mm8(b_g[:, :], k_g, lambda kc: hT[:, kc, js(j)], lambda kc, hf=hf: wg[:, kc, hf * 512:(hf + 1) * 512],
                        hTk + [wgkey])
                    mm8(b_p[:, :], k_p, lambda kc: pT[:, kc, js(j)], lambda kc, hf=hf: wp[:, kc, hf * 512:(hf + 1) * 512],
                        ['pT', wpkey], n=2)
                    t1, t1k = tmpbuf()
                    S.op('act', lambda: nc.scalar.activation(out=t1, in_=b_g[:, :], func=AF.Tanh, scale=0.5),
                         reads=[k_g], writes=[t1k])
                    S.op('dve', lambda: nc.vector.scalar_tensor_tensor(out=t1, in0=t1, scalar=1.0, in1=b_p[:, :], op0=ALU.add,
                                                                       op1=ALU.mult), reads=[t1k, k_p], writes=[t1k])
                    S.op('dve', lambda hf=hf: nc.vector.scalar_tensor_tensor(
                        out=xt[:, j, hf * 512:(hf + 1) * 512], in0=t1, scalar=0.5, in1=xt[:, j, hf * 512:(hf + 1) * 512],
                        op0=ALU.mult, op1=ALU.add), reads=[t1k, 'xt%d' % j], writes=['xt%d' % j])
                if debug:
                    dump('x3', dbg['x3'][t0 + j * 128:t0 + (j + 1) * 128, :], xt[:, j, :], ['xt%d' % j])
                s2 = j % 2
                S.op('act', lambda: nc.scalar.activation(out=sqj[:], in_=xt[:, j, :], func=AF.Square,
                                                         accum_out=ss[:, j:j + 1]), reads=['xt%d' % j], writes=['ssa%d' % j])
                S.op('dve', lambda: nc.vector.tensor_scalar(out=ss[:, NJ + j:NJ + j + 1], in0=ss[:, j:j + 1], scalar1=1.0 / D,
                                                            scalar2=EPS, op0=ALU.mult, op1=ALU.add),
                     reads=['ssa%d' % j], writes=['ssb%d' % j])
                S.op('pool', lambda: nc.gpsimd.tensor_tensor(out=ss[:, 2 * NJ + j:2 * NJ + j + 1], in0=ss[:, NJ + j:NJ + j + 1],
                                                             in1=mh4[:, 0:1], op=ALU.pow),
                     reads=['ssb%d' % j, 'mh4'], writes=['ssc%d' % j])
                S.op('dve', lambda: nc.vector.scalar_tensor_tensor(
                    out=tho[:, s2, :], in0=xt[:, j, :], scalar=ss[:, 2 * NJ + j:2 * NJ + j + 1], in1=fg[:], op0=ALU.mult,
                    op1=ALU.mult), reads=['xt%d' % j, 'ssc%d' % j, 'fg'], writes=['tho%d' % s2])
                S.dma('sp', 'o%d' % s2, y[t0 + j * 128:t0 + (j + 1) * 128, :], tho[:, s2, :], reads=['tho%d' % s2],
                      writes=['y%d_%d' % (T, j)])
                if T + 1 < NT:
                    S.dma('sp', 'x%d' % j, xt[:, j, :], x[t0 + TT + j * 128:t0 + TT + (j + 1) * 128, :], writes=['xt%d' % j])
            wrelease()
            wrelease()
        S.finish('sp', ['y%d_%d' % (T, j) for T in range(NT) for j in range(NJ)] + ['dbgout%d' % (i + 1) for i in range(dcnt['i'])])
        print("sched: waits=%d pe=%d act=%d dve=%d pool=%d" % (S.nwaits, S.cnt['pe'], S.cnt['act'], S.cnt['dve'], S.cnt['pool']))
    return nc


_PROGRAM = {}


def kernel(**inputs):
    f32 = lambda a: np.ascontiguousarray(np.asarray(a, dtype=np.float32))
    x = f32(inputs['x'])
    p = f32(inputs['p'])[0]
    shared = {
        'w_in': f32(inputs['w_in'][0]), 'w_branch_m': f32(inputs['w_branch_m'][0]),
        'w_branch_c': f32(inputs['w_branch_c'][0]), 'w_out': f32(inputs['w_out'][0]),
        'w_ffn_gate': f32(inputs['w_ffn_gate'][0]), 'w_ffn_up': f32(inputs['w_ffn_up'][0]),
        'w_ffn_down': f32(inputs['w_ffn_down'][0]), 'w_ple_gate': f32(inputs['w_ple_gate'][0]),
        'w_ple_proj': f32(inputs['w_ple_proj'][0]),
        'norm_mix_g': f32(inputs['norm_mix_g'][0]), 'mh_norm_g': f32(inputs['mh_norm_g'][0]),
        'conf_conv_b': f32(inputs['conf_conv_b'][0]), 'conf_ln_g': f32(inputs['conf_ln_g'][0]),
        'conf_ln_b': f32(inputs['conf_ln_b'][0]), 'norm_ffn_g': f32(inputs['norm_ffn_g'][0]),
        'norm_ple_g': f32(inputs['norm_ple_g'][0]), 'final_g': f32(inputs['final_g']),
        'b_if': f32(inputs['b_if'][0]), 'conv_qk_w': f32(inputs['conv_qk_w'][0]),
        'conv_qk_b': f32(inputs['conv_qk_b'][0]), 'conf_conv_w': f32(inputs['conf_conv_w'][0]),
    }
    if 'nc' not in _PROGRAM:
        _PROGRAM['nc'] = build_program()
    nc = _PROGRAM['nc']
    in_maps = []
    for b in range(NCORES):
        m = dict(shared)
        m['x'] = np.ascontiguousarray(x[b])
        m['p'] = np.ascontiguousarray(p[b])
        in_maps.append(m)
    res = run_bass_kernel_spmd(nc, in_maps, core_ids=list(range(NCORES)))
    out = np.stack([np.asarray(res.results[b]['y'], dtype=np.float32) for b in range(NCORES)], axis=0)
    return out
```

```python
import contextlib
import numpy as np
import concourse.bass as bass
import concourse.mybir as mybir
from concourse.bass_utils import run_bass_kernel_spmd

F32 = mybir.dt.float32
BF16 = mybir.dt.bfloat16
AF = mybir.ActivationFunctionType
ALU = mybir.AluOpType
AX = mybir.AxisListType

SEQ = 2048
D = 1024
NIN = 8200
DFF = 2816
NFF = DFF // 128
PLE = 256
TT = 512
NJ = TT // 128
NT = SEQ // TT
EPS = 1e-6
HQ = 3
HC = 30
NCORES = 8


class Sched:
    def __init__(self, nc, es, needed=None):
        self.nc = nc
        self.es = es
        self.needed = needed
        self.waited = set()
        self.inc = {}
        self.semval = {}
        self.eng = {'pe': nc.tensor, 'act': nc.scalar, 'dve': nc.vector, 'pool': nc.gpsimd, 'sp': nc.sync}
        self.sems = {}
        self.cnt = {}
        for e in ['pe', 'act', 'dve', 'pool']:
            self.sems[e] = es.enter_context(nc.semaphore('s_' + e))
            self.cnt[e] = 0
        self.clock = {e: {} for e in self.eng}
        self.snap = {}
        self.lastw = {}
        self.readers = {}
        self.nwaits = 0
        self.nbank = 0

    def bank(self):
        b = self.nbank % 6
        self.nbank += 1
        return b

    def _deps(self, reads, writes):
        deps = {}

        def add(k, v):
            if deps.get(k, 0) < v:
                deps[k] = v
        for r in reads:
            if r in self.lastw:
                add(*self.lastw[r])
        for w in writes:
            if w in self.lastw:
                add(*self.lastw[w])
            for k, v in self.readers.get(w, {}).items():
                add(k, v)
        return deps

    def _wait(self, e, deps):
        ck = self.clock[e]
        for k, v in sorted(deps.items(), key=lambda kv: -kv[1]):
            if e == 'pe' and k == 'pe':
                continue
            if ck.get(k, 0) >= v:
                continue
            self.waited.add((k, v))
            if k.startswith('dq'):
                val = 16 * v
            elif self.needed is None:
                val = v
            else:
                val = self.semval[(k, v)]
            self.eng[e].wait_ge(self.sems[k], val)
            self.nwaits += 1
            for kk, vv in self.snap.get((k, v), {}).items():
                if ck.get(kk, 0) < vv:
                    ck[kk] = vv
            ck[k] = max(ck.get(k, 0), v)

    def _record(self, key, val, reads, writes):
        for r in reads:
            d = self.readers.setdefault(r, {})
            if d.get(key, 0) < val:
                d[key] = val
        for w in writes:
            self.lastw[w] = (key, val)
            self.readers[w] = {}

    def op(self, e, fn, reads=(), writes=()):
        self._wait(e, self._deps(reads, writes))
        ins = fn()
        self.cnt[e] += 1
        v = self.cnt[e]
        if self.needed is None:
            ins.then_inc(self.sems[e], 1)
        elif (e, v) in self.needed:
            self.inc[e] = self.inc.get(e, 0) + 1
            self.semval[(e, v)] = self.inc[e]
            ins.then_inc(self.sems[e], 1)
        self.snap[(e, v)] = dict(self.clock[e])
        self._record(e, v, reads, writes)
        return ins

    def dma(self, q, semkey, out, in_, reads=(), writes=(), **kw):
        semkey = 'dq' + semkey
        if semkey not in self.sems:
            self.sems[semkey] = self.es.enter_context(self.nc.semaphore('d_' + semkey))
            self.cnt[semkey] = 0
        deps = self._deps(reads, writes)
        if self.cnt[semkey] > 0:
            if deps.get(semkey, 0) < self.cnt[semkey]:
                deps[semkey] = self.cnt[semkey]
        self._wait(q, deps)
        ins = self.eng[q].dma_start(out=out, in_=in_, **kw)
        self.cnt[semkey] += 1
        v = self.cnt[semkey]
        ins.then_inc(self.sems[semkey], 16)
        self.snap[(semkey, v)] = dict(self.clock[q])
        self._record(semkey, v, reads, writes)
        return ins

    def alias(self, new_keys, old_keys):
        deps = {}
        for k in old_keys:
            if k in self.lastw:
                kk, vv = self.lastw[k]
                if deps.get(kk, 0) < vv:
                    deps[kk] = vv
            for kk, vv in self.readers.get(k, {}).items():
                if deps.get(kk, 0) < vv:
                    deps[kk] = vv
        for nk in new_keys:
            d = self.readers.setdefault(nk, {})
            for kk, vv in deps.items():
                if d.get(kk, 0) < vv:
                    d[kk] = vv

    def finish(self, e, keys):
        deps = {}
        for k in keys:
            if k in self.lastw:
                kk, vv = self.lastw[k]
                if deps.get(kk, 0) < vv:
                    deps[kk] = vv
        self._wait(e, deps)


def weight_tiles():
    t = []
    for hf in range(2):
        t.append(('A%d' % hf, [('w_in', 4104 + 512 * hf, 512, 8), ('w_in', 5128 + 512 * hf, 512, 8)]))
    t.append(('Q', [('w_in', 0, 1024, 8)]))
    t.append(('V', [('w_in', 2048, 1024, 8)]))
    t.append(('K', [('w_in', 1024, 1024, 8)]))
    t.append(('O', [('w_in', 3072, 1024, 8)]))
    for n in range(4):
        t.append(('M%d' % n, [('w_in', 6152 + 256 * n, 256, 8), ('w_in', 7176 + 256 * n, 256, 8),
                              ('w_branch_m', 256 * n, 256, 8), ('w_branch_c', 256 * n, 256, 8)]))
    t.append(('WO', [('w_out', 0, 1024, 8)]))
    for f in range(6):
        nc_ = 512 if f < 5 else 256
        t.append(('F%d' % f, [('w_ffn_gate', 512 * f, nc_, 8), ('w_ffn_up', 512 * f, nc_, 8)]))
    for b in range(4):
        t.append(('D%d' % b, [('w_ffn_down', 256 * b, 256, NFF)]))
    t.append(('PG', [('w_ple_gate', 0, 1024, 8)]))
    t.append(('PP', [('w_ple_proj', 0, 1024, 2)]))
    return t


def _build(debug, needed):
    nc = bass.Bass("TRN2", target_bir_lowering=False)
    din = {}

    def dram_in(name, shape):
        din[name] = nc.dram_tensor(name, shape, F32, kind="ExternalInput").ap()
        return din[name]

    x = dram_in("x", [SEQ, D])
    p_in = dram_in("p", [SEQ, PLE])
    dram_in("w_in", [D, NIN])
    dram_in("w_branch_m", [D, D])
    dram_in("w_branch_c", [D, D])
    dram_in("w_out", [D, D])
    dram_in("w_ffn_gate", [D, DFF])
    dram_in("w_ffn_up", [D, DFF])
    dram_in("w_ffn_down", [DFF, D])
    dram_in("w_ple_gate", [D, D])
    dram_in("w_ple_proj", [PLE, D])
    for nm in ["norm_mix_g", "mh_norm_g", "conf_conv_b", "conf_ln_g", "conf_ln_b", "norm_ffn_g", "norm_ple_g", "final_g"]:
        dram_in(nm, [D])
    dram_in("b_if", [8])
    dram_in("conv_qk_w", [4, 2 * D])
    dram_in("conv_qk_b", [2 * D])
    dram_in("conf_conv_w", [31, D])
    y = nc.dram_tensor("y", [SEQ, D], F32, kind="ExternalOutput").ap()

    dbg = {}
    if debug:
        for nm, shp, dt in [('x1', [SEQ, D], F32), ('x2', [SEQ, D], F32), ('x3', [SEQ, D], F32), ('hm', [SEQ, D], BF16),
                            ('us', [D, SEQ], BF16), ('q', [D, SEQ], BF16), ('k', [D, SEQ], BF16), ('v', [SEQ, D], BF16),
                            ('hT', [D, SEQ], BF16), ('mg', [D, SEQ], BF16), ('gates', [SEQ, 8], F32),
                            ('gsm', [NT * 128, 6 * 4 * NJ], F32),  ('ub', [D, SEQ], BF16), ('uc', [D, SEQ], F32)]:
            dbg[nm] = nc.dram_tensor('dbg_' + nm, shp, dt, kind="ExternalOutput").ap()
    tiles = weight_tiles()
    scr = {}
    for name, pieces in tiles:
        X = sum(ncols * kc for (_, _, ncols, kc) in pieces)
        scr[name] = nc.dram_tensor("scr_" + name, [128, X], BF16).ap()

    with contextlib.ExitStack() as es:
        S = Sched(nc, es, needed)

        def sb(name, shape, dt):
            return es.enter_context(nc.sbuf_tensor(name, shape, dt))

        banks = [es.enter_context(nc.psum_tensor("bank%d" % i, [128, 512], F32)) for i in range(8)]

        def bk(i):
            return banks[i], 'ps%d' % i

        xt = sb("xt", [128, NJ, D], F32)
        hbm = sb("hbm", [128, NJ * D], BF16)
        hb = hbm[:].rearrange("p (j d) -> p j d", j=NJ)
        hmT = hbm[:].rearrange("p (c t) -> p c t", c=8)
        hT = sb("hT", [128, 8, TT], BF16)
        wsl = [sb("wsl%d" % i, [128, 8192], BF16) for i in range(3)]
        ub = sb("ub", [128, 8, HC + TT], BF16)
        XR = sb("XR", [128, 12288], BF16)
        uc = XR[:, 0:8192].bitcast(F32).rearrange("p (i t) -> p i t", i=8)
        ucb = XR[:, 8192:8192 + 2 * TT].rearrange("p (s t) -> p s t", s=2)
        usq = XR[:, 8192 + 2 * TT:8192 + 4 * TT].rearrange("p (s t) -> p s t", s=2)
        qT = XR[:, 0:4096].rearrange("p (c t) -> p c t", c=8)
        kT = XR[:, 4096:8192].rearrange("p (c t) -> p c t", c=8)
        vb = XR[:, 8192:12288].rearrange("p (j d) -> p j d", j=NJ)
        hid = XR[:, 0:NFF * TT].rearrange("p (n t) -> p n t", n=NFF)
        usp = sb("usp", [128, 4096], BF16)
        usT = usp[:].rearrange("p (c t) -> p c t", c=8)
        pt = usp[:, 0:2048].bitcast(F32).rearrange("p (j d) -> p j d", j=NJ)
        pb = usp[:, 2048:3072].rearrange("p (j d) -> p j d", j=NJ)
        pT = usp[:, 3072:4096].rearrange("p (c t) -> p c t", c=2)
        YR = sb("YR", [128, 8 * (HQ + TT)], BF16)
        qp = YR[:].rearrange("p (c t) -> p c t", c=8)
        mgT = YR[:, 0:4096].rearrange("p (c t) -> p c t", c=8)
        qhalo = sb("qhalo", [128, 16, HQ], BF16)
        kw = sb("kw", [128, 2, 4, 256], BF16)
        PTt = sb("PTt", [128, 2, 4, 128], BF16)
        Cd = sb("Cd", [128, 4, 2, 256], BF16)
        ndb = sb("ndb", [128, 8], BF16)
        Cst = sb("Cst", [128, 4, 2, 256], F32)
        nst = sb("nst", [128, 8], F32)
        mprev = sb("mprev", [128, 4], F32)
        hm = sb("hm", [128, 2, D], BF16)
        tmp = sb("tmp", [128, 4, 512], F32)
        tho = sb("tho", [128, 2, D], F32)
        gh = sb("gh", [128, D], F32)
        fg = sb("fg", [128, D], F32)
        identf = sb("identf", [128, 128], F32)
        identb = sb("identb", [128, 128], BF16)
        tri = sb("tri", [128, 128], F32)
        onesf = sb("onesf", [128, 128], F32)
        mask16 = sb("mask16", [128, 128], F32)
        onesm = sb("onesm", [128, 128], BF16)
        onescol = sb("onescol", [128, 2], BF16)
        mh4 = sb("mh4", [128, 16], F32)
        pcol = sb("pcol", [128, 384], F32)
        rows = [sb("rows%d" % i, [128, 128], F32) for i in range(4)]
        dg = sb("dg", [128, 16, 128], BF16)
        wif = sb("wif", [128, 8, 8], BF16)
        ifb = sb("ifb", [128, NJ, 8], F32)
        sqj = sb("sqj", [128, D], BF16)
        ss = sb("ss", [128, 4 * NJ], F32)
        gsb = sb("gsb", [128, NJ, 8], F32)
        gt = sb("gt", [128, 12, 4 * NJ], F32)
        bsb = sb("bsb", [128, 8 * NJ], F32)
        ab = sb("ab", [128, 4 * NJ], F32)
        amx = sb("amx", [128, 1], F32)
        dgm = sb("dgm", [128, 4 * NJ], F32)
        amaxb = sb("amaxb", [128, 4 * NJ], F32)
        Mb = sb("Mb", [128, 4 * NJ], F32)
        dmb = sb("dmb", [128, 4 * NJ], F32)
        wkb = sb("wkb", [128, 4 * NJ], F32)
        decb = sb("decb", [128, 4 * NJ], F32)
        dec16b = sb("dec16b", [128, 4 * NJ], F32)
        thrb = sb("thrb", [128, 4 * NJ], F32)
        lnst = sb("lnst", [128, 4, TT], F32)
        osm = sb("osm", [128, 2, 24], F32)

        PC_MIX, PC_FFN, PC_PLE, PC_QKB, PC_CCB, PC_LNG, PC_LNB = 0, 8, 16, 24, 40, 48, 56
        PC_QKW = 64
        PC_CW0 = 128
        PC_CW1 = 256

        def cw_col(k, i):
            return (PC_CW0 + k * 8 + i) if k < 16 else (PC_CW1 + (k - 16) * 8 + i)

        scr_keys = {}
        cast_state = {'next': 0}
        LOOKAHEAD = 6

        def cast_next():
            i = cast_state['next']
            if i >= len(tiles):
                return
            cast_state['next'] += 1
            name, pieces = tiles[i]
            off = 0
            for pi, (src, c0, ncols, kc) in enumerate(pieces):
                key = 'scr_%s_%d' % (name, pi)
                dst = scr[name][:, off:off + kc * ncols].rearrange("p (c n) -> p c n", c=kc)
                srcap = din[src][:, c0:c0 + ncols].rearrange("(c p) n -> p c n", p=128)
                S.dma('pool', 'cast_%s_%d' % (name, pi), dst, srcap, writes=[key])
                off += kc * ncols

        for name, pieces in tiles:
            scr_keys[name] = ['scr_%s_%d' % (name, pi) for pi in range(len(pieces))]
        S.dma('pool', 'wif', wif[:], din['w_in'][:, 4096:4104].rearrange("(c p) n -> p c n", p=128), writes=['wif'])
        for _ in range(LOOKAHEAD):
            cast_next()

        order = [name for name, _ in tiles] * NT
        wstate = {'next': 0, 'cur': {}}
        tile_X = {name: sum(ncols * kc for (_, _, ncols, kc) in pieces) for name, pieces in tiles}

        def wissue():
            i = wstate['next']
            if i >= len(order):
                return
            name = order[i]
            slot = i % 3
            S.dma('sp', 'w%d' % slot, wsl[slot][:, 0:tile_X[name]], scr[name][:, :],
                  reads=scr_keys[name], writes=['w%d' % slot])
            wstate['next'] += 1

        def wget(name):
            i = wstate['cur'].get('i', -1) + 1
            assert order[i] == name, (order[i], name)
            wstate['cur']['i'] = i
            slot = i % 3
            if i < len(tiles):
                cast_next()
            return wsl[slot], 'w%d' % slot

        def wrelease():
            wissue()

        S.op('pool', lambda: nc.gpsimd.memset(identf[:], 1.0), writes=['identf'])
        S.op('pool', lambda: nc.gpsimd.affine_select(out=identf[:], in_=identf[:], pattern=[[-1, 128]], base=0,
                                                      channel_multiplier=1, compare_op=ALU.is_equal, fill=0.0),
             reads=['identf'], writes=['identf'])
        S.op('pool', lambda: nc.gpsimd.memset(tri[:], 1.0), writes=['tri'])
        S.op('pool', lambda: nc.gpsimd.affine_select(out=tri[:], in_=tri[:], pattern=[[1, 128]], base=0,
                                                      channel_multiplier=-1, compare_op=ALU.is_ge, fill=0.0),
             reads=['tri'], writes=['tri'])
        S.op('pool', lambda: nc.gpsimd.memset(onesf[:], 1.0), writes=['onesf'])
        S.op('pool', lambda: nc.gpsimd.memset(onesm[:], 1.0 / D), writes=['onesm'])
        S.op('pool', lambda: nc.gpsimd.memset(onescol[:], 1.0), writes=['onescol'])
        S.op('pool', lambda: nc.gpsimd.memset(mh4[:], -0.5), writes=['mh4'])
        S.op('dve', lambda: nc.vector.tensor_copy(out=identb[:], in_=identf[:]), reads=['identf'], writes=['identb'])
        S.op('dve', lambda: nc.vector.tensor_scalar(out=mask16[:], in0=tri[:], scalar1=1.0 / 16, scalar2=None,
                                                    op0=ALU.mult), reads=['tri'], writes=['mask16'])
        S.op('pool', lambda: nc.gpsimd.memset(Cst[:], 0.0), writes=['C%d' % h for h in range(4)])
        S.op('pool', lambda: nc.gpsimd.memset(nst[:], 0.0), writes=['nst'])
        S.op('pool', lambda: nc.gpsimd.memset(mprev[:], 0.0), writes=['mprev'])
        S.op('pool', lambda: nc.gpsimd.memset(ub[:], 0.0), writes=['ub%d' % i for i in range(8)])
        S.op('pool', lambda: nc.gpsimd.memset(qhalo[:], 0.0), writes=['qhalo'])
        S.dma('sp', 'c_gh', gh[:], din['mh_norm_g'].partition_broadcast(128), writes=['gh'])
        S.dma('sp', 'c_fg', fg[:], din['final_g'].partition_broadcast(128), writes=['fg'])
        for j in range(NJ):
            S.dma('sp', 'c_ifb%d' % j, ifb[:, j, :], din['b_if'].partition_broadcast(128), writes=['ifb'])
        S.op('dve', lambda: nc.vector.tensor_scalar(out=gh[:], in0=gh[:], scalar1=0.5, scalar2=None, op0=ALU.mult),
             reads=['gh'], writes=['gh'])
        S.op('pool', lambda: nc.gpsimd.memset(rows[0][:], 0.0), writes=['rows0'])
        rowspec = [("norm_mix_g", PC_MIX), ("norm_ffn_g", PC_FFN), ("norm_ple_g", PC_PLE), ("conf_conv_b", PC_CCB),
                   ("conf_ln_g", PC_LNG), ("conf_ln_b", PC_LNB)]
        for nm, base in rowspec:
            S.dma('sp', 'c_' + nm, rows[0][base:base + 8, :], din[nm].rearrange("(i p) -> i p", p=128),
                  reads=['rows0'], writes=['rows0_' + nm])
        S.dma('sp', 'c_qkb', rows[0][PC_QKB:PC_QKB + 16, :], din['conv_qk_b'].rearrange("(i p) -> i p", p=128),
              reads=['rows0'], writes=['rows0_qkb'])
        S.dma('sp', 'c_qkw', rows[1][0:64, :], din['conv_qk_w'].rearrange("k (i p) -> (k i) p", p=128), writes=['rows1'])
        S.dma('sp', 'c_cw0', rows[2][0:128, :], din['conf_conv_w'][0:16, :].rearrange("k (i p) -> (k i) p", p=128),
              writes=['rows2'])
        S.dma('sp', 'c_cw1', rows[3][0:120, :], din['conf_conv_w'][16:31, :].rearrange("k (i p) -> (k i) p", p=128),
              writes=['rows3'])
        r0keys = ['rows0_' + nm for nm, _ in rowspec] + ['rows0_qkb']
        for gi, (R, base, keys) in enumerate([(64, 0, r0keys), (64, PC_QKW, ['rows1']), (128, PC_CW0, ['rows2']),
                                              (120, PC_CW1, ['rows3'])]):
            bt, bkey = bk(S.bank())
            S.op('pe', lambda gi=gi, R=R, bt=bt: nc.tensor.matmul(bt[:, 0:R], lhsT=rows[gi][0:R, :],
                                                               rhs=identf[0:R, 0:R], start=True, stop=True),
                 reads=keys + ['identf'], writes=[bkey])
            S.op('dve', lambda R=R, base=base, bt=bt: nc.vector.tensor_copy(out=pcol[:, base:base + R], in_=bt[:, 0:R]),
                 reads=[bkey], writes=['pcol'])

        for j in range(NJ):
            S.dma('sp', 'x%d' % j, xt[:, j, :], x[j * 128:(j + 1) * 128, :], writes=['xt%d' % j])
        for _ in range(3):
            wissue()

        S.op('dve', lambda: nc.vector.tensor_scalar(out=pcol[:, PC_CW0:PC_CW1 + 120], in0=pcol[:, PC_CW0:PC_CW1 + 120],
                                                    scalar1=0.5, scalar2=None, op0=ALU.mult), reads=['pcol'], writes=['pcol'])

        tmpi = {'i': 0}

        def tmpbuf():
            i = tmpi['i'] % 4
            tmpi['i'] += 1
            return tmp[:, i, :], 'tmp%d' % i

        dgi = {'i': 0}

        def diag(col, scale):
            i = dgi['i'] % 16
            n = dgi['i']
            dgi['i'] += 1
            if n % 3 == 2:
                S.op('pool', lambda: nc.gpsimd.tensor_scalar(out=dg[:, i, :], in0=identf[:], scalar1=pcol[:, col:col + 1],
                                                             scalar2=1.0, op0=ALU.mult, op1=ALU.mult),
                     reads=['identf', 'pcol'], writes=['dg%d' % i])
            elif n % 3 == 1:
                S.op('act', lambda: nc.scalar.activation(out=dg[:, i, :], in_=identf[:], func=AF.Identity,
                                                         scale=pcol[:, col:col + 1]),
                     reads=['identf', 'pcol'], writes=['dg%d' % i])
            else:
                S.op('dve', lambda: nc.vector.tensor_scalar(out=dg[:, i, :], in0=identf[:], scalar1=pcol[:, col:col + 1],
                                                            scalar2=None, op0=ALU.mult),
                     reads=['identf', 'pcol'], writes=['dg%d' % i])
            return dg[:, i, :], 'dg%d' % i

        class DiagStream:
            def __init__(self, cols):
                self.cols = list(cols)
                self.gi_ = 0
                self.q = []

            def prefetch(self, k):
                while k > 0 and self.gi_ < len(self.cols):
                    self.q.append(diag(self.cols[self.gi_], 1.0))
                    self.gi_ += 1
                    k -= 1

            def next(self):
                self.prefetch(1)
                return self.q.pop(0)

        evi = {'i': 0}

        def evac_copy(out, in_, reads, writes):
            evi['i'] += 1
            if evi['i'] % 2:
                S.op('act', lambda: nc.scalar.copy(out=out, in_=in_), reads=reads, writes=writes)
            else:
                S.op('dve', lambda: nc.vector.tensor_copy(out=out, in_=in_), reads=reads, writes=writes)

        def js(j):
            return slice(j * 128, (j + 1) * 128)

        class NormT:
            def __init__(self, gbase, bank_ids=None):
                self.gbase = gbase
                ids = bank_ids if bank_ids is not None else [S.bank() for _ in range(4)]
                self.nb = [bk(i) for i in ids]
                self.nbb = [b_[:].bitcast(BF16) for b_, _ in self.nb]

            def pre(self, j):
                S.op('act', lambda: nc.scalar.activation(out=sqj[:], in_=xt[:, j, :], func=AF.Square,
                                                         accum_out=ss[:, j:j + 1]),
                     reads=['xt%d' % j], writes=['ssa%d' % j])
                S.op('dve', lambda: nc.vector.tensor_scalar(out=ss[:, NJ + j:NJ + j + 1], in0=ss[:, j:j + 1],
                                                            scalar1=1.0 / D, scalar2=EPS, op0=ALU.mult, op1=ALU.add),
                     reads=['ssa%d' % j], writes=['ssb%d' % j])
                S.op('pool', lambda: nc.gpsimd.tensor_tensor(out=ss[:, 2 * NJ + j:2 * NJ + j + 1],
                                                             in0=ss[:, NJ + j:NJ + j + 1], in1=mh4[:, 0:1], op=ALU.pow),
                     reads=['ssb%d' % j, 'mh4'], writes=['ssc%d' % j])
                S.op('dve', lambda: nc.vector.tensor_scalar(out=hb[:, j, :], in0=xt[:, j, :],
                                                            scalar1=ss[:, 2 * NJ + j:2 * NJ + j + 1], scalar2=None,
                                                            op0=ALU.mult),
                     reads=['xt%d' % j, 'ssc%d' % j], writes=['hb%d' % j])

            def tr(self, j):
                for c in range(8):
                    o = (c % 2) * 512 + j * 128
                    S.op('pe', lambda: nc.tensor.transpose(out=self.nbb[c // 2][:, o:o + 128],
                                                          in_=hb[:, j, c * 128:(c + 1) * 128], identity=identb[:]),
                         reads=['hb%d' % j, 'identb'], writes=[self.nb[c // 2][1]])

            def evac(self):
                gbase = self.gbase
                for c in range(8):
                    o = (c % 2) * 512
                    bkey = self.nb[c // 2][1]
                    if (c // 2) % 2 == 0:
                        S.op('dve', lambda: nc.vector.tensor_scalar(out=hT[:, c, :], in0=self.nbb[c // 2][:, o:o + TT],
                                                                    scalar1=pcol[:, gbase + c:gbase + c + 1],
                                                                    scalar2=None, op0=ALU.mult),
                             reads=[bkey, 'pcol'], writes=['hT%d' % c])
                    else:
                        S.op('act', lambda: nc.scalar.activation(out=hT[:, c, :], in_=self.nbb[c // 2][:, o:o + TT],
                                                                 func=AF.Identity,
                                                                 scale=pcol[:, gbase + c:gbase + c + 1]),
                             reads=[bkey, 'pcol'], writes=['hT%d' % c])

        def norm_T(gbase):
            nt = NormT(gbase)
            for j in range(NJ):
                nt.pre(j)
                nt.tr(j)
            nt.evac()

        def mm8(bt, bkey, lhs_fn, rhs_fn, lkeys, n=8):
            for kc in range(n):
                S.op('pe', lambda kc=kc: nc.tensor.matmul(bt, lhsT=lhs_fn(kc), rhs=rhs_fn(kc), start=(kc == 0),
                                                         stop=(kc == n - 1)),
                     reads=lkeys, writes=[bkey])

        hbk = ['hb%d' % j for j in range(NJ)]
        hTk = ['hT%d' % c for c in range(8)]
        hmTk = ['hmT%d' % j for j in range(NJ)]
        uck = ['uc%d' % i for i in range(8)] + ['ucb0', 'ucb1', 'usq0', 'usq1']
        qkvk = ['qT%d' % c for c in range(8)] + ['kT%d' % c for c in range(8)] + ['vb%d' % j for j in range(NJ)]
        hidk = ['hid%d' % n for n in range(NFF)]
        usk = ['us%d' % i for i in range(8)]
        ppk = ['pt%d' % j for j in range(NJ)] + ['pb', 'pT']
        qpk = ['qp%d' % c for c in range(8)]
        mgk = ['mg%d' % c for c in range(8)]
        ubk = ['ub%d' % i for i in range(8)]

        dcnt = {'i': 0}

        def dump(name, dst, src, reads):
            if not debug:
                return
            dcnt['i'] += 1
            S.dma('sp', 'dbg%d' % (dcnt['i'] % 6), dst, src, reads=reads, writes=['dbgout%d' % dcnt['i']])

        for T in range(NT):
            t0 = T * TT

            S.alias(hbk, hmTk)
            norm_T(PC_MIX)

            if debug:
                dump('hT', dbg['hT'][:, t0:t0 + TT].rearrange("(c p) t -> p c t", p=128), hT[:, :, :], hTk)
            gbt, gbkey = bk(S.bank())
            for j in range(NJ):
                mm8(gbt[:, j * 8:(j + 1) * 8], gbkey, lambda kc, j=j: hT[:, kc, js(j)], lambda kc: wif[:, kc, :],
                    hTk + ['wif'])
            S.op('dve', lambda: nc.vector.tensor_tensor(out=gsb[:].rearrange("p j e -> p (j e)"), in0=gbt[:, 0:8 * NJ],
                                                        in1=ifb[:].rearrange("p j e -> p (j e)"), op=ALU.add),
                 reads=[gbkey, 'ifb'], writes=['gsb'])

            if debug:
                dump('gates', dbg['gates'][t0:t0 + TT, :].rearrange("(j p) e -> p j e", p=128), gsb[:, :, :], ['gsb'])
            S.alias(uck, hidk)
            S.op('pool', lambda: nc.gpsimd.tensor_copy(out=ub[:, :, 0:HC], in_=ub[:, :, TT:TT + HC]),
                 reads=ubk, writes=ubk)
            cstream = DiagStream([cw_col(k, i) for i in range(8) for k in range(31)])
            cstream.prefetch(15)
            for hf in range(2):
                wt, wkey = wget('A%d' % hf)
                wv = wt[:, 0:8192].rearrange("p (m c n) -> p m c n", m=2, c=8)
                for il in range(4):
                    i = hf * 4 + il
                    ba, bakey = bk(S.bank())
                    bg, bgkey = bk(S.bank())
                    mm8(ba[:, 0:TT], bakey, lambda kc, il=il: wv[:, 0, kc, il * 128:(il + 1) * 128],
                        lambda kc: hT[:, kc, :], hTk + [wkey])
                    mm8(bg[:, 0:TT], bgkey, lambda kc, il=il: wv[:, 1, kc, il * 128:(il + 1) * 128],
                        lambda kc: hT[:, kc, :], hTk + [wkey])
                    tb, tkey = tmpbuf()
                    S.op('act', lambda: nc.scalar.activation(out=tb, in_=bg[:, 0:TT], func=AF.Tanh, scale=0.5),
                         reads=[bgkey], writes=[tkey])
                    S.op('dve', lambda i=i: nc.vector.scalar_tensor_tensor(out=ub[:, i, HC:HC + TT], in0=tb, scalar=1.0,
                                                                         in1=ba[:, 0:TT], op0=ALU.add, op1=ALU.mult),
                         reads=[tkey, bakey], writes=['ub%d' % i])
                wrelease()
            Mbank, Mkey = bk(6)
            Qbank, Qkey = bk(7)
            for i in range(8):
                bc, bckey = bk(S.bank())
                for k in range(31):
                    dgt, dgkey = cstream.next()
                    S.op('pe', lambda k=k, i=i, dgt=dgt: nc.tensor.matmul(bc[:, 0:TT], lhsT=dgt, rhs=ub[:, i, k:k + TT],
                                                                       start=(k == 0), stop=(k == 30)),
                         reads=[dgkey, 'ub%d' % i], writes=[bckey])
                S.op('act', lambda i=i: nc.scalar.activation(out=uc[:, i, :], in_=bc[:, 0:TT], func=AF.Identity,
                                                           bias=pcol[:, PC_CCB + i:PC_CCB + i + 1]),
                     reads=[bckey, 'pcol'], writes=['uc%d' % i])
                s2 = i % 2
                S.op('dve', lambda i=i, s2=s2: nc.vector.tensor_copy(out=ucb[:, s2, :], in_=uc[:, i, :]),
                     reads=['uc%d' % i], writes=['ucb%d' % s2])
                S.op('act', lambda i=i, s2=s2: nc.scalar.activation(out=usq[:, s2, :], in_=uc[:, i, :], func=AF.Square),
                     reads=['uc%d' % i], writes=['usq%d' % s2])
                S.op('pe', lambda i=i, s2=s2: nc.tensor.matmul(Mbank[:, 0:TT], lhsT=onesm[:], rhs=ucb[:, s2, :],
                                                             start=(i == 0), stop=(i == 7)),
                     reads=['onesm', 'ucb%d' % s2], writes=[Mkey])
                S.op('pe', lambda i=i, s2=s2: nc.tensor.matmul(Qbank[:, 0:TT], lhsT=onesm[:], rhs=usq[:, s2, :],
                                                             start=(i == 0), stop=(i == 7)),
                     reads=['onesm', 'usq%d' % s2], writes=[Qkey])

            if debug:
                dump('ub', dbg['ub'][:, t0:t0 + TT].rearrange("(c p) t -> p c t", p=128), ub[:, :, HC:HC + TT], ubk)
                dump('uc', dbg['uc'][:, t0:t0 + TT].rearrange("(c p) t -> p c t", p=128), uc[:, :, :], uck)
            def qk_proj(piece, between=None):
                wt, wkey = wget('Q' if piece == 0 else 'K')
                wv = wt[:, 0:8192].rearrange("p (c n) -> p c n", c=8)
                S.alias(qpk, mgk)
                for c in range(8):
                    bq, bqkey = bk(S.bank())
                    mm8(bq[:, 0:TT], bqkey, lambda kc, c=c: wv[:, kc, c * 128:(c + 1) * 128], lambda kc: hT[:, kc, :],
                        hTk + [wkey])
                    evac_copy(qp[:, c, HQ:HQ + TT], bq[:, 0:TT], [bqkey], ['qp%d' % c])
                    if between is not None:
                        between(c)
                wrelease()
                S.op('pool', lambda: nc.gpsimd.tensor_copy(out=qp[:, :, 0:HQ], in_=qhalo[:, piece * 8:(piece + 1) * 8, :]),
                     reads=['qhalo'] + qpk, writes=qpk)
                S.op('pool', lambda: nc.gpsimd.tensor_copy(out=qhalo[:, piece * 8:(piece + 1) * 8, :],
                                                           in_=qp[:, :, TT:TT + HQ]),
                     reads=qpk, writes=['qhalo'])

            def qk_conv(piece, stream):
                dstT, dkey = (qT, 'qT') if piece == 0 else (kT, 'kT')
                for c in range(8):
                    bq, bqkey = bk(S.bank())
                    cc = piece * 8 + c
                    for k in range(4):
                        dgt, dgkey = stream.next()
                        S.op('pe', lambda: nc.tensor.matmul(bq[:, 0:TT], lhsT=dgt, rhs=qp[:, c, k:k + TT], start=(k == 0),
                                                            stop=(k == 3)),
                             reads=[dgkey, 'qp%d' % c], writes=[bqkey])
                    S.op('act', lambda: nc.scalar.activation(out=dstT[:, c, :], in_=bq[:, 0:TT], func=AF.Silu,
                                                             bias=pcol[:, PC_QKB + cc:PC_QKB + cc + 1]),
                         reads=[bqkey, 'pcol'], writes=['%s%d' % (dkey, c)])

            msq, lnv, rstd, mr = lnst[:, 0, :], lnst[:, 1, :], lnst[:, 2, :], lnst[:, 3, :]
            S.op('act', lambda: nc.scalar.activation(out=msq, in_=Mbank[:, 0:TT], func=AF.Square),
                 reads=[Mkey], writes=['ln0'])
            S.op('dve', lambda: nc.vector.tensor_tensor(out=msq, in0=Qbank[:, 0:TT], in1=msq, op=ALU.subtract),
                 reads=[Qkey, 'ln0'], writes=['ln0'])
            S.op('act', lambda: nc.scalar.activation(out=lnv, in_=msq, func=AF.Ln, bias=EPS), reads=['ln0'], writes=['ln1'])
            S.op('act', lambda: nc.scalar.activation(out=rstd, in_=lnv, func=AF.Exp, scale=-0.5),
                 reads=['ln1'], writes=['ln2'])
            S.op('dve', lambda: nc.vector.tensor_tensor(out=mr, in0=Mbank[:, 0:TT], in1=rstd, op=ALU.mult),
                 reads=[Mkey, 'ln2'], writes=['ln3'])
            G = 4 * NJ

            def g3(t):
                return t.rearrange("p (j h) -> p j h", j=NJ)
            zf = gsb[:, :, 4:8]
            igf = gsb[:, :, 0:4]
            t_az, t_e, t_l, t_mn, t_lf, t_wa, t_ta = [gt[:, i, :] for i in range(7)]
            S.op('dve', lambda: nc.vector.scalar_tensor_tensor(out=g3(t_az), in0=zf, scalar=-1.0, in1=zf, op0=ALU.mult,
                                                               op1=ALU.max), reads=['gsb'], writes=['gt0'])
            S.op('act', lambda: nc.scalar.activation(out=t_e, in_=t_az, func=AF.Exp, scale=-1.0), reads=['gt0'], writes=['gt1'])
            S.op('act', lambda: nc.scalar.activation(out=t_l, in_=t_e, func=AF.Ln, bias=1.0), reads=['gt1'], writes=['gt2'])
            S.op('dve', lambda: nc.vector.tensor_scalar(out=g3(t_mn), in0=zf, scalar1=0.0, scalar2=None, op0=ALU.min),
                 reads=['gsb'], writes=['gt3'])
            S.op('dve', lambda: nc.vector.tensor_tensor(out=t_lf, in0=t_mn, in1=t_l, op=ALU.subtract),
                 reads=['gt3', 'gt2'], writes=['gt4'])
            g1, g1key = bk(S.bank())
            S.op('pe', lambda: nc.tensor.matmul(g1[:, 0:G], lhsT=tri[:], rhs=t_lf, start=True, stop=True),
                 reads=['tri', 'gt4'], writes=[g1key])
            S.op('pe', lambda: nc.tensor.matmul(g1[:, G:2 * G], lhsT=onesf[:], rhs=t_lf, start=True, stop=True),
                 reads=['onesf', 'gt4'], writes=[g1key])
            S.op('dve', lambda: nc.vector.tensor_copy(out=bsb[:, 0:2 * G], in_=g1[:, 0:2 * G]), reads=[g1key], writes=['bsb'])
            S.op('dve', lambda: nc.vector.tensor_tensor(out=g3(ab[:, :]), in0=igf, in1=g3(bsb[:, 0:G]), op=ALU.subtract),
                 reads=['gsb', 'bsb'], writes=['ab'])
            g2, g2key = bk(S.bank())
            S.op('pe', lambda: nc.tensor.matmul(g2[0:G, 0:128], lhsT=ab[:, :], rhs=identf[:], start=True, stop=True),
                 reads=['ab', 'identf'], writes=[g2key])
            S.op('dve', lambda: nc.vector.tensor_reduce(out=amx[0:G, 0:1], in_=g2[0:G, 0:128], axis=AX.X, op=ALU.max),
                 reads=[g2key], writes=['amx'])
            S.op('dve', lambda: nc.vector.tensor_scalar(out=dgm[0:G, 0:G], in0=identf[0:G, 0:G], scalar1=amx[0:G, 0:1],
                                                        scalar2=None, op0=ALU.mult), reads=['amx', 'identf'], writes=['dgm'])
            S.op('pe', lambda: nc.tensor.matmul(g2[:, 128:128 + G], lhsT=onesf[0:G, :], rhs=dgm[0:G, 0:G], start=True,
                                                stop=True), reads=['onesf', 'dgm'], writes=[g2key])
            S.op('dve', lambda: nc.vector.tensor_copy(out=amaxb[:, :], in_=g2[:, 128:128 + G]), reads=[g2key],
                 writes=['amaxb'])
            for j in range(NJ):
                hs = slice(4 * j, 4 * j + 4)
                S.op('dve', lambda hs=hs: nc.vector.tensor_tensor(out=Mb[:, hs], in0=mprev[:, :], in1=amaxb[:, hs], op=ALU.max),
                     reads=['mprev', 'amaxb'], writes=['Mb'])
                S.op('dve', lambda hs=hs: nc.vector.tensor_tensor(out=dmb[:, hs], in0=mprev[:, :], in1=Mb[:, hs],
                                                                op=ALU.subtract), reads=['mprev', 'Mb'], writes=['dmb'])
                S.op('dve', lambda hs=hs, j=j: nc.vector.tensor_tensor(out=mprev[:, :], in0=bsb[:, G + 4 * j:G + 4 * j + 4],
                                                                     in1=Mb[:, hs], op=ALU.add),
                     reads=['bsb', 'Mb'], writes=['mprev'])
            S.op('dve', lambda: nc.vector.tensor_tensor(out=t_wa, in0=ab[:, :], in1=Mb[:, :], op=ALU.subtract),
                 reads=['ab', 'Mb'], writes=['gt5'])
            S.op('act', lambda: nc.scalar.activation(out=wkb[:, :], in_=t_wa, func=AF.Exp), reads=['gt5'], writes=['wkb'])
            S.op('act', lambda: nc.scalar.activation(out=decb[:, :], in_=dmb[:, :], func=AF.Exp), reads=['dmb'], writes=['decb'])
            S.op('dve', lambda: nc.vector.tensor_scalar(out=dec16b[:, :], in0=decb[:, :], scalar1=1.0 / 16, scalar2=None,
                                                        op0=ALU.mult), reads=['decb'], writes=['dec16b'])
            S.op('dve', lambda: nc.vector.tensor_tensor(out=t_ta, in0=bsb[:, 0:G], in1=Mb[:, :], op=ALU.add),
                 reads=['bsb', 'Mb'], writes=['gt6'])
            S.op('act', lambda: nc.scalar.activation(out=thrb[:, :], in_=t_ta, func=AF.Exp, scale=-1.0),
                 reads=['gt6'], writes=['thrb'])
            S.alias(usk, ppk)

            def ln_norm(i):
                ta, takey = tmpbuf()
                S.op('dve', lambda: nc.vector.tensor_tensor(out=ta, in0=uc[:, i, :], in1=rstd, op=ALU.mult),
                     reads=['uc%d' % i, 'ln2'], writes=[takey])
                S.op('pool', lambda: nc.gpsimd.tensor_tensor(out=ta, in0=ta, in1=mr, op=ALU.subtract),
                     reads=[takey, 'ln3'], writes=[takey])
                S.op('act', lambda: nc.scalar.activation(out=usT[:, i, :], in_=ta, func=AF.Silu,
                                                         scale=pcol[:, PC_LNG + i:PC_LNG + i + 1],
                                                         bias=pcol[:, PC_LNB + i:PC_LNB + i + 1]),
                     reads=[takey, 'pcol'], writes=['us%d' % i])

            qstream = DiagStream([PC_QKW + k * 16 + c for c in range(8) for k in range(4)])
            qstream.prefetch(15)
            qk_proj(0, between=ln_norm)
            if debug:
                dump('us', dbg['us'][:, t0:t0 + TT].rearrange("(c p) t -> p c t", p=128), usT[:, :, :], usk)
                G_ = 4 * NJ
                for ii, (tt_, kk_) in enumerate([(bsb[:, 0:G_], 'bsb'), (ab[:, :], 'ab'), (Mb[:, :], 'Mb'), (wkb[:, :], 'wkb'),
                                                 (decb[:, :], 'decb'), (thrb[:, :], 'thrb')]):
                    dump('gsm', dbg['gsm'][T * 128:(T + 1) * 128, ii * G_:(ii + 1) * G_], tt_, [kk_])

            S.alias(['vb%d' % j for j in range(NJ)], ['ucb0', 'ucb1', 'usq0', 'usq1'])
            wt, wkey = wget('V')
            wv = wt[:, 0:8192].rearrange("p (c n) -> p c n", c=8)
            for j in range(NJ):
                for hf in range(2):
                    bv, bvkey = bk(S.bank())
                    mm8(bv[:, :], bvkey, lambda kc, j=j: hT[:, kc, js(j)], lambda kc, hf=hf: wv[:, kc, hf * 512:(hf + 1) * 512],
                        hTk + [wkey])
                    evac_copy(vb[:, j, hf * 512:(hf + 1) * 512], bv[:, :], [bvkey], ['vb%d' % j])
            wrelease()

            S.alias(qkvk, uck)
            qk_conv(0, qstream)
            kstream = DiagStream([PC_QKW + k * 16 + 8 + c for c in range(8) for k in range(4)])
            kstream.prefetch(15)
            qk_proj(1)
            qk_conv(1, kstream)

            if debug:
                dump('q', dbg['q'][:, t0:t0 + TT].rearrange("(c p) t -> p c t", p=128), qT[:, :, :], qkvk)
                dump('k', dbg['k'][:, t0:t0 + TT].rearrange("(c p) t -> p c t", p=128), kT[:, :, :], qkvk)
                dump('v', dbg['v'][t0:t0 + TT, :].rearrange("(j p) d -> p j d", p=128), vb[:, :, :], qkvk)
            wtO, wOkey = wget('O')
            wO = wtO[:, 0:8192].rearrange("p (c n) -> p c n", c=8)
            S.alias(hmTk, hbk)

            def hm_transposes(jj):
                ss2 = jj % 2
                bH, bHkey = bk(7)
                bHb = bH[:].bitcast(BF16)
                for c in range(8):
                    S.op('pe', lambda: nc.tensor.transpose(out=bHb[:, c * 128:(c + 1) * 128],
                                                          in_=hm[:, ss2, c * 128:(c + 1) * 128], identity=identb[:]),
                         reads=['hm%d' % ss2, 'identb'], writes=[bHkey])
                evac_copy(hmT[:, :, js(jj)], bHb[:, 0:1024].rearrange("p (c t) -> p c t", c=8), [bHkey], ['hmT%d' % jj])

            for j in range(NJ):
                s2 = j % 2
                gi = 4 * j
                for h in range(4):
                    S.op('pool', lambda: nc.gpsimd.tensor_scalar(
                        out=Cd[:, h, :, :].rearrange("p a e -> p (a e)"), in0=Cst[:, h, :, :].rearrange("p a e -> p (a e)"),
                        scalar1=dec16b[:, gi + h:gi + h + 1], scalar2=1.0, op0=ALU.mult, op1=ALU.mult),
                        reads=['C%d' % h, 'dec16b'], writes=['Cd%d' % h])
                    S.op('pool', lambda: nc.gpsimd.tensor_scalar(
                        out=ndb[:, 2 * h:2 * h + 2], in0=nst[:, 2 * h:2 * h + 2], scalar1=dec16b[:, gi + h:gi + h + 1],
                        scalar2=1.0, op0=ALU.mult, op1=ALU.mult), reads=['nst', 'dec16b'], writes=['ndb'])
                bT, bTkey = bk(0)
                bTb = bT[:].bitcast(BF16)
                for c in range(8):
                    S.op('pe', lambda: nc.tensor.transpose(out=bTb[:, c * 128:(c + 1) * 128], in_=kT[:, c, js(j)],
                                                          identity=identb[:]),
                         reads=['kT%d' % c, 'identb'], writes=[bTkey])
                for h in range(4):
                    S.op('dve', lambda: nc.vector.tensor_scalar(out=kw[:, s2, h, :], in0=bTb[:, h * 256:(h + 1) * 256],
                                                                scalar1=wkb[:, gi + h:gi + h + 1], scalar2=None, op0=ALU.mult),
                         reads=[bTkey, 'wkb'], writes=['kw%d' % s2])
                bS, bSkey = bk(1)
                for h in range(4):
                    for a in range(2):
                        S.op('pe', lambda: nc.tensor.matmul(bS[:, h * 128:(h + 1) * 128], lhsT=kT[:, 2 * h + a, js(j)],
                                                            rhs=qT[:, 2 * h + a, js(j)], start=(a == 0), stop=(a == 1)),
                             reads=['kT%d' % (2 * h + a), 'qT%d' % (2 * h + a)], writes=[bSkey])
                for h in range(4):
                    S.op('dve', lambda: nc.vector.scalar_tensor_tensor(out=PTt[:, s2, h, :], in0=bS[:, h * 128:(h + 1) * 128],
                                                                       scalar=wkb[:, gi + h:gi + h + 1], in1=mask16[:],
                                                                       op0=ALU.mult, op1=ALU.mult),
                         reads=[bSkey, 'wkb', 'mask16'], writes=['PT%d' % s2])
                thj = tho[:, s2, :]
                for hf in range(2):
                    bo, bokey = bk(5 + hf)
                    mm8(bo[:, :], bokey, lambda kc: hT[:, kc, js(j)], lambda kc: wO[:, kc, hf * 512:(hf + 1) * 512],
                        hTk + [wOkey])
                    S.op('act', lambda: nc.scalar.activation(out=thj[:, hf * 512:(hf + 1) * 512], in_=bo[:, :],
                                                             func=AF.Tanh, scale=0.5),
                         reads=[bokey], writes=['tho%d' % s2])
                S.op('dve', lambda: nc.vector.scalar_tensor_tensor(out=thj, in0=thj, scalar=1.0, in1=gh[:], op0=ALU.add,
                                                                   op1=ALU.mult), reads=['tho%d' % s2, 'gh'], writes=['tho%d' % s2])
                if j > 0:
                    hm_transposes(j - 1)
                bN = [bk(2), bk(3)]
                bD, bDkey = bk(4)
                for h in range(4):
                    bn, bnkey = bN[h // 2]
                    o = (h % 2) * 256
                    S.op('pe', lambda: nc.tensor.matmul(bn[:, o:o + 256], lhsT=PTt[:, s2, h, :],
                                                        rhs=vb[:, j, h * 256:(h + 1) * 256], start=True, stop=False),
                         reads=['PT%d' % s2, 'vb%d' % j], writes=[bnkey])
                    for a in range(2):
                        S.op('pe', lambda: nc.tensor.matmul(bn[:, o:o + 256], lhsT=qT[:, 2 * h + a, js(j)],
                                                            rhs=Cd[:, h, a, :], start=False, stop=(a == 1)),
                             reads=['qT%d' % (2 * h + a), 'Cd%d' % h], writes=[bnkey])
                    S.op('pe', lambda: nc.tensor.matmul(bD[:, h:h + 1], lhsT=PTt[:, s2, h, :], rhs=onescol[:, 0:1],
                                                        start=True, stop=False),
                         reads=['PT%d' % s2, 'onescol'], writes=[bDkey])
                    for a in range(2):
                        S.op('pe', lambda: nc.tensor.matmul(bD[:, h:h + 1], lhsT=qT[:, 2 * h + a, js(j)],
                                                            rhs=ndb[:, 2 * h + a:2 * h + a + 1], start=False, stop=(a == 1)),
                             reads=['qT%d' % (2 * h + a), 'ndb'], writes=[bDkey])
                bU = [bk(0), bk(1), bk(5), bk(6)]
                for h in range(4):
                    bu, bukey = bU[h]
                    for a in range(2):
                        S.op('pe', lambda: nc.tensor.matmul(bu[:, a * 256:(a + 1) * 256],
                                                            lhsT=kw[:, s2, h, a * 128:(a + 1) * 128],
                                                            rhs=vb[:, j, h * 256:(h + 1) * 256], start=True, stop=True),
                             reads=['kw%d' % s2, 'vb%d' % j], writes=[bukey])
                        S.op('pe', lambda: nc.tensor.matmul(bD[:, 8 + 2 * h + a:9 + 2 * h + a],
                                                            lhsT=kw[:, s2, h, a * 128:(a + 1) * 128], rhs=onescol[:, 0:1],
                                                            start=True, stop=True),
                             reads=['kw%d' % s2, 'onescol'], writes=[bDkey])
                for h in range(4):
                    bu, bukey = bU[h]
                    S.op('dve', lambda: nc.vector.scalar_tensor_tensor(
                        out=Cst[:, h, :, :].rearrange("p a e -> p (a e)"), in0=Cst[:, h, :, :].rearrange("p a e -> p (a e)"),
                        scalar=decb[:, gi + h:gi + h + 1], in1=bu[:, 0:512], op0=ALU.mult, op1=ALU.add),
                        reads=['C%d' % h, 'decb', bukey], writes=['C%d' % h])
                    S.op('dve', lambda: nc.vector.scalar_tensor_tensor(
                        out=nst[:, 2 * h:2 * h + 2], in0=nst[:, 2 * h:2 * h + 2], scalar=decb[:, gi + h:gi + h + 1],
                        in1=bD[:, 8 + 2 * h:10 + 2 * h], op0=ALU.mult, op1=ALU.add),
                        reads=['nst', 'decb', bDkey], writes=['nst'])
                sm = osm[:, s2, :]
                for h in range(4):
                    bn, bnkey = bN[h // 2]
                    o = (h % 2) * 256
                    S.op('act', lambda: nc.scalar.activation(out=sqj[:, 0:256], in_=bn[:, o:o + 256], func=AF.Square,
                                                             accum_out=sm[:, h:h + 1]),
                         reads=[bnkey], writes=['osm%d' % s2])
                S.op('dve', lambda: nc.vector.tensor_copy(out=sm[:, 4:8], in_=bD[:, 0:4]), reads=[bDkey], writes=['osm%d' % s2])
                S.op('dve', lambda: nc.vector.scalar_tensor_tensor(out=sm[:, 8:12], in0=sm[:, 4:8], scalar=-1.0, in1=sm[:, 4:8],
                                                                   op0=ALU.mult, op1=ALU.max),
                     reads=['osm%d' % s2], writes=['osm%d' % s2])
                S.op('dve', lambda: nc.vector.tensor_tensor(out=sm[:, 8:12], in0=sm[:, 8:12], in1=thrb[:, gi:gi + 4], op=ALU.max),
                     reads=['osm%d' % s2, 'thrb'], writes=['osm%d' % s2])
                S.op('dve', lambda: nc.vector.scalar_tensor_tensor(out=sm[:, 12:16], in0=sm[:, 8:12], scalar=EPS, in1=sm[:, 8:12],
                                                                   op0=ALU.mult, op1=ALU.mult),
                     reads=['osm%d' % s2], writes=['osm%d' % s2])
                S.op('dve', lambda: nc.vector.scalar_tensor_tensor(out=sm[:, 16:20], in0=sm[:, 0:4], scalar=1.0 / 256,
                                                                   in1=sm[:, 12:16], op0=ALU.mult, op1=ALU.add),
                     reads=['osm%d' % s2], writes=['osm%d' % s2])
                S.op('pool', lambda: nc.gpsimd.tensor_tensor(out=sm[:, 20:24], in0=sm[:, 16:20], in1=mh4[:, 0:4], op=ALU.pow),
                     reads=['osm%d' % s2, 'mh4'], writes=['osm%d' % s2])
                for h in range(4):
                    bn, bnkey = bN[h // 2]
                    o = (h % 2) * 256
                    S.op('dve', lambda: nc.vector.scalar_tensor_tensor(
                        out=hm[:, s2, h * 256:(h + 1) * 256], in0=bn[:, o:o + 256], scalar=sm[:, 20 + h:21 + h],
                        in1=thj[:, h * 256:(h + 1) * 256], op0=ALU.mult, op1=ALU.mult),
                        reads=[bnkey, 'osm%d' % s2, 'tho%d' % s2], writes=['hm%d' % s2])
                if debug:
                    dump('hm', dbg['hm'][t0 + j * 128:t0 + (j + 1) * 128, :], hm[:, s2, :], ['hm%d' % s2])
            wrelease()

            S.alias(mgk, qpk)
            for n in range(4):
                wt, wkey = wget('M%d' % n)
                wv = wt[:, 0:8192].rearrange("p (m c n) -> p m c n", m=4, c=8)
                for sub in range(2):
                    c = 2 * n + sub
                    cs = slice(sub * 128, (sub + 1) * 128)
                    b_gm, k_gm = bk(S.bank())
                    b_gc, k_gc = bk(S.bank())
                    b_bm, k_bm = bk(S.bank())
                    b_bc, k_bc = bk(S.bank())
                    mm8(b_gm[:, 0:TT], k_gm, lambda kc: wv[:, 0, kc, cs], lambda kc: hT[:, kc, :], hTk + [wkey])
                    mm8(b_gc[:, 0:TT], k_gc, lambda kc: wv[:, 1, kc, cs], lambda kc: hT[:, kc, :], hTk + [wkey])
                    if c == 0:
                        hm_transposes(NJ - 1)
                    t1, t1k = tmpbuf()
                    t2, t2k = tmpbuf()
                    S.op('act', lambda: nc.scalar.activation(out=t1, in_=b_gm[:, 0:TT], func=AF.Tanh, scale=0.5),
                         reads=[k_gm], writes=[t1k])
                    S.op('act', lambda: nc.scalar.activation(out=t2, in_=b_gc[:, 0:TT], func=AF.Tanh, scale=0.5),
                         reads=[k_gc], writes=[t2k])
                    mm8(b_bm[:, 0:TT], k_bm, lambda kc: wv[:, 2, kc, cs], lambda kc: hmT[:, kc, :], hmTk + [wkey])
                    mm8(b_bc[:, 0:TT], k_bc, lambda kc: wv[:, 3, kc, cs], lambda kc: usT[:, kc, :], usk + [wkey])
                    S.op('dve', lambda: nc.vector.scalar_tensor_tensor(out=t1, in0=t1, scalar=1.0, in1=b_bm[:, 0:TT],
                                                                       op0=ALU.add, op1=ALU.mult),
                         reads=[t1k, k_bm], writes=[t1k])
                    S.op('dve', lambda: nc.vector.scalar_tensor_tensor(out=t2, in0=t2, scalar=1.0, in1=b_bc[:, 0:TT],
                                                                       op0=ALU.add, op1=ALU.mult),
                         reads=[t2k, k_bc], writes=[t2k])
                    S.op('pool', lambda c=c: nc.gpsimd.tensor_tensor(out=mgT[:, c, :], in0=t1, in1=t2, op=ALU.add),
                         reads=[t1k, t2k], writes=['mg%d' % c])
                wrelease()

            if debug:
                dump('mg', dbg['mg'][:, t0:t0 + TT].rearrange("(c p) t -> p c t", p=128), mgT[:, :, :], mgk)
            wt, wkey = wget('WO')
            wv = wt[:, 0:8192].rearrange("p (c n) -> p c n", c=8)
            S.alias(hbk, hmTk)
            nt = NormT(PC_FFN, bank_ids=[2, 3, 4, 5])
            for j in range(NJ):
                for hf in range(2):
                    bo, bokey = bk(hf)
                    mm8(bo[:, :], bokey, lambda kc: mgT[:, kc, js(j)], lambda kc, hf=hf: wv[:, kc, hf * 512:(hf + 1) * 512],
                        mgk + [wkey])
                    S.op('dve', lambda hf=hf, bo=bo: nc.vector.scalar_tensor_tensor(
                        out=xt[:, j, hf * 512:(hf + 1) * 512], in0=bo[:, :], scalar=0.5, in1=xt[:, j, hf * 512:(hf + 1) * 512],
                        op0=ALU.mult, op1=ALU.add), reads=[bokey, 'xt%d' % j], writes=['xt%d' % j])
                nt.pre(j)
                if j > 0:
                    nt.tr(j - 1)
            nt.tr(NJ - 1)
            nt.evac()
            wrelease()

            if debug:
                dump('x1', dbg['x1'][t0:t0 + TT, :].rearrange("(j p) d -> p j d", p=128), xt[:, :, :], ['xt%d' % j for j in range(NJ)])
            S.alias(ppk, usk)
            for j in range(NJ):
                S.dma('sp', 'pld%d' % j, pt[:, j, :], p_in[t0 + j * 128:t0 + (j + 1) * 128, :], writes=['pt%d' % j])
            S.op('pool', lambda: nc.gpsimd.tensor_copy(out=pb[:, :, :], in_=pt[:, :, :]), reads=['pt%d' % j for j in range(NJ)],
                 writes=['pb'])

            S.alias(hidk, qkvk)
            for f in range(6):
                wt, wkey = wget('F%d' % f)
                ncl = 4 if f < 5 else 2
                wv = wt[:, 0:2 * 8 * ncl * 128].rearrange("p (m c n) -> p m c n", m=2, c=8)
                for il in range(ncl):
                    n = 4 * f + il
                    cs = slice(il * 128, (il + 1) * 128)
                    b_g, k_g = bk(S.bank())
                    b_u, k_u = bk(S.bank())
                    mm8(b_g[:, 0:TT], k_g, lambda kc: wv[:, 0, kc, cs], lambda kc: hT[:, kc, :], hTk + [wkey])
                    mm8(b_u[:, 0:TT], k_u, lambda kc: wv[:, 1, kc, cs], lambda kc: hT[:, kc, :], hTk + [wkey])
                    t1, t1k = tmpbuf()
                    S.op('act', lambda: nc.scalar.activation(out=t1, in_=b_g[:, 0:TT], func=AF.Silu), reads=[k_g], writes=[t1k])
                    S.op('dve', lambda n=n: nc.vector.tensor_tensor(out=hid[:, n, :], in0=t1, in1=b_u[:, 0:TT], op=ALU.mult),
                         reads=[t1k, k_u], writes=['hid%d' % n])
                wrelease()
            for b in range(4):
                wt, wkey = wget('D%d' % b)
                wv = wt[:, 0:NFF * 256].rearrange("p (c n) -> p c n", c=NFF)
                for j in range(NJ):
                    bd, bdkey = bk(S.bank())
                    mm8(bd[:, 0:256], bdkey, lambda kc: hid[:, kc, js(j)], lambda kc: wv[:, kc, :], hidk + [wkey], n=NFF)
                    S.op('dve', lambda b=b, bd=bd: nc.vector.tensor_tensor(out=xt[:, j, b * 256:(b + 1) * 256], in0=bd[:, 0:256],
                                                                         in1=xt[:, j, b * 256:(b + 1) * 256], op=ALU.add),
                         reads=[bdkey, 'xt%d' % j], writes=['xt%d' % j])
                wrelease()

            if debug:
                dump('x2', dbg['x2'][t0:t0 + TT, :].rearrange("(j p) d -> p j d", p=128), xt[:, :, :], ['xt%d' % j for j in range(NJ)])
            bp_, bpkey = bk(S.bank())
            bpb = bp_[:].bitcast(BF16)
            for c in range(2):
                for j in range(NJ):
                    o = c * 512 + j * 128
                    S.op('pe', lambda: nc.tensor.transpose(out=bpb[:, o:o + 128], in_=pb[:, j, c * 128:(c + 1) * 128],
                                                          identity=identb[:]),
                         reads=['pb', 'identb'], writes=[bpkey])
            evac_copy(pT[:, :, :].rearrange("p c t -> p (c t)"), bpb[:, 0:1024], [bpkey], ['pT'])

            norm_T(PC_PLE)
            wtg, wgkey = wget('PG')
            wtp, wpkey = wget('PP')
            wg = wtg[:, 0:8192].rearrange("p (c n) -> p c n", c=8)
            wp = wtp[:, 0:2048].rearrange("p (c n) -> p c n", c=2)
            for j in range(NJ):
                for hf in range(2):
                    b_g, k_g = bk(S.bank())
                    b_p, k_p = bk(S.bank())
                    mm8(b_g[:, :], k_g, lambda kc: hT[:, kc, js(j)], lambda kc, hf=hf: wg[:, kc, hf * 512:(hf + 1) * 512],
                        hTk + [wgkey])
                    mm8(b_p[:, :], k_p, lambda kc: pT[:, kc, js(j)], lambda kc, hf=hf: wp[:, kc, hf * 512:(hf + 1) * 512],
                        ['pT', wpkey], n=2)
                    t1, t1k = tmpbuf()
                    S.op('act', lambda: nc.scalar.activation(out=t1, in_=b_g[:, :], func=AF.Tanh, scale=0.5),
                         reads=[k_g], writes=[t1k])
                    S.op('dve', lambda: nc.vector.scalar_tensor_tensor(out=t1, in0=t1, scalar=1.0, in1=b_p[:, :], op0=ALU.add,
                                                                       op1=ALU.mult), reads=[t1k, k_p], writes=[t1k])
                    S.op('dve', lambda hf=hf: nc.vector.scalar_tensor_tensor(
                        out=xt[:, j, hf * 512:(hf + 1) * 512], in0=t1, scalar=0.5, in1=xt[:, j, hf * 512:(hf + 1) * 512],
                        op0=ALU.mult, op1=ALU.add), reads=[t1k, 'xt%d' % j], writes=['xt%d' % j])
                if debug:
                    dump('x3', dbg['x3'][t0 + j * 128:t0 + (j + 1) * 128, :], xt[:, j, :], ['xt%d' % j])
                s2 = j % 2
                S.op('act', lambda: nc.scalar.activation(out=sqj[:], in_=xt[:, j, :], func=AF.Square,
                                                         accum_out=ss[:, j:j + 1]), reads=['xt%d' % j], writes=['ssa%d' % j])
                S.op('dve', lambda: nc.vector.tensor_scalar(out=ss[:, NJ + j:NJ + j + 1], in0=ss[:, j:j + 1], scalar1=1.0 / D,
                                                            scalar2=EPS, op0=ALU.mult, op1=ALU.add),
                     reads=['ssa%d' % j], writes=['ssb%d' % j])
                S.op('pool', lambda: nc.gpsimd.tensor_tensor(out=ss[:, 2 * NJ + j:2 * NJ + j + 1], in0=ss[:, NJ + j:NJ + j + 1],
                                                             in1=mh4[:, 0:1], op=ALU.pow),
                     reads=['ssb%d' % j, 'mh4'], writes=['ssc%d' % j])
                S.op('dve', lambda: nc.vector.scalar_tensor_tensor(
                    out=tho[:, s2, :], in0=xt[:, j, :], scalar=ss[:, 2 * NJ + j:2 * NJ + j + 1], in1=fg[:], op0=ALU.mult,
                    op1=ALU.mult), reads=['xt%d' % j, 'ssc%d' % j, 'fg'], writes=['tho%d' % s2])
                S.dma('sp', 'o%d' % s2, y[t0 + j * 128:t0 + (j + 1) * 128, :], tho[:, s2, :], reads=['tho%d' % s2],
                      writes=['y%d_%d' % (T, j)])
                if T + 1 < NT:
                    S.dma('sp', 'x%d' % j, xt[:, j, :], x[t0 + TT + j * 128:t0 + TT + (j + 1) * 128, :], writes=['xt%d' % j])
            wrelease()
            wrelease()
        S.finish('sp', ['y%d_%d' % (T, j) for T in range(NT) for j in range(NJ)] + ['dbgout%d' % (i + 1) for i in range(dcnt['i'])])
        if needed is not None:
            print("sched: waits=%d pe=%d act=%d dve=%d pool=%d incs=%s" % (S.nwaits, S.cnt['pe'], S.cnt['act'], S.cnt['dve'],
                                                                        S.cnt['pool'], S.inc))
    return nc, S.waited


def build_program(debug=False):
    _, waited = _build(debug, None)
    nc, _ = _build(debug, waited)
    return nc


_PROGRAM = {}


def kernel(**inputs):
    f32 = lambda a: np.ascontiguousarray(np.asarray(a, dtype=np.float32))
    x = f32(inputs['x'])
    p = f32(inputs['p'])[0]
    shared = {
        'w_in': f32(inputs['w_in'][0]), 'w_branch_m': f32(inputs['w_branch_m'][0]),
        'w_branch_c': f32(inputs['w_branch_c'][0]), 'w_out': f32(inputs['w_out'][0]),
        'w_ffn_gate': f32(inputs['w_ffn_gate'][0]), 'w_ffn_up': f32(inputs['w_ffn_up'][0]),
        'w_ffn_down': f32(inputs['w_ffn_down'][0]), 'w_ple_gate': f32(inputs['w_ple_gate'][0]),
        'w_ple_proj': f32(inputs['w_ple_proj'][0]),
        'norm_mix_g': f32(inputs['norm_mix_g'][0]), 'mh_norm_g': f32(inputs['mh_norm_g'][0]),
        'conf_conv_b': f32(inputs['conf_conv_b'][0]), 'conf_ln_g': f32(inputs['conf_ln_g'][0]),
        'conf_ln_b': f32(inputs['conf_ln_b'][0]), 'norm_ffn_g': f32(inputs['norm_ffn_g'][0]),
        'norm_ple_g': f32(inputs['norm_ple_g'][0]), 'final_g': f32(inputs['final_g']),
        'b_if': f32(inputs['b_if'][0]), 'conv_qk_w': f32(inputs['conv_qk_w'][0]),
        'conv_qk_b': f32(inputs['conv_qk_b'][0]), 'conf_conv_w': f32(inputs['conf_conv_w'][0]),
    }
    if 'nc' not in _PROGRAM:
        _PROGRAM['nc'] = build_program()
    nc = _PROGRAM['nc']
    in_maps = []
    for b in range(NCORES):
        m = dict(shared)
        m['x'] = np.ascontiguousarray(x[b])
        m['p'] = np.ascontiguousarray(p[b])
        in_maps.append(m)
    res = run_bass_kernel_spmd(nc, in_maps, core_ids=list(range(NCORES)))
    out = np.stack([np.asarray(res.results[b]['y'], dtype=np.float32) for b in range(NCORES)], axis=0)
    return out
```
